# Optimizing a Trainium2 kernel written in Bass

```python
import math
import jax, jax.numpy as jnp
from jax import lax
import numpy as np

D_MODEL = 1024
BATCH = 32
SEQ = 2048
DEPTH = 2

HEAD_DIM = 64
A_HEADS = 6
A_KV_HEADS = 2
B_HEADS = 6
B_KV_HEADS = 2
C_HEADS = 4
C_Q_RANK = 256
C_KV_RANK = 128
C_NOPE_DIM = 64
C_ROPE_DIM = 32
C_V_DIM = 64
C_QK_DIM = C_NOPE_DIM + C_ROPE_DIM
D_FF = 2816
GRID_W = 64
Q_BLOCK = 128
WINDOW = 128
NUM_BUCKETS = 32
MAX_DISTANCE = 128
ROPE_THETA = 10000.0
ADA_CHUNKS = 9
EPS = 1e-6
NEG_INF = -1e30

A_Q_W = A_HEADS * HEAD_DIM
A_KV_W = A_KV_HEADS * HEAD_DIM
B_Q_W = B_HEADS * HEAD_DIM
B_KV_W = B_KV_HEADS * HEAD_DIM
IN_SIZES = (A_Q_W, A_KV_W, A_KV_W, B_Q_W, B_KV_W, B_KV_W,
            C_Q_RANK, C_KV_RANK, C_ROPE_DIM, D_MODEL, D_MODEL, D_MODEL)
IN_COLS = (A_Q_W + 2 * A_KV_W + B_Q_W + 2 * B_KV_W
           + C_Q_RANK + C_KV_RANK + C_ROPE_DIM + 3 * D_MODEL)

kernel_name = 'hybrid_gated_mixer_encoder'


def rms_norm(x, g):
    xf = x.astype(jnp.float32)
    y = xf * lax.rsqrt(jnp.mean(xf * xf, axis=-1, keepdims=True) + EPS)
    return (y * g.astype(jnp.float32)).astype(x.dtype)


def modulate(h, shift, scale):
    return h * (1.0 + scale[:, None, :]) + shift[:, None, :]


def swiglu(h, w_gu, w_down):
    gate, up = jnp.split(h @ w_gu, 2, axis=-1)
    return (jax.nn.silu(gate) * up) @ w_down


def split_cols(y, sizes):
    out, off = [], 0
    for s in sizes:
        out.append(y[..., off:off + s])
        off += s
    return out


def rope_angles(pos, dim):
    inv = ROPE_THETA ** (-jnp.arange(0, dim, 2, dtype=jnp.float32) / dim)
    ang = pos.astype(jnp.float32)[:, None] * inv[None, :]
    return jnp.cos(ang), jnp.sin(ang)


def apply_rope(x, cos, sin):
    half = x.shape[-1] // 2
    xf = x.astype(jnp.float32)
    x1, x2 = xf[..., :half], xf[..., half:]
    cos, sin = cos[:, None, :], sin[:, None, :]
    return jnp.concatenate([x1 * cos - x2 * sin, x1 * sin + x2 * cos], axis=-1).astype(x.dtype)


def axial_rope(x, row_cs, col_cs):
    half = x.shape[-1] // 2
    return jnp.concatenate([apply_rope(x[..., :half], *row_cs),
                            apply_rope(x[..., half:], *col_cs)], axis=-1)


def t5_bucket(rel):
    nb = NUM_BUCKETS // 2
    max_exact = nb // 2
    ret = jnp.where(rel > 0, nb, 0)
    n = jnp.abs(rel)
    large = max_exact + (jnp.log(jnp.maximum(n, 1).astype(jnp.float32) / max_exact)
                         / math.log(MAX_DISTANCE / max_exact) * (nb - max_exact)).astype(jnp.int32)
    large = jnp.minimum(large, nb - 1)
    return ret + jnp.where(n < max_exact, n, large)


def window_bias_mask(rel_bias, seq):
    nblk = seq // Q_BLOCK
    r = jnp.arange(Q_BLOCK)[:, None]
    j = jnp.arange(3 * Q_BLOCK)[None, :]
    rel = j - Q_BLOCK - r
    bias = rel_bias[t5_bucket(rel)].astype(jnp.float32)
    bias = jnp.transpose(bias, (2, 0, 1)).reshape(B_KV_HEADS, B_HEADS // B_KV_HEADS, Q_BLOCK, 3 * Q_BLOCK)
    kpos = jnp.arange(nblk)[:, None, None] * Q_BLOCK - Q_BLOCK + j[None]
    mask = (jnp.abs(rel) <= WINDOW)[None] & (kpos >= 0) & (kpos < seq)
    return bias, mask


def dense_attention(q, k, v, scale):
    b, s, kh, g, dq = q.shape
    nblk = s // Q_BLOCK
    qb = jnp.moveaxis(q.reshape(b, nblk, Q_BLOCK, kh, g, dq), 1, 0)

    def one_block(qi):
        logits = jnp.einsum('bqkgd,bskd->bkgqs', qi, k).astype(jnp.float32) * scale
        p = jax.nn.softmax(logits, axis=-1).astype(v.dtype)
        return jnp.einsum('bkgqs,bskd->bqkgd', p, v)

    out = jnp.moveaxis(lax.map(one_block, qb), 0, 1)
    return out.reshape(b, s, kh * g * v.shape[-1])


def window_attention(q, k, v, bias, mask, sink, scale):
    b, s, kh, g, d = q.shape
    nblk = s // Q_BLOCK
    qb = q.reshape(b, nblk, Q_BLOCK, kh, g, d)

    def band(a):
        pad = jnp.pad(a, ((0, 0), (Q_BLOCK, Q_BLOCK), (0, 0), (0, 0)))
        pb = pad.reshape(b, nblk + 2, Q_BLOCK, kh, a.shape[-1])
        return jnp.concatenate([pb[:, :-2], pb[:, 1:-1], pb[:, 2:]], axis=2)

    kb, vb = band(k), band(v)
    logits = jnp.einsum('bnqkgd,bnskd->bnkgqs', qb, kb).astype(jnp.float32) * scale + bias[None, None]
    logits = jnp.where(mask[None, :, None, None], logits, NEG_INF)
    sink_col = jnp.broadcast_to(sink.astype(jnp.float32).reshape(1, 1, kh, g, 1, 1),
                                logits.shape[:-1] + (1,))
    p = jax.nn.softmax(jnp.concatenate([logits, sink_col], axis=-1), axis=-1)[..., :-1]
    out = jnp.einsum('bnkgqs,bnskd->bnqkgd', p.astype(v.dtype), vb)
    return out.reshape(b, s, kh * g * d)


def token_mix(h, w_in, a_q_norm, a_k_norm, b_sink, c_q_lat_norm, c_w_q_up, c_kv_lat_norm,
              c_w_kv_up, w_br_a, w_br_b, w_br_c, w_out, row_cs, col_cs, seq_cs, win_bias, win_mask):
    b, s, _ = h.shape
    (aq, ak, av, bq, bk, bv, cq_lat, ckv_lat, ck_rope,
     gate_a, gate_b, gate_c) = split_cols(h @ w_in, IN_SIZES)

    qa = axial_rope(rms_norm(aq.reshape(b, s, A_HEADS, HEAD_DIM), a_q_norm), row_cs, col_cs)
    ka = axial_rope(rms_norm(ak.reshape(b, s, A_KV_HEADS, HEAD_DIM), a_k_norm), row_cs, col_cs)
    va = av.reshape(b, s, A_KV_HEADS, HEAD_DIM)
    qa = qa.reshape(b, s, A_KV_HEADS, A_HEADS // A_KV_HEADS, HEAD_DIM)
    o_a = dense_attention(qa, ka, va, HEAD_DIM ** -0.5)

    qb = bq.reshape(b, s, B_KV_HEADS, B_HEADS // B_KV_HEADS, HEAD_DIM)
    kb = bk.reshape(b, s, B_KV_HEADS, HEAD_DIM)
    vb = bv.reshape(b, s, B_KV_HEADS, HEAD_DIM)
    o_b = window_attention(qb, kb, vb, win_bias, win_mask, b_sink, HEAD_DIM ** -0.5)

    qc = (rms_norm(cq_lat, c_q_lat_norm) @ c_w_q_up).reshape(b, s, C_HEADS, C_QK_DIM)
    qc_nope, qc_rope = qc[..., :C_NOPE_DIM], apply_rope(qc[..., C_NOPE_DIM:], *seq_cs)
    kv = (rms_norm(ckv_lat, c_kv_lat_norm) @ c_w_kv_up).reshape(b, s, C_HEADS, C_NOPE_DIM + C_V_DIM)
    kc_nope, vc = kv[..., :C_NOPE_DIM], kv[..., C_NOPE_DIM:]
    kc_rope = jnp.broadcast_to(apply_rope(ck_rope[:, :, None, :], *seq_cs), (b, s, C_HEADS, C_ROPE_DIM))
    qc_full = jnp.concatenate([qc_nope, qc_rope], axis=-1).reshape(b, s, C_HEADS, 1, C_QK_DIM)
    kc_full = jnp.concatenate([kc_nope, kc_rope], axis=-1)
    o_c = dense_attention(qc_full, kc_full, vc, C_QK_DIM ** -0.5)

    merged = (jax.nn.sigmoid(gate_a) * (o_a @ w_br_a)
              + jax.nn.sigmoid(gate_b) * (o_b @ w_br_b)
              + jax.nn.sigmoid(gate_c) * (o_c @ w_br_c))
    return merged @ w_out


def setup_inputs(seed: int = 0) -> dict:
    key = jax.random.key(seed)
    ks = jax.random.split(key, 32)
    L, D = DEPTH, D_MODEL

    def nrm(k, shape, scale):
        return jax.random.normal(k, shape, jnp.float32) * scale

    def gain(k, shape):
        return 1.0 + 0.05 * jax.random.normal(k, shape, jnp.float32)

    return {
        'x': nrm(ks[0], (BATCH, SEQ, D), 1.0),
        'c': nrm(ks[1], (BATCH, D), 1.0),
        'ada_w': nrm(ks[2], (L, D, ADA_CHUNKS * D), 0.5 * D ** -0.5),
        'ada_b': nrm(ks[3], (L, ADA_CHUNKS * D), 0.02),
        'norm_ffn1': gain(ks[4], (L, D)),
        'ffn1_w_gu': nrm(ks[5], (L, D, 2 * D_FF), D ** -0.5),
        'ffn1_w_down': nrm(ks[6], (L, D_FF, D), D_FF ** -0.5),
        'norm_mix': gain(ks[7], (L, D)),
        'w_in': nrm(ks[8], (L, D, IN_COLS), D ** -0.5),
        'a_q_norm': gain(ks[9], (L, HEAD_DIM)),
        'a_k_norm': gain(ks[10], (L, HEAD_DIM)),
        'b_sink': nrm(ks[11], (L, B_HEADS), 1.0),
        'rel_bias': nrm(ks[12], (NUM_BUCKETS, B_HEADS), 0.5),
        'c_q_lat_norm': gain(ks[13], (L, C_Q_RANK)),
        'c_w_q_up': nrm(ks[14], (L, C_Q_RANK, C_HEADS * C_QK_DIM), C_Q_RANK ** -0.5),
        'c_kv_lat_norm': gain(ks[15], (L, C_KV_RANK)),
        'c_w_kv_up': nrm(ks[16], (L, C_KV_RANK, C_HEADS * (C_NOPE_DIM + C_V_DIM)), C_KV_RANK ** -0.5),
        'w_br_a': nrm(ks[17], (L, A_Q_W, D), A_Q_W ** -0.5),
        'w_br_b': nrm(ks[18], (L, B_Q_W, D), B_Q_W ** -0.5),
        'w_br_c': nrm(ks[19], (L, C_HEADS * C_V_DIM, D), (C_HEADS * C_V_DIM) ** -0.5),
        'w_out': nrm(ks[20], (L, D, D), D ** -0.5),
        'norm_ffn2': gain(ks[21], (L, D)),
        'ffn2_w_gu': nrm(ks[22], (L, D, 2 * D_FF), D ** -0.5),
        'ffn2_w_down': nrm(ks[23], (L, D_FF, D), D_FF ** -0.5),
        'final_norm': gain(ks[24], (D,)),
    }


def reference(x, c, ada_w, ada_b, norm_ffn1, ffn1_w_gu, ffn1_w_down, norm_mix, w_in,
              a_q_norm, a_k_norm, b_sink, rel_bias, c_q_lat_norm, c_w_q_up, c_kv_lat_norm,
              c_w_kv_up, w_br_a, w_br_b, w_br_c, w_out, norm_ffn2, ffn2_w_gu, ffn2_w_down,
              final_norm):
    b, s, _ = x.shape
    rows = s // GRID_W
    t = jnp.arange(s)
    row_pos = jnp.repeat(jnp.arange(rows), GRID_W)
    col_pos = jnp.tile(jnp.arange(GRID_W), rows)
    row_cs = rope_angles(row_pos, HEAD_DIM // 2)
    col_cs = rope_angles(col_pos, HEAD_DIM // 2)
    seq_cs = rope_angles(t, C_ROPE_DIM)
    win_bias, win_mask = window_bias_mask(rel_bias, s)
    cond = jax.nn.silu(c)

    for l in range(DEPTH):
        mods = cond @ ada_w[l] + ada_b[l]
        sh1, sc1, g1, sh2, sc2, g2, sh3, sc3, g3 = jnp.split(mods, ADA_CHUNKS, axis=-1)

        h = modulate(rms_norm(x, norm_ffn1[l]), sh1, sc1)
        x = x + 0.5 * g1[:, None, :] * swiglu(h, ffn1_w_gu[l], ffn1_w_down[l])

        h = modulate(rms_norm(x, norm_mix[l]), sh2, sc2)
        x = x + g2[:, None, :] * token_mix(
            h, w_in[l], a_q_norm[l], a_k_norm[l], b_sink[l], c_q_lat_norm[l], c_w_q_up[l],
            c_kv_lat_norm[l], c_w_kv_up[l], w_br_a[l], w_br_b[l], w_br_c[l], w_out[l],
            row_cs, col_cs, seq_cs, win_bias, win_mask)

        h = modulate(rms_norm(x, norm_ffn2[l]), sh3, sc3)
        x = x + 0.5 * g3[:, None, :] * swiglu(h, ffn2_w_gu[l], ffn2_w_down[l])

    return rms_norm(x, final_norm)
```

```python
import numpy as np
import concourse.bass as bass
import concourse.mybir as mybir
from concourse.bass_utils import run_bass_kernel_spmd

F32 = mybir.dt.float32
BF16 = mybir.dt.bfloat16
AF = mybir.ActivationFunctionType
ALU = mybir.AluOpType

D = 1024
S = 2048
KC = 8
FF = 2816
FC = 22
TT = 512
NT = 4
L = 2
EPS = 1e-6
IN_COLS = 4768
NCORES = 8


class Res:
    __slots__ = ("name", "lw", "rd")

    def __init__(self, name):
        self.name = name
        self.lw = None
        self.rd = {}


class Chan:
    def __init__(self, name):
        self.name = name
        self.n = 0
        self.sem = None


class Prog:
    ENG = ("pe", "act", "dve", "pool", "sp")

    def __init__(self, nc, serialize=False):
        self.nc = nc
        self.ops = []
        self.serialize = serialize
        self.eng = {"pe": nc.tensor, "act": nc.scalar, "dve": nc.vector,
                    "pool": nc.gpsimd, "sp": nc.sync}
        self.chans = []

    def chan(self, name):
        c = Chan(name)
        self.chans.append(c)
        return c

    def op(self, eng, fn, reads=(), writes=(), chan=None):
        idx = len(self.ops)
        deps = set()
        rkey = ("c", id(chan)) if chan is not None else eng
        for r in reads:
            if r.lw is not None:
                deps.add((r.lw, True))
        for w in writes:
            if w.lw is not None:
                pw = self.ops[w.lw]
                if not (chan is not None and pw["chan"] is chan and pw["eng"] == eng):
                    deps.add((w.lw, False))
            for k, i in w.rd.items():
                deps.add((i, False))
        if self.serialize and idx > 0:
            deps.add((idx - 1, True))
        for r in reads:
            r.rd[rkey] = idx
        for w in writes:
            w.lw = idx
            w.rd = {}
        ordn = None
        if chan is not None:
            chan.n += 1
            ordn = chan.n
        self.ops.append(dict(eng=eng, fn=fn, deps=deps, chan=chan, ordn=ordn, sig=False))
        return idx

    def emit(self, final_chans=()):
        nc = self.nc
        ops = self.ops
        real = []
        for i, o in enumerate(ops):
            rl = []
            for (j, raw) in o["deps"]:
                p = ops[j]
                if p["chan"] is None:
                    if p["eng"] == o["eng"] and o["chan"] is None:
                        if o["eng"] == "pe":
                            continue
                    p["sig"] = True
                rl.append(j)
            real.append(rl)
        sem = {e: nc.alloc_semaphore("s_" + e) for e in ("pe", "act", "dve", "pool")}
        for c in self.chans:
            c.sem = nc.alloc_semaphore("c_" + c.name)
        cnt = {e: 0 for e in sem}
        sigval = {}
        seen = {e: {} for e in self.ENG}
        nwait = 0
        for i, o in enumerate(ops):
            e = o["eng"]
            E = self.eng[e]
            need = {}
            for j in real[i]:
                p = ops[j]
                if p["chan"] is not None:
                    key = ("c", id(p["chan"]))
                    s, v = p["chan"].sem, 16 * p["ordn"]
                else:
                    key = p["eng"]
                    s, v = sem[p["eng"]], sigval[j]
                if seen[e].get(key, 0) >= v:
                    continue
                if key not in need or need[key][1] < v:
                    need[key] = (s, v)
            for key, (s, v) in need.items():
                E.wait_ge(s, v)
                seen[e][key] = v
                nwait += 1
            ins = o["fn"](E)
            if o["chan"] is not None:
                ins.then_inc(o["chan"].sem, 16)
            elif o["sig"]:
                cnt[e] += 1
                sigval[i] = cnt[e]
                ins.then_inc(sem[e], 1)
        for c in final_chans:
            nc.sync.wait_ge(c.sem, 16 * c.n)
        self.stats = dict(n_ops=len(ops), n_wait=nwait, cnt=dict(cnt))
        return self.stats


def fm(v):
    v = np.asarray(v, np.float32)
    return np.ascontiguousarray(v.reshape(-1, 128).T)


SM = {}
_o = 0
for _l in range(L):
    for _n, _w in (("nf1", 8), ("nmix", 8), ("nf2", 8), ("adab", 72), ("aqn", 1), ("akn", 1), ("cqn", 2), ("ckvn", 1), ("sink", 6)):
        SM[(_n, _l)] = (_o, _w)
        _o += _w
SM[("fin", 0)] = (_o, 8)
_o += 8
SM[("relb", 0)] = (_o, 6)
_o += 6
CF_COLS = 1536
SM_COLS = _o


def pack_smalls(inp):
    sm = np.zeros((128, SM_COLS), np.float32)

    def put(key, arr):
        o, w = SM[key]
        assert arr.shape == (128, w), (key, arr.shape)
        sm[:, o:o + w] = arr

    for l in range(L):
        put(("nf1", l), fm(inp["norm_ffn1"][l]))
        put(("nmix", l), fm(inp["norm_mix"][l]))
        put(("nf2", l), fm(inp["norm_ffn2"][l]))
        put(("adab", l), fm(inp["ada_b"][l]))
        put(("aqn", l), np.tile(np.asarray(inp["a_q_norm"][l], np.float32), 2).reshape(128, 1))
        put(("akn", l), np.tile(np.asarray(inp["a_k_norm"][l], np.float32), 2).reshape(128, 1))
        put(("cqn", l), fm(inp["c_q_lat_norm"][l]))
        put(("ckvn", l), fm(inp["c_kv_lat_norm"][l]))
        put(("sink", l), np.broadcast_to(np.asarray(inp["b_sink"][l], np.float32)[None, :], (128, 6)))
    put(("fin", 0), fm(inp["final_norm"]))
    o, w = SM[("relb", 0)]
    sm[0:32, o:o + w] = np.asarray(inp["rel_bias"], np.float32)
    return sm


NSLOT = 3
SLOT_E = 4608


def build(n_seq=4, depth=L, do_mix=True, serialize=False, do_ffn=True, do_ada=True, ffn_parts="ngd", mixers="abc"):
    nc = bass.Bass("TRN2", target_bir_lowering=False)
    NTOK = n_seq * S
    x_d = nc.dram_tensor("x", [NTOK, D], F32, kind="ExternalInput").ap()
    ct_d = nc.dram_tensor("cT", [128, KC * n_seq], F32, kind="ExternalInput").ap()
    sm_d = nc.dram_tensor("smalls", [128, SM_COLS], F32, kind="ExternalInput").ap()
    adaw_d = nc.dram_tensor("ada_w", [L, D, 9 * D], F32, kind="ExternalInput").ap()
    wgu_d = [nc.dram_tensor("ffn%d_w_gu" % i, [L, D, 2 * FF], F32, kind="ExternalInput").ap() for i in (1, 2)]
    wdn_d = [nc.dram_tensor("ffn%d_w_down" % i, [L, FF, D], F32, kind="ExternalInput").ap() for i in (1, 2)]
    out_d = nc.dram_tensor("out", [NTOK, D], F32, kind="ExternalOutput").ap()
    win_d = nc.dram_tensor("w_in", [L, D, IN_COLS], F32, kind="ExternalInput").ap()
    cqup_d = nc.dram_tensor("c_w_q_up", [L, 256, 384], F32, kind="ExternalInput").ap()
    ckvup_d = nc.dram_tensor("c_w_kv_up", [L, 128, 512], F32, kind="ExternalInput").ap()
    wbra_d = nc.dram_tensor("w_br_a", [L, 384, D], F32, kind="ExternalInput").ap()
    wbrb_d = nc.dram_tensor("w_br_b", [L, 384, D], F32, kind="ExternalInput").ap()
    wbrc_d = nc.dram_tensor("w_br_c", [L, 256, D], F32, kind="ExternalInput").ap()
    wout_d = nc.dram_tensor("w_out", [L, D, D], F32, kind="ExternalInput").ap()
    cf_d = nc.dram_tensor("cf", [128, CF_COLS], F32, kind="ExternalInput").ap()
    zer_d = nc.dram_tensor("zer", [128, KC * 64], F32, kind="ExternalInput").ap()
    tabs_d = [nc.dram_tensor(n, [128, 2, S], F32, kind="ExternalInput").ap() for n in ("ropeA", "ropeC")]
    tb_d = nc.dram_tensor("tb_scratch", [6, 512], BF16)

    P = Prog(nc, serialize=serialize)

    xT = nc.alloc_sbuf_tensor("xT", [128, KC, S], F32)
    hT = nc.alloc_sbuf_tensor("hT", [128, KC, S], BF16)
    SCR_E = 12 * S + 16 * 192
    assert SCR_E >= FC * 2 * TT + 2 * 2 * D
    scr = nc.alloc_sbuf_tensor("scr", [128, SCR_E], BF16)
    slots = [nc.alloc_sbuf_tensor("wslot%d" % i, [128, SLOT_E], BF16) for i in range(NSLOT)]
    slot_res = [Res("wslot%d" % i) for i in range(NSLOT)]
    slot_ch = [P.chan("w%d" % i) for i in range(NSLOT)]
    xin = [scr[:, FC * 2 * TT + i * 2 * D: FC * 2 * TT + (i + 1) * 2 * D].bitcast(F32) for i in range(2)]
    xin_res = [Res("xin%d" % i) for i in range(2)]
    xin_ch = [P.chan("xi%d" % i) for i in range(2)]
    xout_ch = [P.chan("xo%d" % i) for i in range(2)]
    smalls = nc.alloc_sbuf_tensor("smalls_sb", [128, SM_COLS], F32)
    cT = nc.alloc_sbuf_tensor("cT_sb", [128, KC * n_seq], F32)
    condT = nc.alloc_sbuf_tensor("condT", [128, KC * n_seq], F32)
    MODW = 72 * n_seq
    mods = nc.alloc_sbuf_tensor("mods", [128, L * MODW], F32)
    modA = nc.alloc_sbuf_tensor("modA", [128, L * 3 * n_seq * KC], F32)
    modG = nc.alloc_sbuf_tensor("modG", [128, L * 3 * n_seq * KC], F32)
    ident = nc.alloc_sbuf_tensor("ident", [128, 128], F32)
    ones_bf = nc.alloc_sbuf_tensor("ones_bf", [128, 128], BF16)
    epsb = nc.alloc_sbuf_tensor("epsb", [128, 1], F32)
    sq = [nc.alloc_sbuf_tensor("sq%d" % i, [128, TT], BF16) for i in range(2)]
    sq_res = [Res("sq%d" % i) for i in range(2)]
    NRSTD = 1
    rstd = [nc.alloc_sbuf_tensor("rstd%d" % i, [128, TT], F32) for i in range(NRSTD)]
    rstd_res = [Res("rstd%d" % i) for i in range(NRSTD)]
    tmpf = [nc.alloc_sbuf_tensor("tmpf%d" % i, [128, TT], F32) for i in range(3)]
    tmpf_res = [Res("tmpf%d" % i) for i in range(3)]
    banks = [nc.alloc_psum_tensor("bank%d" % i, [128, TT], F32) for i in range(8)]
    bank_res = [Res("bank%d" % i) for i in range(8)]
    const_res = Res("consts")
    const_ch = P.chan("const")
    mods_res = Res("mods")
    x_res = [[Res("x%d_%d" % (i, c)) for c in range(KC)] for i in range(NT)]
    h_res = [[Res("h%d_%d" % (i, c)) for c in range(KC)] for i in range(NT)]

    rr = {"bank": 0, "abank": 0, "tmpf": 0, "sq": 0, "sqx": 0, "rstd": 0, "xin": 0}

    def nxt(kind, n):
        i = rr[kind]
        rr[kind] = (i + 1) % n
        return i

    held = set()

    def get_bank(acc=False):
        if acc:
            i = nxt("abank", 4)
        else:
            for _ in range(4):
                i = 4 + nxt("bank", 4)
                if i not in held:
                    break
            else:
                raise RuntimeError("no free PSUM bank")
        return banks[i], bank_res[i]

    def hold(bk):
        held.add(banks.index(bk))

    def release(bk):
        held.discard(banks.index(bk))

    def get_tmpf():
        i = nxt("tmpf", 3)
        return tmpf[i], tmpf_res[i]

    P.op("sp", lambda E: E.dma_start(out=smalls[:], in_=sm_d), writes=[const_res], chan=const_ch)
    P.op("sp", lambda E: E.dma_start(out=cT[:], in_=ct_d), writes=[const_res], chan=const_ch)
    P.op("dve", lambda E: E.memset(ident[:], 0.0), writes=[const_res])
    P.op("pool", lambda E: E.affine_select(out=ident[:], in_=ident[:], compare_op=ALU.not_equal, fill=1.0,
                                           base=0, pattern=[[-1, 128]], channel_multiplier=1),
         reads=[const_res], writes=[const_res])
    P.op("dve", lambda E: E.memset(ones_bf[:], 1.0), writes=[const_res])
    P.op("dve", lambda E: E.memset(epsb[:], EPS), writes=[const_res])
    P.op("act", lambda E: E.activation(out=condT[:], in_=cT[:], func=AF.Silu), reads=[const_res], writes=[const_res])

    def smcol(key, c=None):
        o, w = SM[key]
        if c is None:
            return smalls[:, o:o + w]
        return smalls[:, o + c:o + c + 1]

    steps = []

    step_tag = {}

    def add(wjob, fn, tag=None):
        step_tag[id(fn)] = tag
        steps.append((wjob, fn))

    ADA_CW = 256

    def ada_steps(l, m3, out_list):
        pbank = {}
        T0 = m3 * 12

        def wjob(t):
            def load(slot, res, ch):
                dst = slot[:, 0:2 * KC * ADA_CW].bitcast(F32).rearrange("p (k j) -> p k j", k=KC)
                src = adaw_d[l].rearrange("(k p) n -> p k n", p=128)[:, :, t * ADA_CW:(t + 1) * ADA_CW]
                P.op("pool", lambda E: E.dma_start(out=dst, in_=src), writes=[res], chan=ch)
            return load

        def comp(t):
            def fn(slot, res):
                if t == T0:
                    pbank["b"] = get_bank(acc=True)
                bk, bres = pbank["b"]
                W = slot[:, 0:2 * KC * ADA_CW].bitcast(F32).rearrange("p (k j) -> p k j", k=KC)
                for jj in range(ADA_CW // 128):
                    j = t * (ADA_CW // 128) + jj
                    for k in range(KC):
                        P.op("pe", lambda E, j=j, jj=jj, k=k: E.matmul(
                            bk[:, j * n_seq:(j + 1) * n_seq], lhsT=W[:, k, jj * 128:(jj + 1) * 128],
                            rhs=condT[:, k * n_seq:(k + 1) * n_seq], start=(k == 0), stop=(k == KC - 1)),
                            reads=[res, const_res], writes=[bres])
                if t == T0 + 11:
                    o, w = SM[("adab", l)]
                    j0, j1 = m3 * 24, (m3 + 1) * 24
                    src_b = smalls[:, o + j0:o + j1].unsqueeze(2).to_broadcast([128, 24, n_seq])
                    dstm = mods[:, l * MODW:(l + 1) * MODW].rearrange("p (j s) -> p j s", s=n_seq)[:, j0:j1, :]
                    srcp = bk[:, 0:MODW].rearrange("p (j s) -> p j s", s=n_seq)[:, j0:j1, :]
                    P.op("dve", lambda E: E.tensor_tensor(out=dstm, in0=srcp, in1=src_b, op=ALU.add),
                         reads=[bres, const_res], writes=[mods_res])
                    for m in (m3,):
                        gk = (("nf1", l), ("nmix", l), ("nf2", l))[m]
                        go, _ = SM[gk]
                        for s in range(n_seq):
                            base = ((l * 3 + m) * n_seq + s) * KC
                            sc = mods[:, l * MODW:(l + 1) * MODW].rearrange("p (j s) -> p j s", s=n_seq)[:, (3 * m + 1) * 8:(3 * m + 2) * 8, s]
                            gt = mods[:, l * MODW:(l + 1) * MODW].rearrange("p (j s) -> p j s", s=n_seq)[:, (3 * m + 2) * 8:(3 * m + 3) * 8, s]
                            P.op("dve", lambda E, sc=sc, base=base, go=go: E.scalar_tensor_tensor(
                                out=modA[:, base:base + KC], in0=sc, scalar=1.0, in1=smalls[:, go:go + KC],
                                op0=ALU.add, op1=ALU.mult), reads=[mods_res, const_res], writes=[mods_res])
                            rw = 1.0 if m == 1 else 0.5
                            P.op("dve", lambda E, gt=gt, base=base, rw=rw: E.tensor_scalar(
                                out=modG[:, base:base + KC], in0=gt, scalar1=rw, scalar2=None, op0=ALU.mult),
                                reads=[mods_res], writes=[mods_res])
            return fn

        for t in range(T0, T0 + 12):
            out_list.append((wjob(t), comp(t)))

    def mA(l, m, s, c):
        b = ((l * 3 + m) * n_seq + s) * KC + c
        return modA[:, b:b + 1]

    def mG(l, m, s, c):
        b = ((l * 3 + m) * n_seq + s) * KC + c
        return modG[:, b:b + 1]

    def mB(l, m, s, c):
        j = (3 * m) * 8 + c
        b = l * MODW + j * n_seq + s
        return mods[:, b:b + 1]

    def load_seq_steps(s):
        def fn(slot, res):
            for i in range(S // 128):
                xi = nxt("xin", 2)
                src = x_d[s * S + i * 128: s * S + (i + 1) * 128, :]
                P.op("sp", lambda E, xi=xi, src=src: E.dma_start(out=xin[xi], in_=src),
                     writes=[xin_res[xi]], chan=xin_ch[xi])
                for half in range(2):
                    bk, bres = get_bank()
                    for cc in range(4):
                        c = half * 4 + cc
                        P.op("pe", lambda E, bk=bk, cc=cc, c=c, xi=xi: E.transpose(
                            out=bk[:, cc * 128:(cc + 1) * 128], in_=xin[xi][:, c * 128:(c + 1) * 128], identity=ident[:]),
                            reads=[xin_res[xi], const_res], writes=[bres])
                    dst = xT[:, half * 4:(half + 1) * 4, i * 128:(i + 1) * 128]
                    srcp = bk[:, :].rearrange("p (c t) -> p c t", c=4)
                    eng = "dve" if half == 0 else "act"
                    if eng == "dve":
                        P.op("dve", lambda E, dst=dst, srcp=srcp: E.tensor_copy(out=dst, in_=srcp),
                             reads=[bres], writes=x_res[i // 4][half * 4:(half + 1) * 4])
                    else:
                        P.op("act", lambda E, dst=dst, srcp=srcp: E.activation(out=dst, in_=srcp, func=AF.Copy),
                             reads=[bres], writes=x_res[i // 4][half * 4:(half + 1) * 4])
                if subs and i % 4 == 3 and i // 4 >= 1:
                    emit_norm_tile(subs[0][0], subs[0][1], s, i // 4 - 1)
            if subs:
                emit_norm_tile(subs[0][0], subs[0][1], s, NT - 1)
        add(None, fn)

    def emit_rstd(tt):
        bk, bres = get_bank()
        pool6 = [(sq[0], sq_res[0]), (sq[1], sq_res[1])] + ([(pT[i], pT_res[i]) for i in range(NPT)] if do_mix else [])
        for c in range(KC):
            qb, qbres = pool6[nxt("sqx", len(pool6))]
            P.op("act", lambda E, qb=qb, c=c: E.activation(out=qb[:], in_=xT[:, c, tt * TT:(tt + 1) * TT], func=AF.Square),
                 reads=[x_res[tt][c]], writes=[qbres])
            P.op("pe", lambda E, qb=qb, c=c, bk=bk: E.matmul(bk[:], lhsT=ones_bf[:], rhs=qb[:], start=(c == 0), stop=(c == KC - 1)),
                 reads=[qbres, const_res], writes=[bres])
        ri = nxt("rstd", NRSTD)
        P.op("act", lambda E, ri=ri, bk=bk: E.activation(out=rstd[ri][:], in_=bk[:], func=AF.Ln, bias=epsb[:], scale=1.0 / D),
             reads=[bres, const_res], writes=[rstd_res[ri]])
        P.op("act", lambda E, ri=ri: E.activation(out=rstd[ri][:], in_=rstd[ri][:], func=AF.Exp, scale=-0.5),
             reads=[rstd_res[ri]], writes=[rstd_res[ri]])
        return rstd[ri], rstd_res[ri]

    def emit_norm_tile(l, m, s, tt):
        r, rres = emit_rstd(tt)
        for c in range(KC):
            tf, tres = get_tmpf()
            P.op("dve", lambda E, tf=tf, c=c: E.tensor_tensor(
                out=tf[:], in0=xT[:, c, tt * TT:(tt + 1) * TT], in1=r[:], op=ALU.mult),
                reads=[x_res[tt][c], rres], writes=[tres])
            P.op("act", lambda E, tf=tf, c=c: E.activation(
                out=hT[:, c, tt * TT:(tt + 1) * TT], in_=tf[:], func=AF.Identity,
                scale=mA(l, m, s, c), bias=mB(l, m, s, c)),
                reads=[tres, mods_res], writes=[h_res[tt][c]])

    subs = []
    for _l in range(depth):
        if do_ffn:
            subs.append((_l, 0))
        if do_mix:
            subs.append((_l, 1))
        if do_ffn:
            subs.append((_l, 2))
    nxt_sub = {subs[i]: (subs[i + 1] if i + 1 < len(subs) else None) for i in range(len(subs))}

    def add_next_norm(l, m, s, tiles):
        ns = nxt_sub[(l, m)]
        if ns is None:
            return

        def fn(slot, res):
            for tt in tiles:
                emit_norm_tile(ns[0], ns[1], s, tt)
        add(None, fn)

    GU_CW = 256
    NGU = FF // GU_CW
    act_res = [[Res("act%d_%d" % (f, t)) for t in range(2)] for f in range(FC)]

    def act_view(f, t2):
        return scr[:, (f * 2 + t2) * TT:(f * 2 + t2 + 1) * TT]

    def ffn_steps(l, which, s):
        m = 0 if which == 0 else 2
        wgu = wgu_d[which][l].rearrange("(k p) n -> p k n", p=128)
        wdn = wdn_d[which][l].rearrange("(f p) n -> p f n", p=128)

        def gu_job(wt):
            def load(slot, res, ch):
                for g in range(2):
                    dst = slot[:, 0:KC * 2 * GU_CW].rearrange("p (k g j) -> p k g j", k=KC, g=2)[:, :, g, :]
                    src = wgu[:, :, g * FF + wt * GU_CW: g * FF + (wt + 1) * GU_CW]
                    P.op("pool", lambda E, dst=dst, src=src: E.dma_start(out=dst, in_=src), writes=[res], chan=ch)
            return load

        def gu_comp(wt, tg):
            def fn(slot, res):
                W = slot[:, 0:KC * 2 * GU_CW].rearrange("p (k g j) -> p k g j", k=KC, g=2)
                for t2 in range(2):
                    tt = tg * 2 + t2
                    for half in range(GU_CW // 128):
                        f = wt * (GU_CW // 128) + half
                        pg, pgres = get_bank()
                        pu, pures = get_bank()
                        for g, (bk, bres) in enumerate(((pg, pgres), (pu, pures))):
                            for k in range(KC):
                                P.op("pe", lambda E, bk=bk, g=g, k=k, half=half, tt=tt: E.matmul(
                                    bk[:], lhsT=W[:, k, g, half * 128:(half + 1) * 128],
                                    rhs=hT[:, k, tt * TT:(tt + 1) * TT], start=(k == 0), stop=(k == KC - 1)),
                                    reads=[res, h_res[tt][k]], writes=[bres])
                        tf, tres = get_tmpf()
                        P.op("act", lambda E, tf=tf, pg=pg: E.activation(out=tf[:], in_=pg[:], func=AF.Silu),
                             reads=[pgres], writes=[tres])
                        P.op("dve", lambda E, tf=tf, pu=pu, f=f, t2=t2: E.tensor_tensor(
                            out=act_view(f, t2), in0=tf[:], in1=pu[:], op=ALU.mult),
                            reads=[tres, pures], writes=[act_res[f][t2]])
            return fn

        def dn_job(dc):
            def load(slot, res, ch):
                for a in range(2):
                    f0, f1 = a * 11, (a + 1) * 11
                    dst = slot[:, 0:FC * 128].rearrange("p (f j) -> p f j", f=FC)[:, f0:f1, :]
                    src = wdn[:, f0:f1, dc * 128:(dc + 1) * 128]
                    P.op("pool", lambda E, dst=dst, src=src: E.dma_start(out=dst, in_=src), writes=[res], chan=ch)
            return load

        def dn_comp(dc, tg):
            def fn(slot, res):
                W = slot[:, 0:FC * 128].rearrange("p (f j) -> p f j", f=FC)
                for t2 in range(2):
                    tt = tg * 2 + t2
                    bk, bres = get_bank()
                    for f in range(FC):
                        P.op("pe", lambda E, bk=bk, f=f, t2=t2: E.matmul(
                            bk[:], lhsT=W[:, f, :], rhs=act_view(f, t2), start=(f == 0), stop=(f == FC - 1)),
                            reads=[res, act_res[f][t2]], writes=[bres])
                    xs = xT[:, dc, tt * TT:(tt + 1) * TT]
                    P.op("dve", lambda E, bk=bk, xs=xs: E.scalar_tensor_tensor(
                        out=xs, in0=bk[:], scalar=mG(l, m, s, dc), in1=xs, op0=ALU.mult, op1=ALU.add),
                        reads=[bres, x_res[tt][dc], mods_res], writes=[x_res[tt][dc]])
            return fn

        for tg in range(2):
            for wt in range(NGU):
                add(gu_job(wt), gu_comp(wt, tg), tag=("ffn", l, which, s))
                if tg == 1 and wt == 1:
                    add_next_norm(l, m, s, [0, 1])
            for dc in range(KC):
                add(dn_job(dc), dn_comp(dc, tg), tag=("ffn", l, which, s))
        add_next_norm(l, m, s, [2, 3])

    def final_steps(s):
        def fn(slot, res):
            fo, _ = SM[("fin", 0)]
            for tt in range(NT):
                r, rres = emit_rstd(tt)
                for c in range(KC):
                    P.op("dve", lambda E, c=c, r=r, tt=tt: E.scalar_tensor_tensor(
                        out=xT[:, c, tt * TT:(tt + 1) * TT], in0=xT[:, c, tt * TT:(tt + 1) * TT],
                        scalar=smalls[:, fo + c:fo + c + 1], in1=r[:], op0=ALU.mult, op1=ALU.mult),
                        reads=[x_res[tt][c], rres, const_res], writes=[x_res[tt][c]])
                for i4 in range(4):
                    i = tt * 4 + i4
                    xi = nxt("xin", 2)
                    for half in range(2):
                        bk, bres = get_bank()
                        for cc in range(4):
                            c = half * 4 + cc
                            P.op("pe", lambda E, bk=bk, cc=cc, c=c, i=i: E.transpose(
                                out=bk[:, cc * 128:(cc + 1) * 128], in_=xT[:, c, i * 128:(i + 1) * 128], identity=ident[:]),
                                reads=[x_res[tt][c], const_res], writes=[bres])
                        dst = xin[xi][:, half * 512:(half + 1) * 512]
                        if half == 0:
                            P.op("dve", lambda E, dst=dst, bk=bk: E.tensor_copy(out=dst, in_=bk[:]),
                                 reads=[bres], writes=[xin_res[xi]])
                        else:
                            P.op("act", lambda E, dst=dst, bk=bk: E.activation(out=dst, in_=bk[:], func=AF.Copy),
                                 reads=[bres], writes=[xin_res[xi]])
                    dsto = out_d[s * S + i * 128: s * S + (i + 1) * 128, :]
                    P.op("sp", lambda E, xi=xi, dsto=dsto: E.dma_start(out=dsto, in_=xin[xi]),
                         reads=[xin_res[xi]], chan=xout_ch[xi])
        add(None, fn)

    HD = 64
    A_SCALE = HD ** -0.5
    C_SCALE = 96 ** -0.5
    winl = [win_d[l].rearrange("(k p) n -> p k n", p=128) for l in range(L)]
    O_OA, O_OB, O_OC = 0, 3 * S, 6 * S
    O_Q = 8 * S
    O_K = O_Q + 3 * S
    O_V = O_K + S
    O_MG = O_Q
    assert O_V + 16 * 192 <= SCR_E
    oa_v = scr[:, O_OA:O_OA + 3 * S].rearrange("p (c t) -> p c t", c=3)
    ob_v = scr[:, O_OB:O_OB + 3 * S].rearrange("p (c t) -> p c t", c=3)
    oc_v = scr[:, O_OC:O_OC + 2 * S].rearrange("p (c t) -> p c t", c=2)
    q_v = scr[:, O_Q:O_Q + 3 * S].rearrange("p (c t) -> p c t", c=3)
    k_v = scr[:, O_K:O_K + S]
    qc_v = scr[:, O_Q:O_Q + 2 * S].rearrange("p (c t) -> p c t", c=2)
    kc_v = scr[:, O_Q + 2 * S:O_Q + 4 * S].rearrange("p (c t) -> p c t", c=2)
    va_v = scr[:, O_V:O_V + 16 * 192].rearrange("p (i c) -> p i c", c=192)
    mg_v = scr[:, O_MG:O_MG + KC * TT].rearrange("p (c t) -> p c t", c=KC)
    q_res = [[Res("q%d_%d" % (c, t)) for t in range(NT)] for c in range(3)]
    k_res = [[Res("k%d_%d" % (c, t)) for t in range(NT)] for c in range(2)]
    v_res = Res("vaug")
    o_res = {m: [[Res("o%s%d_%d" % (m, c, t)) for t in range(NT)] for c in range(3)] for m in "abc"}
    mg_res = [Res("mg%d" % c) for c in range(KC)]
    NPT = 4
    pT = [nc.alloc_sbuf_tensor("pT%d" % i, [128, TT], BF16) for i in range(NPT)]
    pT_res = [Res("pT%d" % i) for i in range(NPT)]
    NPF = 2
    pF = [nc.alloc_sbuf_tensor("pF%d" % i, [128, 384], F32) for i in range(NPF)]
    pF_res = [Res("pF%d" % i) for i in range(NPF)]
    stg = [nc.alloc_sbuf_tensor("stg%d" % i, [128, TT], BF16) for i in range(2)]
    stg_res = [Res("stg%d" % i) for i in range(2)]
    tabs = [scr[:, O_OC + S:O_OC + 2 * S].bitcast(F32).rearrange("p (a t) -> p a t", a=2)]
    tab_res = [Res("tab0")]
    tab_ch = [P.chan("tab0")]
    rr.update({"pT": 0, "pF": 0, "stg": 0, "tab": 0})
    cbf = nc.alloc_sbuf_tensor("cbf", [128, 5 * 128], BF16)
    RA, R96, BONES, JREV, ZER = (cbf[:, i * 128:(i + 1) * 128] for i in range(5))
    EB = nc.alloc_sbuf_tensor("EB", [128, 6 * 384], BF16)
    esink = nc.alloc_sbuf_tensor("esink", [128, L * 6], F32)
    eb_res = Res("EB")

    def mix_consts():
        def fn(slot, res):
            cf = scr[:, 0:2 * CF_COLS].bitcast(F32)
            P.op("sp", lambda E: E.dma_start(out=cf, in_=cf_d), writes=[const_res], chan=const_ch)
            P.op("dve", lambda E: E.tensor_copy(out=cbf[:, 0:384], in_=cf[:, 0:384]), reads=[const_res], writes=[const_res])
            P.op("dve", lambda E: E.tensor_copy(out=cbf[:, 384:512], in_=cf[:, 1408:1536]), reads=[const_res], writes=[const_res])
            P.op("dve", lambda E: E.memset(cbf[:, 512:640], 0.0), writes=[const_res])
            for l in range(L):
                o, _ = SM[("sink", l)]
                P.op("act", lambda E, o=o, l=l: E.activation(out=esink[:, l * 6:(l + 1) * 6], in_=smalls[:, o:o + 6], func=AF.Exp),
                     reads=[const_res], writes=[const_res])
            bk, bres = get_bank()
            ro, _ = SM[("relb", 0)]
            P.op("pe", lambda E: E.matmul(bk[0:6, 0:512], lhsT=smalls[0:32, ro:ro + 6], rhs=cf[0:32, 384:896], start=True, stop=True),
                 reads=[const_res], writes=[bres])
            tb = tmpf[0][0:6, :]
            tbb = stg[0][0:6, :]
            P.op("act", lambda E: E.activation(out=tb, in_=bk[0:6, 0:512], func=AF.Exp), reads=[bres], writes=[eb_res, tmpf_res[0]])
            P.op("dve", lambda E: E.tensor_tensor(out=tbb, in0=tb, in1=cf[0:6, 896:1408], op=ALU.mult),
                 reads=[eb_res, const_res, tmpf_res[0]], writes=[eb_res, stg_res[0]])
            ebc = P.chan("ebc")
            P.op("sp", lambda E: E.dma_start(out=tb_d.ap(), in_=tbb), reads=[eb_res, stg_res[0]], writes=[eb_res], chan=ebc)
            for h in range(6):
                src = bass.AP(tensor=tb_d, offset=h * 512, ap=[[1, 128], [1, 384]])
                P.op("sp", lambda E, h=h, src=src: E.dma_start(out=stg[1][:, 0:384], in_=src),
                     reads=[eb_res], writes=[stg_res[1]], chan=ebc)
                b5, b5res = get_bank()
                P.op("pe", lambda E, b5=b5: E.matmul(b5[:, 0:384], lhsT=JREV, rhs=stg[1][:, 0:384], start=True, stop=True),
                     reads=[stg_res[1], const_res], writes=[b5res])
                P.op("act", lambda E, b5=b5, h=h: E.activation(out=EB[:, h * 384:(h + 1) * 384], in_=b5[:, 0:384], func=AF.Copy),
                     reads=[b5res], writes=[eb_res])
        add(None, fn)

    def load_tab(which, tt):
        ti = 0
        src = tabs_d[which][:, :, tt * TT:(tt + 1) * TT]
        P.op("sp", lambda E, ti=ti, src=src: E.dma_start(out=tabs[ti], in_=src), writes=[tab_res[ti]] + o_res["c"][1] + [act_res[14][0], act_res[14][1], act_res[15][0], act_res[15][1]], chan=tab_ch[ti])
        return tabs[ti], tab_res[ti]

    def finish_qk(ps, pres, nr, dst, dres, norm=None, rope=None, defer=False):
        si = nxt("stg", 2)
        st, sres = stg[si], stg_res[si]
        tgt = st[0:nr, :] if rope is not None else dst
        tres = sres if rope is not None else dres
        if norm is not None:
            ones_l, nd, gain = norm
            qi = nxt("sq", 2)
            P.op("act", lambda E: E.activation(out=sq[qi][0:nr, :], in_=ps[0:nr, :], func=AF.Square), reads=[pres], writes=[sq_res[qi]])
            b2, b2res = get_bank()
            P.op("pe", lambda E: E.matmul(b2[0:nr, :], lhsT=ones_l, rhs=sq[qi][0:nr, :], start=True, stop=True),
                 reads=[sq_res[qi], const_res], writes=[b2res])
            ri = nxt("rstd", NRSTD)
            P.op("act", lambda E: E.activation(out=rstd[ri][0:nr, :], in_=b2[0:nr, :], func=AF.Ln, bias=epsb[0:nr, :], scale=1.0 / nd),
                 reads=[b2res, const_res], writes=[rstd_res[ri]])
            P.op("act", lambda E: E.activation(out=rstd[ri][0:nr, :], in_=rstd[ri][0:nr, :], func=AF.Exp, scale=-0.5),
                 reads=[rstd_res[ri]], writes=[rstd_res[ri]])
            P.op("dve", lambda E: E.scalar_tensor_tensor(out=tgt, in0=ps[0:nr, :], scalar=gain, in1=rstd[ri][0:nr, :],
                                                         op0=ALU.mult, op1=ALU.mult),
                 reads=[pres, rstd_res[ri], const_res], writes=[tres])
        else:
            P.op("act", lambda E: E.activation(out=tgt, in_=ps[0:nr, :], func=AF.Copy), reads=[pres], writes=[tres])
        if rope is None:
            return None

        def part2():
            Rl, tab, tabres = rope
            b3, b3res = get_bank()
            P.op("pe", lambda E: E.matmul(b3[0:nr, :], lhsT=Rl, rhs=st[0:nr, :], start=True, stop=True),
                 reads=[sres, const_res], writes=[b3res])
            t1, t1res = get_tmpf()
            t2, t2res = get_tmpf()
            P.op("dve", lambda E: E.tensor_tensor(out=t1[0:nr, :], in0=st[0:nr, :], in1=tab[0:nr, 0, :], op=ALU.mult),
                 reads=[sres, tabres], writes=[t1res])
            P.op("dve", lambda E: E.tensor_tensor(out=t2[0:nr, :], in0=b3[0:nr, :], in1=tab[0:nr, 1, :], op=ALU.mult),
                 reads=[b3res, tabres], writes=[t2res])
            P.op("dve", lambda E: E.tensor_tensor(out=dst, in0=t1[0:nr, :], in1=t2[0:nr, :], op=ALU.add),
                 reads=[t1res, t2res], writes=[dres])
        if defer:
            return part2
        part2()
        return None

    def proj(W, wres, ncolchunk, tt, M=128, acc=False):
        bk, bres = get_bank(acc)
        for k in range(KC):
            P.op("pe", lambda E, k=k: E.matmul(bk[0:M, :], lhsT=W[:, k, ncolchunk], rhs=hT[:, k, tt * TT:(tt + 1) * TT],
                                               start=(k == 0), stop=(k == KC - 1)),
                 reads=[wres, h_res[tt][k]], writes=[bres])
        return bk, bres

    def proj_v(W, wres, cols0, ncols, i):
        bk, bres = get_bank()
        for k in range(KC):
            P.op("pe", lambda E, k=k: E.matmul(bk[:, 0:ncols], lhsT=hT[:, k, i * 128:(i + 1) * 128], rhs=W[:, k, cols0:cols0 + ncols],
                                               start=(k == 0), stop=(k == KC - 1)),
                 reads=[wres, h_res[i // 4][k]], writes=[bres])
        return bk, bres

    LA = 3

    def attn_dense(qsel, ksel, K, groups, scale, omap):
        its = [(g, tt, kc) for g in range(len(groups)) for tt in range(NT) for kc in range(S // 128)]
        obs = {}
        pend = []
        la = 1 if max(len(g) for g in groups) > 1 else 2

        def emit_S(g, tt, kc):
            pis = []
            sbs = []
            for (h, vsel, half) in groups[g]:
                if kc == 0:
                    obs[(h, tt)] = get_bank(acc=True)
                qa, qres = qsel(h, tt)
                ka, kres = ksel(h, kc)
                sb, sbres = get_bank()
                P.op("pe", lambda E, sb=sb, ka=ka, qa=qa: E.matmul(sb[:], lhsT=ka, rhs=qa, start=True, stop=True), reads=[kres, qres], writes=[sbres])
                sbs.append((sb, sbres))
            for (sb, sbres) in sbs:
                pi = nxt("pT", NPT)
                P.op("act", lambda E, sb=sb, pi=pi: E.activation(out=pT[pi][:], in_=sb[:], func=AF.Exp, scale=scale), reads=[sbres], writes=[pT_res[pi]])
                pis.append(pi)
            return pis

        def emit_PV(g, tt, kc, pis):
            for (h, vsel, half), pi in zip(groups[g], pis):
                ob, obres = obs[(h, tt)]
                va = vsel(kc)
                P.op("pe", lambda E, ob=ob, va=va, pi=pi: E.matmul(ob[:], lhsT=va, rhs=pT[pi][:], start=(kc == 0), stop=(kc == S // 128 - 1)),
                     reads=[v_res, pT_res[pi]], writes=[obres])
            if kc == S // 128 - 1:
                for (h, vsel, half) in groups[g]:
                    ob, obres = obs[(h, tt)]
                    emit_onorm(ob, obres, half, omap(h, tt), None)

        for (g, tt, kc) in its:
            pend.append((g, tt, kc, emit_S(g, tt, kc)))
            if len(pend) > la:
                emit_PV(*pend.pop(0))
        while pend:
            emit_PV(*pend.pop(0))

    def emit_onorm(ob, obres, half, out, sink_ap, ncols=TT):
        oap, ores = out
        olo, dlo = (0, 64) if half == 0 else (64, 0)
        tf, tres = get_tmpf()
        if sink_ap is not None:
            sap = sink_ap(dlo)
            P.op("act", lambda E: E.activation(out=tf[dlo:dlo + 64, 0:ncols], in_=ob[dlo:dlo + 64, 0:ncols], func=AF.Ln, bias=sap, scale=1.0),
                 reads=[obres, const_res], writes=[tres])
            P.op("act", lambda E: E.activation(out=tf[dlo:dlo + 64, 0:ncols], in_=tf[dlo:dlo + 64, 0:ncols], func=AF.Exp, scale=-1.0),
                 reads=[tres], writes=[tres])
        else:
            P.op("dve", lambda E: E.reciprocal(out=tf[dlo:dlo + 64, 0:ncols], in_=ob[dlo:dlo + 64, 0:ncols]), reads=[obres], writes=[tres])
        P.op("dve", lambda E: E.tensor_tensor(out=oap, in0=ob[olo:olo + 64, 0:ncols], in1=tf[dlo:dlo + 64, 0:ncols], op=ALU.mult),
             reads=[obres, tres], writes=[ores])

    def vaug_ones():
        P.op("dve", lambda E: E.memset(va_v[:, :, 64:128], 1.0), writes=[v_res])

    def vsel_ab(half):
        return (lambda kc: va_v[:, kc, 0:128]) if half == 0 else (lambda kc: va_v[:, kc, 64:192])

    def mixer_ab_steps(l, s, which):
        base = 0 if which == "a" else 640
        ov = oa_v if which == "a" else ob_v

        def job1(slot, res, ch):
            W = slot[:, 0:KC * 512].rearrange("p (k j) -> p k j", k=KC)
            for j in range(3):
                for half in range(2):
                    h = j + 3 * half
                    src = winl[l][:, :, base + h * 64: base + (h + 1) * 64]
                    dst = W[:, :, j * 128 + half * 64: j * 128 + (half + 1) * 64]
                    P.op("pool", lambda E, dst=dst, src=src: E.dma_start(out=dst, in_=src), writes=[res], chan=ch)
            src = winl[l][:, :, base + 384: base + 512]
            P.op("pool", lambda E, src=src: E.dma_start(out=W[:, :, 384:512], in_=src), writes=[res], chan=ch)

        def comp1(slot, res):
            W = slot[:, 0:KC * 512].rearrange("p (k j) -> p k j", k=KC)
            if which == "a":
                go, _ = SM[("aqn", l)]
                ko, _ = SM[("akn", l)]
            items = [(tt, c) for tt in range(NT) for c in range(4)]
            pj = {}
            tabst = {}

            def do_proj(idx):
                tt, c = items[idx]
                pj[idx] = proj(W, res, slice(c * 128, (c + 1) * 128), tt)
                hold(pj[idx][0])

            do_proj(0)
            prev2 = None
            for idx, (tt, c) in enumerate(items):
                if idx + 1 < len(items):
                    do_proj(idx + 1)
                bk, bres = pj.pop(idx)
                if c < 3:
                    dst, dres = q_v[:, c, tt * TT:(tt + 1) * TT], q_res[c][tt]
                else:
                    dst, dres = k_v[:, tt * TT:(tt + 1) * TT], k_res[0][tt]
                if which == "a":
                    if c == 0:
                        if prev2 is not None:
                            prev2()
                            prev2 = None
                        tabst["t"] = load_tab(0, tt)
                    tab, tabres = tabst["t"]
                    g = smalls[:, go:go + 1] if c < 3 else smalls[:, ko:ko + 1]
                    p2 = finish_qk(bk, bres, 128, dst, dres, norm=(BONES, HD, g), rope=(RA, tab, tabres), defer=True)
                    release(bk)
                    if prev2 is not None:
                        prev2()
                    prev2 = p2
                else:
                    finish_qk(bk, bres, 128, dst, dres)
                    release(bk)
            if prev2 is not None:
                prev2()

        def job2(slot, res, ch):
            W = slot[:, 0:KC * 128].rearrange("p (k j) -> p k j", k=KC)
            src = winl[l][:, :, base + 512: base + 640]
            P.op("pool", lambda E, src=src: E.dma_start(out=W, in_=src), writes=[res], chan=ch)

        def comp2(slot, res):
            W = slot[:, 0:KC * 128].rearrange("p (k j) -> p k j", k=KC)
            vaug_ones()
            for i in range(S // 128):
                bk, bres = proj_v(W, res, 0, 128, i)
                P.op("act", lambda E, bk=bk, i=i: E.activation(out=va_v[:, i, 0:64], in_=bk[:, 0:64], func=AF.Copy), reads=[bres], writes=[v_res])
                P.op("dve", lambda E, bk=bk, i=i: E.tensor_copy(out=va_v[:, i, 128:192], in_=bk[:, 64:128]), reads=[bres], writes=[v_res])

        def attn(slot, res):
            if which == "a":
                def qsel(h, tt):
                    j, half = h % 3, h // 3
                    return q_v[half * 64:(half + 1) * 64, j, tt * TT:(tt + 1) * TT], q_res[j][tt]

                def ksel(h, kc):
                    half = h // 3
                    return k_v[half * 64:(half + 1) * 64, kc * 128:(kc + 1) * 128], k_res[0][kc // 4]

                def omap(h, tt):
                    j, half = h % 3, h // 3
                    return ov[half * 64:(half + 1) * 64, j, tt * TT:(tt + 1) * TT], o_res["a"][j][tt]
                attn_dense(qsel, ksel, 64, [[(j, vsel_ab(0), 0), (j + 3, vsel_ab(1), 1)] for j in range(3)], A_SCALE, omap)
            else:
                attn_window(l)
        add(job1, comp1)
        add(job2, comp2)
        add(None, attn)

    def attn_window(l):
        its = []
        for h in range(6):
            for tt in range(NT):
                kcs = [kc for kc in range(4 * tt - 1, 4 * tt + 5) if 0 <= kc < S // 128]
                for n_i, kc in enumerate(kcs):
                    its.append((h, tt, kc, n_i == 0, n_i == len(kcs) - 1))
        obs = {}
        pend = []

        def emit_S(h, tt, kc, first, last):
            j, half = h % 3, h // 3
            if first:
                ob, obres = get_bank(acc=True)
                obs[(h, tt)] = (ob, obres)
                P.op("pe", lambda E: E.matmul(ob[:], lhsT=ZER, rhs=hT[:, 0, 0:TT], start=True, stop=False),
                     reads=[const_res, h_res[0][0]], writes=[obres])
            qb0 = max(kc - 1, 4 * tt)
            qb1 = min(kc + 1, 4 * tt + 3)
            ncol = (qb1 - qb0 + 1) * 128
            q0 = qb0 * 128
            e0 = (qb0 - (kc - 1)) * 128
            c0 = q0 - tt * TT
            sb, sbres = get_bank()
            P.op("pe", lambda E: E.matmul(
                sb[:, 0:ncol], lhsT=k_v[half * 64:(half + 1) * 64, kc * 128:(kc + 1) * 128],
                rhs=q_v[half * 64:(half + 1) * 64, j, q0:q0 + ncol], start=True, stop=True),
                reads=[k_res[0][kc // 4], q_res[j][tt]], writes=[sbres])
            fi = nxt("pF", NPF)
            P.op("act", lambda E: E.activation(out=pF[fi][:, 0:ncol], in_=sb[:, 0:ncol], func=AF.Exp, scale=A_SCALE),
                 reads=[sbres], writes=[pF_res[fi]])
            pi = nxt("pT", NPT)
            P.op("pool" if (kc % 2 == 0) else "dve", lambda E: E.tensor_tensor(
                out=pT[pi][:, 0:ncol], in0=pF[fi][:, 0:ncol], in1=EB[:, h * 384 + e0:h * 384 + e0 + ncol], op=ALU.mult),
                reads=[pF_res[fi], eb_res], writes=[pT_res[pi]])
            return (pi, c0, ncol)

        def emit_PV(h, tt, kc, first, last, info):
            pi, c0, ncol = info
            j, half = h % 3, h // 3
            ob, obres = obs[(h, tt)]
            va = vsel_ab(half)(kc)
            P.op("pe", lambda E: E.matmul(ob[:, c0:c0 + ncol], lhsT=va, rhs=pT[pi][:, 0:ncol], start=False, stop=last),
                 reads=[v_res, pT_res[pi]], writes=[obres])
            if last:
                sk = (lambda dlo: esink[dlo:dlo + 64, l * 6 + h:l * 6 + h + 1])
                emit_onorm(ob, obres, half, (ob_v[half * 64:(half + 1) * 64, j, tt * TT:(tt + 1) * TT], o_res["b"][j][tt]), sk)

        for it in its:
            pend.append(it + (emit_S(*it),))
            if len(pend) > LA:
                emit_PV(*pend.pop(0))
        while pend:
            emit_PV(*pend.pop(0))

    NW = 256 + 128 + 96
    C_WQ = KC * NW
    C_WKN = C_WQ + 384
    C_WV = C_WKN + 192
    assert C_WV + 128 <= SLOT_E

    def mixer_c_steps(l, s, p):
        def job(slot, res, ch):
            W = slot[:, 0:KC * NW].rearrange("p (k j) -> p k j", k=KC)
            P.op("pool", lambda E: E.dma_start(out=W[:, :, 384:448], in_=zer_d.rearrange("p (k j) -> p k j", k=KC)), writes=[res], chan=ch)
            src = winl[l][:, :, 1280:1664]
            P.op("pool", lambda E: E.dma_start(out=W[:, :, 0:384], in_=src), writes=[res], chan=ch)
            src2 = winl[l][:, :, 1664:1696]
            P.op("pool", lambda E: E.dma_start(out=W[:, :, 448:480], in_=src2), writes=[res], chan=ch)
            wq = cqup_d[l].rearrange("(k p) n -> p k n", p=128)[:, :, p * 192:(p + 1) * 192]
            P.op("pool", lambda E: E.dma_start(out=slot[:, C_WQ:C_WQ + 384].rearrange("p (k j) -> p k j", k=2), in_=wq), writes=[res], chan=ch)
            for hh in range(2):
                hd = 2 * p + hh
                P.op("pool", lambda E, hh=hh, hd=hd: E.dma_start(out=slot[:, C_WKN + hh * 96: C_WKN + hh * 96 + 64], in_=ckvup_d[l][:, hd * 128: hd * 128 + 64]),
                     writes=[res], chan=ch)
                P.op("pool", lambda E, hh=hh: E.dma_start(out=slot[:, C_WKN + hh * 96 + 64: C_WKN + (hh + 1) * 96], in_=zer_d[:, 0:32]),
                     writes=[res], chan=ch)
                P.op("pool", lambda E, hh=hh, hd=hd: E.dma_start(out=slot[:, C_WV + hh * 64: C_WV + (hh + 1) * 64], in_=ckvup_d[l][:, hd * 128 + 64: hd * 128 + 128]),
                     writes=[res], chan=ch)

        def c_tile(slot, res, tt):
            W1 = slot[:, 0:KC * NW].rearrange("p (k j) -> p k j", k=KC)
            Wq = slot[:, C_WQ:C_WQ + 384].rearrange("p (k j) -> p k j", k=2)
            qo, _ = SM[("cqn", l)]
            kvo, _ = SM[("ckvn", l)]
            tab, tabres = load_tab(1, tt)
            ql = [proj(W1, res, slice(c * 128, (c + 1) * 128), tt) for c in range(2)]
            kvl = proj(W1, res, slice(256, 384), tt, acc=True)
            bkks = []
            for hh in range(2):
                bkk, bkres = get_bank(acc=True)
                for k in range(KC):
                    P.op("pe", lambda E, k=k, bkk=bkk: E.matmul(bkk[0:96, :], lhsT=W1[:, k, 384:480], rhs=hT[:, k, tt * TT:(tt + 1) * TT], start=(k == 0), stop=False),
                         reads=[res, h_res[tt][k]], writes=[bkres])
                bkks.append((bkk, bkres))
            b2, b2res = get_bank()
            for c in range(2):
                qi = nxt("sq", 2)
                P.op("act", lambda E, qi=qi, c=c: E.activation(out=sq[qi][:], in_=ql[c][0][:], func=AF.Square), reads=[ql[c][1]], writes=[sq_res[qi]])
                P.op("pe", lambda E, qi=qi, c=c: E.matmul(b2[:], lhsT=ones_bf[:], rhs=sq[qi][:], start=(c == 0), stop=(c == 1)),
                     reads=[sq_res[qi], const_res], writes=[b2res])
            ri = nxt("rstd", NRSTD)
            P.op("act", lambda E: E.activation(out=rstd[ri][:], in_=b2[:], func=AF.Ln, bias=epsb[:], scale=1.0 / 256), reads=[b2res, const_res], writes=[rstd_res[ri]])
            P.op("act", lambda E: E.activation(out=rstd[ri][:], in_=rstd[ri][:], func=AF.Exp, scale=-0.5), reads=[rstd_res[ri]], writes=[rstd_res[ri]])
            qln = []
            for c in range(2):
                si = nxt("stg", 2)
                P.op("dve", lambda E, si=si, c=c: E.scalar_tensor_tensor(out=stg[si][:], in0=ql[c][0][:], scalar=smalls[:, qo + c:qo + c + 1], in1=rstd[ri][:],
                                                                     op0=ALU.mult, op1=ALU.mult),
                     reads=[ql[c][1], rstd_res[ri], const_res], writes=[stg_res[si]])
                qln.append((stg[si], stg_res[si]))
            bqs = []
            for hh in range(2):
                bq, bqres = get_bank()
                for c in range(2):
                    P.op("pe", lambda E, c=c, hh=hh, bq=bq: E.matmul(bq[0:96, :], lhsT=Wq[:, c, hh * 96:(hh + 1) * 96], rhs=qln[c][0][:], start=(c == 0), stop=(c == 1)),
                         reads=[res, qln[c][1]], writes=[bqres])
                bqs.append((bq, bqres))
            for hh in range(2):
                finish_qk(bqs[hh][0], bqs[hh][1], 96, qc_v[0:96, hh, tt * TT:(tt + 1) * TT], q_res[hh][tt], rope=(R96[0:96, 0:96], tab, tabres))
            qi2 = nxt("sq", 2)
            P.op("act", lambda E: E.activation(out=sq[qi2][:], in_=kvl[0][:], func=AF.Square), reads=[kvl[1]], writes=[sq_res[qi2]])
            b4, b4res = get_bank()
            P.op("pe", lambda E: E.matmul(b4[:], lhsT=ones_bf[:], rhs=sq[qi2][:], start=True, stop=True), reads=[sq_res[qi2], const_res], writes=[b4res])
            ri2 = nxt("rstd", NRSTD)
            P.op("act", lambda E: E.activation(out=rstd[ri2][:], in_=b4[:], func=AF.Ln, bias=epsb[:], scale=1.0 / 128), reads=[b4res, const_res], writes=[rstd_res[ri2]])
            P.op("act", lambda E: E.activation(out=rstd[ri2][:], in_=rstd[ri2][:], func=AF.Exp, scale=-0.5), reads=[rstd_res[ri2]], writes=[rstd_res[ri2]])
            pk = nxt("pT", NPT)
            kvn, kvnres = pT[pk], pT_res[pk]
            P.op("dve", lambda E: E.scalar_tensor_tensor(out=kvn[:], in0=kvl[0][:], scalar=smalls[:, kvo:kvo + 1], in1=rstd[ri2][:], op0=ALU.mult, op1=ALU.mult),
                 reads=[kvl[1], rstd_res[ri2], const_res], writes=[kvnres])
            for hh in range(2):
                bkk, bkres = bkks[hh]
                P.op("pe", lambda E, hh=hh, bkk=bkk: E.matmul(bkk[0:96, :], lhsT=slot[:, C_WKN + hh * 96: C_WKN + (hh + 1) * 96], rhs=kvn[:], start=False, stop=True),
                     reads=[res, kvnres], writes=[bkres])
                finish_qk(bkk, bkres, 96, kc_v[0:96, hh, tt * TT:(tt + 1) * TT], k_res[hh][tt], rope=(R96[0:96, 0:96], tab, tabres))
            for i4 in range(4):
                i = tt * 4 + i4
                bv, bvres = get_bank()
                P.op("pe", lambda E, i4=i4, bv=bv: E.matmul(bv[:, 0:128], lhsT=kvn[:, i4 * 128:(i4 + 1) * 128], rhs=slot[:, C_WV:C_WV + 128], start=True, stop=True),
                     reads=[res, kvnres], writes=[bvres])
                P.op("act", lambda E, bv=bv, i=i: E.activation(out=va_v[:, i, 0:64], in_=bv[:, 0:64], func=AF.Copy), reads=[bvres], writes=[v_res])
                P.op("dve", lambda E, bv=bv, i=i: E.tensor_copy(out=va_v[:, i, 128:192], in_=bv[:, 64:128]), reads=[bvres], writes=[v_res])

        def comp(slot, res):
            vaug_ones()
            P.op("dve", lambda E: E.memset(scr[64:128, O_Q:O_Q + 4 * S], 0.0),
                 writes=[q_res[hh][t] for hh in range(2) for t in range(NT)] + [k_res[hh][t] for hh in range(2) for t in range(NT)])
            for tt in range(NT):
                c_tile(slot, res, tt)

        def attn(slot, res):
            def qsel(h, tt):
                return qc_v[:, h, tt * TT:(tt + 1) * TT], q_res[h][tt]

            def ksel(h, kc):
                return kc_v[:, h, kc * 128:(kc + 1) * 128], k_res[h][kc // 4]

            def omap(h, tt):
                return oc_v[h * 64:(h + 1) * 64, p, tt * TT:(tt + 1) * TT], o_res["c"][p][tt]
            attn_dense(qsel, ksel, 96, [[(hh, vsel_ab(hh), hh)] for hh in range(2)], C_SCALE, omap)
        add(job, comp)
        add(None, attn)

    def merge_steps(l, s):
        for tt in range(NT):
            for dc in range(KC):
                def job(slot, res, ch, dc=dc):
                    G = slot[:, 0:3 * KC * 128].rearrange("p (m k j) -> p m k j", m=3, k=KC)
                    for m in range(3):
                        src = winl[l][:, :, 1696 + m * 1024 + dc * 128: 1696 + m * 1024 + (dc + 1) * 128]
                        P.op("pool", lambda E, m=m, src=src: E.dma_start(out=G[:, m, :, :], in_=src), writes=[res], chan=ch)
                    BR = slot[:, 3072:4096].rearrange("p (c j) -> p c j", c=8)
                    for mi, wd in enumerate((wbra_d, wbrb_d)):
                        for half in range(2):
                            src = wd[l].rearrange("(half j p) n -> half p j n", half=2, j=3)[half, :, :, dc * 128:(dc + 1) * 128]
                            P.op("pool", lambda E, mi=mi, half=half, src=src: E.dma_start(out=BR[half * 64:(half + 1) * 64, mi * 3:(mi + 1) * 3, :], in_=src),
                                 writes=[res], chan=ch)
                    src = wbrc_d[l].rearrange("(j p) n -> p j n", p=128)[:, :, dc * 128:(dc + 1) * 128]
                    P.op("pool", lambda E, src=src: E.dma_start(out=BR[:, 6:8, :], in_=src), writes=[res], chan=ch)

                def comp(slot, res, dc=dc, tt=tt):
                    G = slot[:, 0:3 * KC * 128].rearrange("p (m k j) -> p m k j", m=3, k=KC)
                    BR = slot[:, 3072:4096].rearrange("p (c j) -> p c j", c=8)
                    sgs = []
                    for m in range(3):
                        bk, bres = get_bank()
                        for k in range(KC):
                            P.op("pe", lambda E, bk=bk, m=m, k=k: E.matmul(bk[:], lhsT=G[:, m, k, :], rhs=hT[:, k, tt * TT:(tt + 1) * TT], start=(k == 0), stop=(k == KC - 1)),
                                 reads=[res, h_res[tt][k]], writes=[bres])
                        tf, tres = get_tmpf()
                        P.op("act", lambda E, bk=bk, tf=tf: E.activation(out=tf[:], in_=bk[:], func=AF.Sigmoid), reads=[bres], writes=[tres])
                        sgs.append((tf, tres))
                    acc = None
                    for m, (ov, nch, key) in enumerate(((oa_v, 3, "a"), (ob_v, 3, "b"), (oc_v, 2, "c"))):
                        bk, bres = get_bank()
                        for c in range(nch):
                            P.op("pe", lambda E, bk=bk, m=m, c=c, ov=ov, nch=nch: E.matmul(bk[:], lhsT=BR[:, m * 3 + c, :], rhs=ov[:, c, tt * TT:(tt + 1) * TT],
                                                                                       start=(c == 0), stop=(c == nch - 1)),
                                 reads=[res, o_res[key][c][tt]], writes=[bres])
                        tf, tres = sgs[m]
                        P.op("dve", lambda E, bk=bk, tf=tf: E.tensor_tensor(out=tf[:], in0=tf[:], in1=bk[:], op=ALU.mult), reads=[tres, bres], writes=[tres])
                    t0, t0res = sgs[0]
                    P.op("dve", lambda E: E.tensor_tensor(out=t0[:], in0=t0[:], in1=sgs[1][0][:], op=ALU.add), reads=[t0res, sgs[1][1]], writes=[t0res])
                    P.op("dve", lambda E: E.tensor_tensor(out=mg_v[:, dc, :], in0=t0[:], in1=sgs[2][0][:], op=ALU.add), reads=[t0res, sgs[2][1]], writes=[mg_res[dc]])
                add(job, comp)
            for oc2 in range(2):
                def jobo(slot, res, ch, oc2=oc2):
                    W = slot[:, 0:KC * 512].rearrange("p (k j) -> p k j", k=KC)
                    src = wout_d[l].rearrange("(k p) n -> p k n", p=128)[:, :, oc2 * 512:(oc2 + 1) * 512]
                    P.op("pool", lambda E, src=src: E.dma_start(out=W, in_=src), writes=[res], chan=ch)

                def compo(slot, res, oc2=oc2, tt=tt):
                    W = slot[:, 0:KC * 512].rearrange("p (k j) -> p k j", k=KC)
                    for c4 in range(4):
                        dcc = oc2 * 4 + c4
                        bk, bres = get_bank()
                        for k in range(KC):
                            P.op("pe", lambda E, bk=bk, k=k, c4=c4: E.matmul(bk[:], lhsT=W[:, k, c4 * 128:(c4 + 1) * 128], rhs=mg_v[:, k, :], start=(k == 0), stop=(k == KC - 1)),
                                 reads=[res, mg_res[k]], writes=[bres])
                        xs = xT[:, dcc, tt * TT:(tt + 1) * TT]
                        P.op("dve", lambda E, bk=bk, xs=xs, dcc=dcc: E.scalar_tensor_tensor(out=xs, in0=bk[:], scalar=mG(l, 1, s, dcc), in1=xs, op0=ALU.mult, op1=ALU.add),
                             reads=[bres, x_res[tt][dcc], mods_res], writes=[x_res[tt][dcc]])
                add(jobo, compo)
            if tt >= 1:
                add_next_norm(l, 1, s, [tt - 1])
        add_next_norm(l, 1, s, [NT - 1])

    def mix_steps(l, s):

        def zero_unused(slot, res):
            for key, ov, nch in (("a", oa_v, 3), ("b", ob_v, 3), ("c", oc_v, 2)):
                if key not in mixers:
                    for c in range(nch):
                        P.op("dve", lambda E, ov=ov, c=c: E.memset(ov[:, c, :], 0.0), writes=o_res[key][c])

        if "a" in mixers:
            mixer_ab_steps(l, s, "a")
        if "b" in mixers:
            mixer_ab_steps(l, s, "b")
        if "c" in mixers:
            mixer_c_steps(l, s, 0)
            mixer_c_steps(l, s, 1)
        if mixers != "abc":
            add(None, zero_unused)
        merge_steps(l, s)

    if do_mix:
        mix_consts()
    ada_pending = []
    if do_ada:
        for l in range(depth):
            for m3 in range(3):
                ada_steps(l, m3, steps if (l == 0 and m3 == 0) else ada_pending)
    n_pre = len(steps)
    for s in range(n_seq):
        load_seq_steps(s)
        for l in range(depth):
            if do_ffn:
                ffn_steps(l, 0, s)
            if do_mix:
                mix_steps(l, s)
            if do_ffn:
                ffn_steps(l, 1, s)
        final_steps(s)

    if ada_pending:
        merged = steps[:n_pre]
        npop = {}
        for st in steps[n_pre:]:
            merged.append(st)
            tag = step_tag.get(id(st[1]))
            if tag is not None and ada_pending and npop.get(tag, 0) < 36:
                npop[tag] = npop.get(tag, 0) + 1
                merged.append(ada_pending.pop(0))
        assert not ada_pending
        steps = merged

    wjobs = [(i, st[0]) for i, st in enumerate(steps) if st[0] is not None]
    jidx = {i: n for n, (i, _) in enumerate(wjobs)}
    loaded = 0

    def ensure(nj):
        nonlocal loaded
        while loaded < min(nj, len(wjobs)):
            sl = loaded % NSLOT
            wjobs[loaded][1](slots[sl], slot_res[sl], slot_ch[sl])
            loaded += 1

    for i, (wj, fn) in enumerate(steps):
        if wj is not None:
            n = jidx[i]
            ensure(n + NSLOT)
            fn(slots[n % NSLOT], slot_res[n % NSLOT])
        else:
            fn(None, None)
    stats = P.emit(final_chans=xout_ch)
    return nc, stats


_CACHE = {}


_CONSTS = {}


def make_consts():
    if _CONSTS:
        return _CONSTS
    import math
    import jax
    import jax.numpy as jnp
    theta = 10000.0
    with jax.default_device(jax.devices("cpu")[0]):
        def angles(pos, dim):
            inv = theta ** (-jnp.arange(0, dim, 2, dtype=jnp.float32) / dim)
            ang = pos.astype(jnp.float32)[:, None] * inv[None, :]
            return np.asarray(jnp.cos(ang)), np.asarray(jnp.sin(ang))
        t = jnp.arange(S)
        row_c, row_s = angles(t // 64, 32)
        col_c, col_s = angles(t % 64, 32)
        seq_c, seq_s = angles(t, 32)
        idx = np.arange(512)
        rel = jnp.asarray(128 - (idx - 127))
        nb, max_exact = 16, 8
        ret = jnp.where(rel > 0, nb, 0)
        n = jnp.abs(rel)
        large = max_exact + (jnp.log(jnp.maximum(n, 1).astype(jnp.float32) / max_exact)
                             / math.log(128 / max_exact) * (nb - max_exact)).astype(jnp.int32)
        large = jnp.minimum(large, nb - 1)
        bucket = np.asarray(ret + jnp.where(n < max_exact, n, large))
    ropeA = np.zeros((128, 2, S), np.float32)
    for p in range(128):
        d = p % 64
        cs, sn = (row_c, row_s) if d < 32 else (col_c, col_s)
        dd = d % 32
        i = dd % 16
        ropeA[p, 0] = cs[:, i]
        ropeA[p, 1] = -sn[:, i] if dd < 16 else sn[:, i]
    ropeC = np.zeros((128, 2, S), np.float32)
    ropeC[0:64, 0] = 1.0
    for p in range(64, 96):
        dd = p - 64
        i = dd % 16
        ropeC[p, 0] = seq_c[:, i]
        ropeC[p, 1] = -seq_s[:, i] if dd < 16 else seq_s[:, i]
    cf = np.zeros((128, CF_COLS), np.float32)
    for m in range(128):
        dd = m % 32
        k = m + 16 if dd < 16 else m - 16
        cf[k, m] = 1.0
    for m in range(64, 96):
        dd = m - 64
        k = m + 16 if dd < 16 else m - 16
        cf[k, 128 + m] = 1.0
    cf[0:64, 256:320] = 1.0
    cf[64:128, 320:384] = 1.0
    u = idx - 127
    valid = (u >= 0) & (u <= 256) & (idx < 511)
    for i in range(511):
        cf[bucket[i], 384 + i] = 1.0
    cf[0:6, 896:1408] = valid.astype(np.float32)[None, :]
    for m in range(128):
        cf[127 - m, 1408 + m] = 1.0
    _CONSTS.update(cf=cf, ropeA=ropeA, ropeC=ropeC, zer=np.zeros((128, KC * 64), np.float32))
    return _CONSTS


def make_in_maps(inp, n_seq, ncores):
    x = np.ascontiguousarray(inp["x"], np.float32)
    sm = pack_smalls(inp)
    shared = {"smalls": sm}
    shared.update(make_consts())
    for k in ("ada_w", "ffn1_w_gu", "ffn2_w_gu", "ffn1_w_down", "ffn2_w_down", "w_in", "c_w_q_up", "c_w_kv_up",
              "w_br_a", "w_br_b", "w_br_c", "w_out"):
        shared[k] = np.ascontiguousarray(inp[k], np.float32)
    c = np.asarray(inp["c"], np.float32)
    in_maps = []
    for i in range(ncores):
        cs = c[i * n_seq:(i + 1) * n_seq]
        ct = cs.reshape(n_seq, KC, 128).transpose(2, 1, 0)
        m = dict(shared)
        m["x"] = x[i * n_seq:(i + 1) * n_seq].reshape(n_seq * S, D)
        m["cT"] = np.ascontiguousarray(ct.reshape(128, KC * n_seq))
        in_maps.append(m)
    return in_maps


def kernel(**inp):
    B = inp["x"].shape[0]
    n_seq = B // NCORES
    if "nc" not in _CACHE:
        _CACHE["nc"] = build(n_seq=n_seq)[0]
    nc = _CACHE["nc"]
    in_maps = make_in_maps(inp, n_seq, NCORES)
    res = run_bass_kernel_spmd(nc, in_maps, core_ids=list(range(NCORES)))
    out = np.concatenate([r["out"].reshape(n_seq, S, D) for r in res.results], axis=0)
    return out.astype(np.float32)
```

```python
import numpy as np
import concourse.bass as bass
import concourse.mybir as mybir
from concourse.bass_utils import run_bass_kernel_spmd

F32 = mybir.dt.float32
BF16 = mybir.dt.bfloat16
AF = mybir.ActivationFunctionType
ALU = mybir.AluOpType

D = 1024
S = 2048
KC = 8
FF = 2816
FC = 22
TT = 512
NT = 4
L = 2
EPS = 1e-6
IN_COLS = 4768
NCORES = 8


class Res:
    __slots__ = ("name", "lw", "rd")

    def __init__(self, name):
        self.name = name
        self.lw = None
        self.rd = {}


class Chan:
    def __init__(self, name):
        self.name = name
        self.n = 0
        self.sem = None


class Prog:
    ENG = ("pe", "act", "dve", "pool", "sp")

    def __init__(self, nc, serialize=False):
        self.nc = nc
        self.ops = []
        self.serialize = serialize
        self.eng = {"pe": nc.tensor, "act": nc.scalar, "dve": nc.vector,
                    "pool": nc.gpsimd, "sp": nc.sync}
        self.chans = []

    def chan(self, name):
        c = Chan(name)
        self.chans.append(c)
        return c

    def op(self, eng, fn, reads=(), writes=(), chan=None):
        idx = len(self.ops)
        deps = set()
        rkey = ("c", id(chan)) if chan is not None else eng
        for r in reads:
            if r.lw is not None:
                deps.add((r.lw, True))
        for w in writes:
            if w.lw is not None:
                pw = self.ops[w.lw]
                if not (chan is not None and pw["chan"] is chan and pw["eng"] == eng):
                    deps.add((w.lw, False))
            for k, i in w.rd.items():
                deps.add((i, False))
        if self.serialize and idx > 0:
            deps.add((idx - 1, True))
        for r in reads:
            r.rd[rkey] = idx
        for w in writes:
            w.lw = idx
            w.rd = {}
        ordn = None
        if chan is not None:
            chan.n += 1
            ordn = chan.n
        self.ops.append(dict(eng=eng, fn=fn, deps=deps, chan=chan, ordn=ordn, sig=False))
        return idx

    def emit(self, final_chans=()):
        nc = self.nc
        ops = self.ops
        real = []
        for i, o in enumerate(ops):
            rl = []
            for (j, raw) in o["deps"]:
                p = ops[j]
                if p["chan"] is None:
                    if p["eng"] == o["eng"] and o["chan"] is None:
                        if o["eng"] == "pe":
                            continue
                    p["sig"] = True
                rl.append(j)
            real.append(rl)
        sem = {e: nc.alloc_semaphore("s_" + e) for e in ("pe", "act", "dve", "pool")}
        for c in self.chans:
            c.sem = nc.alloc_semaphore("c_" + c.name)
        cnt = {e: 0 for e in sem}
        sigval = {}
        seen = {e: {} for e in self.ENG}
        nwait = 0
        for i, o in enumerate(ops):
            e = o["eng"]
            E = self.eng[e]
            need = {}
            for j in real[i]:
                p = ops[j]
                if p["chan"] is not None:
                    key = ("c", id(p["chan"]))
                    s, v = p["chan"].sem, 16 * p["ordn"]
                else:
                    key = p["eng"]
                    s, v = sem[p["eng"]], sigval[j]
                if seen[e].get(key, 0) >= v:
                    continue
                if key not in need or need[key][1] < v:
                    need[key] = (s, v)
            for key, (s, v) in need.items():
                E.wait_ge(s, v)
                seen[e][key] = v
                nwait += 1
            ins = o["fn"](E)
            if o["chan"] is not None:
                ins.then_inc(o["chan"].sem, 16)
            elif o["sig"]:
                cnt[e] += 1
                sigval[i] = cnt[e]
                ins.then_inc(sem[e], 1)
        for c in final_chans:
            nc.sync.wait_ge(c.sem, 16 * c.n)
        self.stats = dict(n_ops=len(ops), n_wait=nwait, cnt=dict(cnt))
        return self.stats


def fm(v):
    v = np.asarray(v, np.float32)
    return np.ascontiguousarray(v.reshape(-1, 128).T)


SM = {}
_o = 0
for _l in range(L):
    for _n, _w in (("nf1", 8), ("nmix", 8), ("nf2", 8), ("adab", 72), ("aqn", 1), ("akn", 1), ("cqn", 2), ("ckvn", 1), ("sink", 6)):
        SM[(_n, _l)] = (_o, _w)
        _o += _w
SM[("fin", 0)] = (_o, 8)
_o += 8
SM[("relb", 0)] = (_o, 6)
_o += 6
CF_COLS = 1536
SM_COLS = _o


def pack_smalls(inp):
    sm = np.zeros((128, SM_COLS), np.float32)

    def put(key, arr):
        o, w = SM[key]
        assert arr.shape == (128, w), (key, arr.shape)
        sm[:, o:o + w] = arr

    for l in range(L):
        put(("nf1", l), fm(inp["norm_ffn1"][l]))
        put(("nmix", l), fm(inp["norm_mix"][l]))
        put(("nf2", l), fm(inp["norm_ffn2"][l]))
        put(("adab", l), fm(inp["ada_b"][l]))
        put(("aqn", l), np.tile(np.asarray(inp["a_q_norm"][l], np.float32), 2).reshape(128, 1))
        put(("akn", l), np.tile(np.asarray(inp["a_k_norm"][l], np.float32), 2).reshape(128, 1))
        put(("cqn", l), fm(inp["c_q_lat_norm"][l]))
        put(("ckvn", l), fm(inp["c_kv_lat_norm"][l]))
        put(("sink", l), np.broadcast_to(np.asarray(inp["b_sink"][l], np.float32)[None, :], (128, 6)))
    put(("fin", 0), fm(inp["final_norm"]))
    o, w = SM[("relb", 0)]
    sm[0:32, o:o + w] = np.asarray(inp["rel_bias"], np.float32)
    return sm


NSLOT = 3
SLOT_E = 4608


def build(n_seq=4, depth=L, do_mix=True, serialize=False, do_ffn=True, do_ada=True, ffn_parts="ngd", mixers="abc"):
    nc = bass.Bass("TRN2", target_bir_lowering=False)
    NTOK = n_seq * S
    x_d = nc.dram_tensor("x", [NTOK, D], F32, kind="ExternalInput").ap()
    ct_d = nc.dram_tensor("cT", [128, KC * n_seq], F32, kind="ExternalInput").ap()
    sm_d = nc.dram_tensor("smalls", [128, SM_COLS], F32, kind="ExternalInput").ap()
    adaw_d = nc.dram_tensor("ada_w", [L, D, 9 * D], F32, kind="ExternalInput").ap()
    wgu_d = [nc.dram_tensor("ffn%d_w_gu" % i, [L, D, 2 * FF], F32, kind="ExternalInput").ap() for i in (1, 2)]
    wdn_d = [nc.dram_tensor("ffn%d_w_down" % i, [L, FF, D], F32, kind="ExternalInput").ap() for i in (1, 2)]
    out_d = nc.dram_tensor("out", [NTOK, D], F32, kind="ExternalOutput").ap()
    win_d = nc.dram_tensor("w_in", [L, D, IN_COLS], F32, kind="ExternalInput").ap()
    cqup_d = nc.dram_tensor("c_w_q_up", [L, 256, 384], F32, kind="ExternalInput").ap()
    ckvup_d = nc.dram_tensor("c_w_kv_up", [L, 128, 512], F32, kind="ExternalInput").ap()
    wbra_d = nc.dram_tensor("w_br_a", [L, 384, D], F32, kind="ExternalInput").ap()
    wbrb_d = nc.dram_tensor("w_br_b", [L, 384, D], F32, kind="ExternalInput").ap()
    wbrc_d = nc.dram_tensor("w_br_c", [L, 256, D], F32, kind="ExternalInput").ap()
    wout_d = nc.dram_tensor("w_out", [L, D, D], F32, kind="ExternalInput").ap()
    cf_d = nc.dram_tensor("cf", [128, CF_COLS], F32, kind="ExternalInput").ap()
    zer_d = nc.dram_tensor("zer", [128, KC * 64], F32, kind="ExternalInput").ap()
    tabs_d = [nc.dram_tensor(n, [128, 2, S], F32, kind="ExternalInput").ap() for n in ("ropeA", "ropeC")]
    tb_d = nc.dram_tensor("tb_scratch", [6, 512], BF16)

    P = Prog(nc, serialize=serialize)

    xT = nc.alloc_sbuf_tensor("xT", [128, KC, S], F32)
    hT = nc.alloc_sbuf_tensor("hT", [128, KC, S], BF16)
    SCR_E = 12 * S + 16 * 192
    assert SCR_E >= FC * 2 * TT + 2 * 2 * D
    scr = nc.alloc_sbuf_tensor("scr", [128, SCR_E], BF16)
    slots = [nc.alloc_sbuf_tensor("wslot%d" % i, [128, SLOT_E], BF16) for i in range(NSLOT)]
    slot_res = [Res("wslot%d" % i) for i in range(NSLOT)]
    slot_ch = [P.chan("w%d" % i) for i in range(NSLOT)]
    xin = [scr[:, FC * 2 * TT + i * 2 * D: FC * 2 * TT + (i + 1) * 2 * D].bitcast(F32) for i in range(2)]
    xin_res = [Res("xin%d" % i) for i in range(2)]
    xin_ch = [P.chan("xi%d" % i) for i in range(2)]
    xout_ch = [P.chan("xo%d" % i) for i in range(2)]
    smalls = nc.alloc_sbuf_tensor("smalls_sb", [128, SM_COLS], F32)
    cT = nc.alloc_sbuf_tensor("cT_sb", [128, KC * n_seq], F32)
    condT = nc.alloc_sbuf_tensor("condT", [128, KC * n_seq], F32)
    MODW = 72 * n_seq
    mods = nc.alloc_sbuf_tensor("mods", [128, L * MODW], F32)
    modA = nc.alloc_sbuf_tensor("modA", [128, L * 3 * n_seq * KC], F32)
    modG = nc.alloc_sbuf_tensor("modG", [128, L * 3 * n_seq * KC], F32)
    ident = nc.alloc_sbuf_tensor("ident", [128, 128], F32)
    ones_bf = nc.alloc_sbuf_tensor("ones_bf", [128, 128], BF16)
    epsb = nc.alloc_sbuf_tensor("epsb", [128, 1], F32)
    sq = [nc.alloc_sbuf_tensor("sq%d" % i, [128, TT], BF16) for i in range(2)]
    sq_res = [Res("sq%d" % i) for i in range(2)]
    NRSTD = 1
    rstd = [nc.alloc_sbuf_tensor("rstd%d" % i, [128, TT], F32) for i in range(NRSTD)]
    rstd_res = [Res("rstd%d" % i) for i in range(NRSTD)]
    tmpf = [nc.alloc_sbuf_tensor("tmpf%d" % i, [128, TT], F32) for i in range(3)]
    tmpf_res = [Res("tmpf%d" % i) for i in range(3)]
    banks = [nc.alloc_psum_tensor("bank%d" % i, [128, TT], F32) for i in range(8)]
    bank_res = [Res("bank%d" % i) for i in range(8)]
    const_res = Res("consts")
    const_ch = P.chan("const")
    mods_res = Res("mods")
    x_res = [[Res("x%d_%d" % (i, c)) for c in range(KC)] for i in range(NT)]
    h_res = [[Res("h%d_%d" % (i, c)) for c in range(KC)] for i in range(NT)]

    rr = {"bank": 0, "abank": 0, "tmpf": 0, "sq": 0, "sqx": 0, "stgx": 0, "rstd": 0, "xin": 0}

    def nxt(kind, n):
        i = rr[kind]
        rr[kind] = (i + 1) % n
        return i

    held = set()

    def get_bank(acc=False):
        if acc:
            i = nxt("abank", 4)
        else:
            for _ in range(4):
                i = 4 + nxt("bank", 4)
                if i not in held:
                    break
            else:
                raise RuntimeError("no free PSUM bank")
        return banks[i], bank_res[i]

    def hold(bk):
        held.add(banks.index(bk))

    def release(bk):
        held.discard(banks.index(bk))

    def get_tmpf():
        i = nxt("tmpf", 3)
        return tmpf[i], tmpf_res[i]

    P.op("sp", lambda E: E.dma_start(out=smalls[:], in_=sm_d), writes=[const_res], chan=const_ch)
    P.op("sp", lambda E: E.dma_start(out=cT[:], in_=ct_d), writes=[const_res], chan=const_ch)
    P.op("dve", lambda E: E.memset(ident[:], 0.0), writes=[const_res])
    P.op("pool", lambda E: E.affine_select(out=ident[:], in_=ident[:], compare_op=ALU.not_equal, fill=1.0,
                                           base=0, pattern=[[-1, 128]], channel_multiplier=1),
         reads=[const_res], writes=[const_res])
    P.op("dve", lambda E: E.memset(ones_bf[:], 1.0), writes=[const_res])
    P.op("dve", lambda E: E.memset(epsb[:], EPS), writes=[const_res])
    P.op("act", lambda E: E.activation(out=condT[:], in_=cT[:], func=AF.Silu), reads=[const_res], writes=[const_res])

    def smcol(key, c=None):
        o, w = SM[key]
        if c is None:
            return smalls[:, o:o + w]
        return smalls[:, o + c:o + c + 1]

    steps = []

    step_tag = {}

    def add(wjob, fn, tag=None):
        step_tag[id(fn)] = tag
        steps.append((wjob, fn))

    ADA_CW = 256

    def ada_steps(l, m3, out_list):
        pbank = {}
        T0 = m3 * 12

        def wjob(t):
            def load(slot, res, ch):
                dst = slot[:, 0:2 * KC * ADA_CW].bitcast(F32).rearrange("p (k j) -> p k j", k=KC)
                src = adaw_d[l].rearrange("(k p) n -> p k n", p=128)[:, :, t * ADA_CW:(t + 1) * ADA_CW]
                P.op("pool", lambda E: E.dma_start(out=dst, in_=src), writes=[res], chan=ch)
            return load

        def comp(t):
            def fn(slot, res):
                if t == T0:
                    pbank["b"] = get_bank(acc=True)
                bk, bres = pbank["b"]
                W = slot[:, 0:2 * KC * ADA_CW].bitcast(F32).rearrange("p (k j) -> p k j", k=KC)
                for jj in range(ADA_CW // 128):
                    j = t * (ADA_CW // 128) + jj
                    for k in range(KC):
                        P.op("pe", lambda E, j=j, jj=jj, k=k: E.matmul(
                            bk[:, j * n_seq:(j + 1) * n_seq], lhsT=W[:, k, jj * 128:(jj + 1) * 128],
                            rhs=condT[:, k * n_seq:(k + 1) * n_seq], start=(k == 0), stop=(k == KC - 1)),
                            reads=[res, const_res], writes=[bres])
                if t == T0 + 11:
                    o, w = SM[("adab", l)]
                    j0, j1 = m3 * 24, (m3 + 1) * 24
                    src_b = smalls[:, o + j0:o + j1].unsqueeze(2).to_broadcast([128, 24, n_seq])
                    dstm = mods[:, l * MODW:(l + 1) * MODW].rearrange("p (j s) -> p j s", s=n_seq)[:, j0:j1, :]
                    srcp = bk[:, 0:MODW].rearrange("p (j s) -> p j s", s=n_seq)[:, j0:j1, :]
                    P.op("dve", lambda E: E.tensor_tensor(out=dstm, in0=srcp, in1=src_b, op=ALU.add),
                         reads=[bres, const_res], writes=[mods_res])
                    for m in (m3,):
                        gk = (("nf1", l), ("nmix", l), ("nf2", l))[m]
                        go, _ = SM[gk]
                        for s in range(n_seq):
                            base = ((l * 3 + m) * n_seq + s) * KC
                            sc = mods[:, l * MODW:(l + 1) * MODW].rearrange("p (j s) -> p j s", s=n_seq)[:, (3 * m + 1) * 8:(3 * m + 2) * 8, s]
                            gt = mods[:, l * MODW:(l + 1) * MODW].rearrange("p (j s) -> p j s", s=n_seq)[:, (3 * m + 2) * 8:(3 * m + 3) * 8, s]
                            P.op("dve", lambda E, sc=sc, base=base, go=go: E.scalar_tensor_tensor(
                                out=modA[:, base:base + KC], in0=sc, scalar=1.0, in1=smalls[:, go:go + KC],
                                op0=ALU.add, op1=ALU.mult), reads=[mods_res, const_res], writes=[mods_res])
                            rw = 1.0 if m == 1 else 0.5
                            P.op("dve", lambda E, gt=gt, base=base, rw=rw: E.tensor_scalar(
                                out=modG[:, base:base + KC], in0=gt, scalar1=rw, scalar2=None, op0=ALU.mult),
                                reads=[mods_res], writes=[mods_res])
            return fn

        for t in range(T0, T0 + 12):
            out_list.append((wjob(t), comp(t)))

    def mA(l, m, s, c):
        b = ((l * 3 + m) * n_seq + s) * KC + c
        return modA[:, b:b + 1]

    def mG(l, m, s, c):
        b = ((l * 3 + m) * n_seq + s) * KC + c
        return modG[:, b:b + 1]

    def mB(l, m, s, c):
        j = (3 * m) * 8 + c
        b = l * MODW + j * n_seq + s
        return mods[:, b:b + 1]

    def load_seq_steps(s):
        def fn(slot, res):
            for i in range(S // 128):
                xi = nxt("xin", 2)
                src = x_d[s * S + i * 128: s * S + (i + 1) * 128, :]
                P.op("sp", lambda E, xi=xi, src=src: E.dma_start(out=xin[xi], in_=src),
                     writes=[xin_res[xi]], chan=xin_ch[xi])
                for half in range(2):
                    bk, bres = get_bank()
                    for cc in range(4):
                        c = half * 4 + cc
                        P.op("pe", lambda E, bk=bk, cc=cc, c=c, xi=xi: E.transpose(
                            out=bk[:, cc * 128:(cc + 1) * 128], in_=xin[xi][:, c * 128:(c + 1) * 128], identity=ident[:]),
                            reads=[xin_res[xi], const_res], writes=[bres])
                    dst = xT[:, half * 4:(half + 1) * 4, i * 128:(i + 1) * 128]
                    srcp = bk[:, :].rearrange("p (c t) -> p c t", c=4)
                    eng = "dve" if half == 0 else "act"
                    if eng == "dve":
                        P.op("dve", lambda E, dst=dst, srcp=srcp: E.tensor_copy(out=dst, in_=srcp),
                             reads=[bres], writes=x_res[i // 4][half * 4:(half + 1) * 4])
                    else:
                        P.op("act", lambda E, dst=dst, srcp=srcp: E.activation(out=dst, in_=srcp, func=AF.Copy),
                             reads=[bres], writes=x_res[i // 4][half * 4:(half + 1) * 4])
                if subs and i % 4 == 3 and i // 4 >= 1:
                    emit_norm_tile(subs[0][0], subs[0][1], s, i // 4 - 1)
            if subs:
                emit_norm_tile(subs[0][0], subs[0][1], s, NT - 1)
        add(None, fn)

    def emit_rstd(tt):
        bk, bres = get_bank()
        pool6 = [(sq[0], sq_res[0]), (sq[1], sq_res[1])] + ([(pT[i], pT_res[i]) for i in range(NPT)] if do_mix else [])
        for c in range(KC):
            qb, qbres = pool6[nxt("sqx", len(pool6))]
            P.op("act", lambda E, qb=qb, c=c: E.activation(out=qb[:], in_=xT[:, c, tt * TT:(tt + 1) * TT], func=AF.Square),
                 reads=[x_res[tt][c]], writes=[qbres])
            P.op("pe", lambda E, qb=qb, c=c, bk=bk: E.matmul(bk[:], lhsT=ones_bf[:], rhs=qb[:], start=(c == 0), stop=(c == KC - 1)),
                 reads=[qbres, const_res], writes=[bres])
        ri = nxt("rstd", NRSTD)
        P.op("act", lambda E, ri=ri, bk=bk: E.activation(out=rstd[ri][:], in_=bk[:], func=AF.Ln, bias=epsb[:], scale=1.0 / D),
             reads=[bres, const_res], writes=[rstd_res[ri]])
        P.op("act", lambda E, ri=ri: E.activation(out=rstd[ri][:], in_=rstd[ri][:], func=AF.Exp, scale=-0.5),
             reads=[rstd_res[ri]], writes=[rstd_res[ri]])
        return rstd[ri], rstd_res[ri]

    def emit_norm_tile(l, m, s, tt):
        r, rres = emit_rstd(tt)
        for c in range(KC):
            tf, tres = get_tmpf()
            P.op("dve", lambda E, tf=tf, c=c: E.tensor_tensor(
                out=tf[:], in0=xT[:, c, tt * TT:(tt + 1) * TT], in1=r[:], op=ALU.mult),
                reads=[x_res[tt][c], rres], writes=[tres])
            P.op("act", lambda E, tf=tf, c=c: E.activation(
                out=hT[:, c, tt * TT:(tt + 1) * TT], in_=tf[:], func=AF.Identity,
                scale=mA(l, m, s, c), bias=mB(l, m, s, c)),
                reads=[tres, mods_res], writes=[h_res[tt][c]])

    subs = []
    for _l in range(depth):
        if do_ffn:
            subs.append((_l, 0))
        if do_mix:
            subs.append((_l, 1))
        if do_ffn:
            subs.append((_l, 2))
    nxt_sub = {subs[i]: (subs[i + 1] if i + 1 < len(subs) else None) for i in range(len(subs))}

    def add_next_norm(l, m, s, tiles):
        ns = nxt_sub[(l, m)]
        if ns is None:
            return

        def fn(slot, res):
            for tt in tiles:
                emit_norm_tile(ns[0], ns[1], s, tt)
        add(None, fn)

    GU_CW = 256
    NGU = FF // GU_CW
    act_res = [[Res("act%d_%d" % (f, t)) for t in range(2)] for f in range(FC)]

    def act_view(f, t2):
        return scr[:, (f * 2 + t2) * TT:(f * 2 + t2 + 1) * TT]

    def ffn_steps(l, which, s):
        m = 0 if which == 0 else 2
        wgu = wgu_d[which][l].rearrange("(k p) n -> p k n", p=128)
        wdn = wdn_d[which][l].rearrange("(f p) n -> p f n", p=128)

        def gu_job(wt):
            def load(slot, res, ch):
                for g in range(2):
                    dst = slot[:, 0:KC * 2 * GU_CW].rearrange("p (k g j) -> p k g j", k=KC, g=2)[:, :, g, :]
                    src = wgu[:, :, g * FF + wt * GU_CW: g * FF + (wt + 1) * GU_CW]
                    P.op("pool", lambda E, dst=dst, src=src: E.dma_start(out=dst, in_=src), writes=[res], chan=ch)
            return load

        def gu_comp(wt, tg):
            def fn(slot, res):
                W = slot[:, 0:KC * 2 * GU_CW].rearrange("p (k g j) -> p k g j", k=KC, g=2)
                for t2 in range(2):
                    tt = tg * 2 + t2
                    for half in range(GU_CW // 128):
                        f = wt * (GU_CW // 128) + half
                        pg, pgres = get_bank()
                        pu, pures = get_bank()
                        for g, (bk, bres) in enumerate(((pg, pgres), (pu, pures))):
                            for k in range(KC):
                                P.op("pe", lambda E, bk=bk, g=g, k=k, half=half, tt=tt: E.matmul(
                                    bk[:], lhsT=W[:, k, g, half * 128:(half + 1) * 128],
                                    rhs=hT[:, k, tt * TT:(tt + 1) * TT], start=(k == 0), stop=(k == KC - 1)),
                                    reads=[res, h_res[tt][k]], writes=[bres])
                        tf, tres = get_tmpf()
                        P.op("act", lambda E, tf=tf, pg=pg: E.activation(out=tf[:], in_=pg[:], func=AF.Silu),
                             reads=[pgres], writes=[tres])
                        P.op("dve", lambda E, tf=tf, pu=pu, f=f, t2=t2: E.tensor_tensor(
                            out=act_view(f, t2), in0=tf[:], in1=pu[:], op=ALU.mult),
                            reads=[tres, pures], writes=[act_res[f][t2]])
            return fn

        def dn_job(dc):
            def load(slot, res, ch):
                for a in range(2):
                    f0, f1 = a * 11, (a + 1) * 11
                    dst = slot[:, 0:FC * 128].rearrange("p (f j) -> p f j", f=FC)[:, f0:f1, :]
                    src = wdn[:, f0:f1, dc * 128:(dc + 1) * 128]
                    P.op("pool", lambda E, dst=dst, src=src: E.dma_start(out=dst, in_=src), writes=[res], chan=ch)
            return load

        def dn_comp(dc, tg):
            def fn(slot, res):
                W = slot[:, 0:FC * 128].rearrange("p (f j) -> p f j", f=FC)
                for t2 in range(2):
                    tt = tg * 2 + t2
                    bk, bres = get_bank()
                    for f in range(FC):
                        P.op("pe", lambda E, bk=bk, f=f, t2=t2: E.matmul(
                            bk[:], lhsT=W[:, f, :], rhs=act_view(f, t2), start=(f == 0), stop=(f == FC - 1)),
                            reads=[res, act_res[f][t2]], writes=[bres])
                    xs = xT[:, dc, tt * TT:(tt + 1) * TT]
                    P.op("dve", lambda E, bk=bk, xs=xs: E.scalar_tensor_tensor(
                        out=xs, in0=bk[:], scalar=mG(l, m, s, dc), in1=xs, op0=ALU.mult, op1=ALU.add),
                        reads=[bres, x_res[tt][dc], mods_res], writes=[x_res[tt][dc]])
            return fn

        for tg in range(2):
            for wt in range(NGU):
                add(gu_job(wt), gu_comp(wt, tg), tag=("ffn", l, which, s))
                if tg == 1 and wt == 1:
                    add_next_norm(l, m, s, [0, 1])
            for dc in range(KC):
                add(dn_job(dc), dn_comp(dc, tg), tag=("ffn", l, which, s))
        add_next_norm(l, m, s, [2, 3])

    def final_steps(s):
        def fn(slot, res):
            fo, _ = SM[("fin", 0)]
            for tt in range(NT):
                r, rres = emit_rstd(tt)
                for c in range(KC):
                    P.op("dve", lambda E, c=c, r=r, tt=tt: E.scalar_tensor_tensor(
                        out=xT[:, c, tt * TT:(tt + 1) * TT], in0=xT[:, c, tt * TT:(tt + 1) * TT],
                        scalar=smalls[:, fo + c:fo + c + 1], in1=r[:], op0=ALU.mult, op1=ALU.mult),
                        reads=[x_res[tt][c], rres, const_res], writes=[x_res[tt][c]])
                for i4 in range(4):
                    i = tt * 4 + i4
                    xi = nxt("xin", 2)
                    for half in range(2):
                        bk, bres = get_bank()
                        for cc in range(4):
                            c = half * 4 + cc
                            P.op("pe", lambda E, bk=bk, cc=cc, c=c, i=i: E.transpose(
                                out=bk[:, cc * 128:(cc + 1) * 128], in_=xT[:, c, i * 128:(i + 1) * 128], identity=ident[:]),
                                reads=[x_res[tt][c], const_res], writes=[bres])
                        dst = xin[xi][:, half * 512:(half + 1) * 512]
                        if half == 0:
                            P.op("dve", lambda E, dst=dst, bk=bk: E.tensor_copy(out=dst, in_=bk[:]),
                                 reads=[bres], writes=[xin_res[xi]])
                        else:
                            P.op("act", lambda E, dst=dst, bk=bk: E.activation(out=dst, in_=bk[:], func=AF.Copy),
                                 reads=[bres], writes=[xin_res[xi]])
                    dsto = out_d[s * S + i * 128: s * S + (i + 1) * 128, :]
                    P.op("sp", lambda E, xi=xi, dsto=dsto: E.dma_start(out=dsto, in_=xin[xi]),
                         reads=[xin_res[xi]], chan=xout_ch[xi])
        add(None, fn)

    HD = 64
    A_SCALE = HD ** -0.5
    C_SCALE = 96 ** -0.5
    winl = [win_d[l].rearrange("(k p) n -> p k n", p=128) for l in range(L)]
    O_OA, O_OB, O_OC = 0, 3 * S, 6 * S
    O_Q = 8 * S
    O_K = O_Q + 3 * S
    O_V = O_K + S
    O_MG = O_Q
    assert O_V + 16 * 192 <= SCR_E
    oa_v = scr[:, O_OA:O_OA + 3 * S].rearrange("p (c t) -> p c t", c=3)
    ob_v = scr[:, O_OB:O_OB + 3 * S].rearrange("p (c t) -> p c t", c=3)
    oc_v = scr[:, O_OC:O_OC + 2 * S].rearrange("p (c t) -> p c t", c=2)
    q_v = scr[:, O_Q:O_Q + 3 * S].rearrange("p (c t) -> p c t", c=3)
    k_v = scr[:, O_K:O_K + S]
    qc_v = scr[:, O_Q:O_Q + 2 * S].rearrange("p (c t) -> p c t", c=2)
    kc_v = scr[:, O_Q + 2 * S:O_Q + 4 * S].rearrange("p (c t) -> p c t", c=2)
    va_v = scr[:, O_V:O_V + 16 * 192].rearrange("p (i c) -> p i c", c=192)
    mg_v = scr[:, O_MG:O_MG + KC * TT].rearrange("p (c t) -> p c t", c=KC)
    q_res = [[Res("q%d_%d" % (c, t)) for t in range(NT)] for c in range(3)]
    k_res = [[Res("k%d_%d" % (c, t)) for t in range(NT)] for c in range(2)]
    v_res = Res("vaug")
    o_res = {m: [[Res("o%s%d_%d" % (m, c, t)) for t in range(NT)] for c in range(3)] for m in "abc"}
    mg_res = [Res("mg%d" % c) for c in range(KC)]
    NPT = 4
    pT = [nc.alloc_sbuf_tensor("pT%d" % i, [128, TT], BF16) for i in range(NPT)]
    pT_res = [Res("pT%d" % i) for i in range(NPT)]
    NPF = 2
    pF = [nc.alloc_sbuf_tensor("pF%d" % i, [128, 384], F32) for i in range(NPF)]
    pF_res = [Res("pF%d" % i) for i in range(NPF)]
    stg = [nc.alloc_sbuf_tensor("stg%d" % i, [128, TT], BF16) for i in range(2)]
    stg_res = [Res("stg%d" % i) for i in range(2)]
    tabs = [scr[:, O_OC + S:O_OC + 2 * S].bitcast(F32).rearrange("p (a t) -> p a t", a=2)]
    tab_res = [Res("tab0")]
    tab_ch = [P.chan("tab0")]
    rr.update({"pT": 0, "pF": 0, "stg": 0, "tab": 0})
    cbf = nc.alloc_sbuf_tensor("cbf", [128, 5 * 128], BF16)
    RA, R96, BONES, JREV, ZER = (cbf[:, i * 128:(i + 1) * 128] for i in range(5))
    EB = nc.alloc_sbuf_tensor("EB", [128, 6 * 384], BF16)
    esink = nc.alloc_sbuf_tensor("esink", [128, L * 6], F32)
    eb_res = Res("EB")

    def mix_consts():
        def fn(slot, res):
            cf = scr[:, 0:2 * CF_COLS].bitcast(F32)
            P.op("sp", lambda E: E.dma_start(out=cf, in_=cf_d), writes=[const_res], chan=const_ch)
            P.op("dve", lambda E: E.tensor_copy(out=cbf[:, 0:384], in_=cf[:, 0:384]), reads=[const_res], writes=[const_res])
            P.op("dve", lambda E: E.tensor_copy(out=cbf[:, 384:512], in_=cf[:, 1408:1536]), reads=[const_res], writes=[const_res])
            P.op("dve", lambda E: E.memset(cbf[:, 512:640], 0.0), writes=[const_res])
            for l in range(L):
                o, _ = SM[("sink", l)]
                P.op("act", lambda E, o=o, l=l: E.activation(out=esink[:, l * 6:(l + 1) * 6], in_=smalls[:, o:o + 6], func=AF.Exp),
                     reads=[const_res], writes=[const_res])
            bk, bres = get_bank()
            ro, _ = SM[("relb", 0)]
            P.op("pe", lambda E: E.matmul(bk[0:6, 0:512], lhsT=smalls[0:32, ro:ro + 6], rhs=cf[0:32, 384:896], start=True, stop=True),
                 reads=[const_res], writes=[bres])
            tb = tmpf[0][0:6, :]
            tbb = stg[0][0:6, :]
            P.op("act", lambda E: E.activation(out=tb, in_=bk[0:6, 0:512], func=AF.Exp), reads=[bres], writes=[eb_res, tmpf_res[0]])
            P.op("dve", lambda E: E.tensor_tensor(out=tbb, in0=tb, in1=cf[0:6, 896:1408], op=ALU.mult),
                 reads=[eb_res, const_res, tmpf_res[0]], writes=[eb_res, stg_res[0]])
            ebc = P.chan("ebc")
            P.op("sp", lambda E: E.dma_start(out=tb_d.ap(), in_=tbb), reads=[eb_res, stg_res[0]], writes=[eb_res], chan=ebc)
            for h in range(6):
                src = bass.AP(tensor=tb_d, offset=h * 512, ap=[[1, 128], [1, 384]])
                P.op("sp", lambda E, h=h, src=src: E.dma_start(out=stg[1][:, 0:384], in_=src),
                     reads=[eb_res], writes=[stg_res[1]], chan=ebc)
                b5, b5res = get_bank()
                P.op("pe", lambda E, b5=b5: E.matmul(b5[:, 0:384], lhsT=JREV, rhs=stg[1][:, 0:384], start=True, stop=True),
                     reads=[stg_res[1], const_res], writes=[b5res])
                P.op("act", lambda E, b5=b5, h=h: E.activation(out=EB[:, h * 384:(h + 1) * 384], in_=b5[:, 0:384], func=AF.Copy),
                     reads=[b5res], writes=[eb_res])
        add(None, fn)

    def load_tab(which, tt):
        ti = 0
        src = tabs_d[which][:, :, tt * TT:(tt + 1) * TT]
        P.op("sp", lambda E, ti=ti, src=src: E.dma_start(out=tabs[ti], in_=src), writes=[tab_res[ti]] + o_res["c"][1] + [act_res[14][0], act_res[14][1], act_res[15][0], act_res[15][1]], chan=tab_ch[ti])
        return tabs[ti], tab_res[ti]

    def finish_qk(ps, pres, nr, dst, dres, norm=None, rope=None, defer=False, spool=None):
        if spool is None:
            spool = [(stg[0], stg_res[0]), (stg[1], stg_res[1])]
        rr["stgx"] = (rr["stgx"] + 1) % len(spool)
        st, sres = spool[rr["stgx"]]
        tgt = st[0:nr, :] if rope is not None else dst
        tres = sres if rope is not None else dres
        if norm is not None:
            ones_l, nd, gain = norm
            qi = nxt("sq", 2)
            P.op("act", lambda E: E.activation(out=sq[qi][0:nr, :], in_=ps[0:nr, :], func=AF.Square), reads=[pres], writes=[sq_res[qi]])
            b2, b2res = get_bank()
            P.op("pe", lambda E: E.matmul(b2[0:nr, :], lhsT=ones_l, rhs=sq[qi][0:nr, :], start=True, stop=True),
                 reads=[sq_res[qi], const_res], writes=[b2res])
            ri = nxt("rstd", NRSTD)
            P.op("act", lambda E: E.activation(out=rstd[ri][0:nr, :], in_=b2[0:nr, :], func=AF.Ln, bias=epsb[0:nr, :], scale=1.0 / nd),
                 reads=[b2res, const_res], writes=[rstd_res[ri]])
            P.op("act", lambda E: E.activation(out=rstd[ri][0:nr, :], in_=rstd[ri][0:nr, :], func=AF.Exp, scale=-0.5),
                 reads=[rstd_res[ri]], writes=[rstd_res[ri]])
            P.op("dve", lambda E: E.scalar_tensor_tensor(out=tgt, in0=ps[0:nr, :], scalar=gain, in1=rstd[ri][0:nr, :],
                                                         op0=ALU.mult, op1=ALU.mult),
                 reads=[pres, rstd_res[ri], const_res], writes=[tres])
        else:
            P.op("act", lambda E: E.activation(out=tgt, in_=ps[0:nr, :], func=AF.Copy), reads=[pres], writes=[tres])
        if rope is None:
            return None

        def part2():
            Rl, tab, tabres = rope
            b3, b3res = get_bank()
            P.op("pe", lambda E: E.matmul(b3[0:nr, :], lhsT=Rl, rhs=st[0:nr, :], start=True, stop=True),
                 reads=[sres, const_res], writes=[b3res])
            t1, t1res = get_tmpf()
            t2, t2res = get_tmpf()
            P.op("dve", lambda E: E.tensor_tensor(out=t1[0:nr, :], in0=st[0:nr, :], in1=tab[0:nr, 0, :], op=ALU.mult),
                 reads=[sres, tabres], writes=[t1res])
            P.op("dve", lambda E: E.tensor_tensor(out=t2[0:nr, :], in0=b3[0:nr, :], in1=tab[0:nr, 1, :], op=ALU.mult),
                 reads=[b3res, tabres], writes=[t2res])
            P.op("dve", lambda E: E.tensor_tensor(out=dst, in0=t1[0:nr, :], in1=t2[0:nr, :], op=ALU.add),
                 reads=[t1res, t2res], writes=[dres])
        if defer:
            return part2
        part2()
        return None

    def proj(W, wres, ncolchunk, tt, M=128, acc=False):
        bk, bres = get_bank(acc)
        for k in range(KC):
            P.op("pe", lambda E, k=k: E.matmul(bk[0:M, :], lhsT=W[:, k, ncolchunk], rhs=hT[:, k, tt * TT:(tt + 1) * TT],
                                               start=(k == 0), stop=(k == KC - 1)),
                 reads=[wres, h_res[tt][k]], writes=[bres])
        return bk, bres

    def proj_v(W, wres, cols0, ncols, i):
        bk, bres = get_bank()
        for k in range(KC):
            P.op("pe", lambda E, k=k: E.matmul(bk[:, 0:ncols], lhsT=hT[:, k, i * 128:(i + 1) * 128], rhs=W[:, k, cols0:cols0 + ncols],
                                               start=(k == 0), stop=(k == KC - 1)),
                 reads=[wres, h_res[i // 4][k]], writes=[bres])
        return bk, bres

    LA = 3

    def attn_dense(qsel, ksel, K, groups, scale, omap):
        its = [(g, tt, kc) for g in range(len(groups)) for tt in range(NT) for kc in range(S // 128)]
        obs = {}
        pend = []
        la = 1 if max(len(g) for g in groups) > 1 else 2

        def emit_S(g, tt, kc):
            pis = []
            sbs = []
            for (h, vsel, half) in groups[g]:
                if kc == 0:
                    obs[(h, tt)] = get_bank(acc=True)
                qa, qres = qsel(h, tt)
                ka, kres = ksel(h, kc)
                sb, sbres = get_bank()
                P.op("pe", lambda E, sb=sb, ka=ka, qa=qa: E.matmul(sb[:], lhsT=ka, rhs=qa, start=True, stop=True), reads=[kres, qres], writes=[sbres])
                sbs.append((sb, sbres))
            for (sb, sbres) in sbs:
                pi = nxt("pT", NPT)
                P.op("act", lambda E, sb=sb, pi=pi: E.activation(out=pT[pi][:], in_=sb[:], func=AF.Exp, scale=scale), reads=[sbres], writes=[pT_res[pi]])
                pis.append(pi)
            return pis

        def emit_PV(g, tt, kc, pis):
            for (h, vsel, half), pi in zip(groups[g], pis):
                ob, obres = obs[(h, tt)]
                va = vsel(kc)
                P.op("pe", lambda E, ob=ob, va=va, pi=pi: E.matmul(ob[:], lhsT=va, rhs=pT[pi][:], start=(kc == 0), stop=(kc == S // 128 - 1)),
                     reads=[v_res, pT_res[pi]], writes=[obres])
            if kc == S // 128 - 1:
                for (h, vsel, half) in groups[g]:
                    ob, obres = obs[(h, tt)]
                    emit_onorm(ob, obres, half, omap(h, tt), None)

        for (g, tt, kc) in its:
            pend.append((g, tt, kc, emit_S(g, tt, kc)))
            if len(pend) > la:
                emit_PV(*pend.pop(0))
        while pend:
            emit_PV(*pend.pop(0))

    def emit_onorm(ob, obres, half, out, sink_ap, ncols=TT):
        oap, ores = out
        olo, dlo = (0, 64) if half == 0 else (64, 0)
        tf, tres = get_tmpf()
        if sink_ap is not None:
            sap = sink_ap(dlo)
            P.op("act", lambda E: E.activation(out=tf[dlo:dlo + 64, 0:ncols], in_=ob[dlo:dlo + 64, 0:ncols], func=AF.Ln, bias=sap, scale=1.0),
                 reads=[obres, const_res], writes=[tres])
            P.op("act", lambda E: E.activation(out=tf[dlo:dlo + 64, 0:ncols], in_=tf[dlo:dlo + 64, 0:ncols], func=AF.Exp, scale=-1.0),
                 reads=[tres], writes=[tres])
        else:
            P.op("dve", lambda E: E.reciprocal(out=tf[dlo:dlo + 64, 0:ncols], in_=ob[dlo:dlo + 64, 0:ncols]), reads=[obres], writes=[tres])
        P.op("dve", lambda E: E.tensor_tensor(out=oap, in0=ob[olo:olo + 64, 0:ncols], in1=tf[dlo:dlo + 64, 0:ncols], op=ALU.mult),
             reads=[obres, tres], writes=[ores])

    def vaug_ones():
        P.op("dve", lambda E: E.memset(va_v[:, :, 64:128], 1.0), writes=[v_res])

    def vsel_ab(half):
        return (lambda kc: va_v[:, kc, 0:128]) if half == 0 else (lambda kc: va_v[:, kc, 64:192])

    def mixer_ab_steps(l, s, which):
        base = 0 if which == "a" else 640
        ov = oa_v if which == "a" else ob_v

        def job1(slot, res, ch):
            W = slot[:, 0:KC * 512].rearrange("p (k j) -> p k j", k=KC)
            for j in range(3):
                for half in range(2):
                    h = j + 3 * half
                    src = winl[l][:, :, base + h * 64: base + (h + 1) * 64]
                    dst = W[:, :, j * 128 + half * 64: j * 128 + (half + 1) * 64]
                    P.op("pool", lambda E, dst=dst, src=src: E.dma_start(out=dst, in_=src), writes=[res], chan=ch)
            src = winl[l][:, :, base + 384: base + 512]
            P.op("pool", lambda E, src=src: E.dma_start(out=W[:, :, 384:512], in_=src), writes=[res], chan=ch)

        def comp1(slot, res):
            W = slot[:, 0:KC * 512].rearrange("p (k j) -> p k j", k=KC)
            if which == "a":
                go, _ = SM[("aqn", l)]
                ko, _ = SM[("akn", l)]
            items = [(tt, c) for tt in range(NT) for c in range(4)]
            pj = {}
            tabst = {}

            def do_proj(idx):
                tt, c = items[idx]
                pj[idx] = proj(W, res, slice(c * 128, (c + 1) * 128), tt)
                hold(pj[idx][0])

            do_proj(0)
            pend2 = []
            for idx, (tt, c) in enumerate(items):
                if idx + 1 < len(items):
                    do_proj(idx + 1)
                bk, bres = pj.pop(idx)
                if c < 3:
                    dst, dres = q_v[:, c, tt * TT:(tt + 1) * TT], q_res[c][tt]
                else:
                    dst, dres = k_v[:, tt * TT:(tt + 1) * TT], k_res[0][tt]
                if which == "a":
                    if c == 0:
                        while pend2:
                            pend2.pop(0)()
                        tabst["t"] = load_tab(0, tt)
                    tab, tabres = tabst["t"]
                    g = smalls[:, go:go + 1] if c < 3 else smalls[:, ko:ko + 1]
                    p2 = finish_qk(bk, bres, 128, dst, dres, norm=(BONES, HD, g), rope=(RA, tab, tabres), defer=True,
                                   spool=[(stg[0], stg_res[0]), (stg[1], stg_res[1])] + [(pT[i], pT_res[i]) for i in range(NPT)])
                    release(bk)
                    pend2.append(p2)
                    if len(pend2) > 2:
                        pend2.pop(0)()
                else:
                    finish_qk(bk, bres, 128, dst, dres)
                    release(bk)
            while pend2:
                pend2.pop(0)()

        def job2(slot, res, ch):
            W = slot[:, 0:KC * 128].rearrange("p (k j) -> p k j", k=KC)
            src = winl[l][:, :, base + 512: base + 640]
            P.op("pool", lambda E, src=src: E.dma_start(out=W, in_=src), writes=[res], chan=ch)

        def comp2(slot, res):
            W = slot[:, 0:KC * 128].rearrange("p (k j) -> p k j", k=KC)
            vaug_ones()
            for i in range(S // 128):
                bk, bres = proj_v(W, res, 0, 128, i)
                P.op("act", lambda E, bk=bk, i=i: E.activation(out=va_v[:, i, 0:64], in_=bk[:, 0:64], func=AF.Copy), reads=[bres], writes=[v_res])
                P.op("dve", lambda E, bk=bk, i=i: E.tensor_copy(out=va_v[:, i, 128:192], in_=bk[:, 64:128]), reads=[bres], writes=[v_res])

        def attn(slot, res):
            if which == "a":
                def qsel(h, tt):
                    j, half = h % 3, h // 3
                    return q_v[half * 64:(half + 1) * 64, j, tt * TT:(tt + 1) * TT], q_res[j][tt]

                def ksel(h, kc):
                    half = h // 3
                    return k_v[half * 64:(half + 1) * 64, kc * 128:(kc + 1) * 128], k_res[0][kc // 4]

                def omap(h, tt):
                    j, half = h % 3, h // 3
                    return ov[half * 64:(half + 1) * 64, j, tt * TT:(tt + 1) * TT], o_res["a"][j][tt]
                attn_dense(qsel, ksel, 64, [[(j, vsel_ab(0), 0), (j + 3, vsel_ab(1), 1)] for j in range(3)], A_SCALE, omap)
            else:
                attn_window(l)
        add(job1, comp1)
        add(job2, comp2)
        add(None, attn)

    def attn_window(l):
        its = []
        for h in range(6):
            for tt in range(NT):
                kcs = [kc for kc in range(4 * tt - 1, 4 * tt + 5) if 0 <= kc < S // 128]
                for n_i, kc in enumerate(kcs):
                    its.append((h, tt, kc, n_i == 0, n_i == len(kcs) - 1))
        obs = {}
        pend = []

        def emit_S(h, tt, kc, first, last):
            j, half = h % 3, h // 3
            if first:
                ob, obres = get_bank(acc=True)
                obs[(h, tt)] = (ob, obres)
                P.op("pe", lambda E: E.matmul(ob[:], lhsT=ZER, rhs=hT[:, 0, 0:TT], start=True, stop=False),
                     reads=[const_res, h_res[0][0]], writes=[obres])
            qb0 = max(kc - 1, 4 * tt)
            qb1 = min(kc + 1, 4 * tt + 3)
            ncol = (qb1 - qb0 + 1) * 128
            q0 = qb0 * 128
            e0 = (qb0 - (kc - 1)) * 128
            c0 = q0 - tt * TT
            sb, sbres = get_bank()
            P.op("pe", lambda E: E.matmul(
                sb[:, 0:ncol], lhsT=k_v[half * 64:(half + 1) * 64, kc * 128:(kc + 1) * 128],
                rhs=q_v[half * 64:(half + 1) * 64, j, q0:q0 + ncol], start=True, stop=True),
                reads=[k_res[0][kc // 4], q_res[j][tt]], writes=[sbres])
            fi = nxt("pF", NPF)
            P.op("act", lambda E: E.activation(out=pF[fi][:, 0:ncol], in_=sb[:, 0:ncol], func=AF.Exp, scale=A_SCALE),
                 reads=[sbres], writes=[pF_res[fi]])
            pi = nxt("pT", NPT)
            P.op("pool" if (kc % 2 == 0) else "dve", lambda E: E.tensor_tensor(
                out=pT[pi][:, 0:ncol], in0=pF[fi][:, 0:ncol], in1=EB[:, h * 384 + e0:h * 384 + e0 + ncol], op=ALU.mult),
                reads=[pF_res[fi], eb_res], writes=[pT_res[pi]])
            return (pi, c0, ncol)

        def emit_PV(h, tt, kc, first, last, info):
            pi, c0, ncol = info
            j, half = h % 3, h // 3
            ob, obres = obs[(h, tt)]
            va = vsel_ab(half)(kc)
            P.op("pe", lambda E: E.matmul(ob[:, c0:c0 + ncol], lhsT=va, rhs=pT[pi][:, 0:ncol], start=False, stop=last),
                 reads=[v_res, pT_res[pi]], writes=[obres])
            if last:
                sk = (lambda dlo: esink[dlo:dlo + 64, l * 6 + h:l * 6 + h + 1])
                emit_onorm(ob, obres, half, (ob_v[half * 64:(half + 1) * 64, j, tt * TT:(tt + 1) * TT], o_res["b"][j][tt]), sk)

        for it in its:
            pend.append(it + (emit_S(*it),))
            if len(pend) > LA:
                emit_PV(*pend.pop(0))
        while pend:
            emit_PV(*pend.pop(0))

    NW = 256 + 128 + 96
    C_WQ = KC * NW
    C_WKN = C_WQ + 384
    C_WV = C_WKN + 192
    assert C_WV + 128 <= SLOT_E

    def mixer_c_steps(l, s, p):
        def job(slot, res, ch):
            W = slot[:, 0:KC * NW].rearrange("p (k j) -> p k j", k=KC)
            P.op("pool", lambda E: E.dma_start(out=W[:, :, 384:448], in_=zer_d.rearrange("p (k j) -> p k j", k=KC)), writes=[res], chan=ch)
            src = winl[l][:, :, 1280:1664]
            P.op("pool", lambda E: E.dma_start(out=W[:, :, 0:384], in_=src), writes=[res], chan=ch)
            src2 = winl[l][:, :, 1664:1696]
            P.op("pool", lambda E: E.dma_start(out=W[:, :, 448:480], in_=src2), writes=[res], chan=ch)
            wq = cqup_d[l].rearrange("(k p) n -> p k n", p=128)[:, :, p * 192:(p + 1) * 192]
            P.op("pool", lambda E: E.dma_start(out=slot[:, C_WQ:C_WQ + 384].rearrange("p (k j) -> p k j", k=2), in_=wq), writes=[res], chan=ch)
            for hh in range(2):
                hd = 2 * p + hh
                P.op("pool", lambda E, hh=hh, hd=hd: E.dma_start(out=slot[:, C_WKN + hh * 96: C_WKN + hh * 96 + 64], in_=ckvup_d[l][:, hd * 128: hd * 128 + 64]),
                     writes=[res], chan=ch)
                P.op("pool", lambda E, hh=hh: E.dma_start(out=slot[:, C_WKN + hh * 96 + 64: C_WKN + (hh + 1) * 96], in_=zer_d[:, 0:32]),
                     writes=[res], chan=ch)
                P.op("pool", lambda E, hh=hh, hd=hd: E.dma_start(out=slot[:, C_WV + hh * 64: C_WV + (hh + 1) * 64], in_=ckvup_d[l][:, hd * 128 + 64: hd * 128 + 128]),
                     writes=[res], chan=ch)

        def c_tile(slot, res, tt):
            W1 = slot[:, 0:KC * NW].rearrange("p (k j) -> p k j", k=KC)
            Wq = slot[:, C_WQ:C_WQ + 384].rearrange("p (k j) -> p k j", k=2)
            qo, _ = SM[("cqn", l)]
            kvo, _ = SM[("ckvn", l)]
            tab, tabres = load_tab(1, tt)
            cpool = [(pT[i], pT_res[i]) for i in range(1, NPT)]
            ql = [proj(W1, res, slice(c * 128, (c + 1) * 128), tt) for c in range(2)]
            kvl = proj(W1, res, slice(256, 384), tt, acc=True)
            bkks = []
            for hh in range(2):
                bkk, bkres = get_bank(acc=True)
                for k in range(KC):
                    P.op("pe", lambda E, k=k, bkk=bkk: E.matmul(bkk[0:96, :], lhsT=W1[:, k, 384:480], rhs=hT[:, k, tt * TT:(tt + 1) * TT], start=(k == 0), stop=False),
                         reads=[res, h_res[tt][k]], writes=[bkres])
                bkks.append((bkk, bkres))
            b2, b2res = get_bank()
            for c in range(2):
                qi = nxt("sq", 2)
                P.op("act", lambda E, qi=qi, c=c: E.activation(out=sq[qi][:], in_=ql[c][0][:], func=AF.Square), reads=[ql[c][1]], writes=[sq_res[qi]])
                P.op("pe", lambda E, qi=qi, c=c: E.matmul(b2[:], lhsT=ones_bf[:], rhs=sq[qi][:], start=(c == 0), stop=(c == 1)),
                     reads=[sq_res[qi], const_res], writes=[b2res])
            ri = nxt("rstd", NRSTD)
            P.op("act", lambda E: E.activation(out=rstd[ri][:], in_=b2[:], func=AF.Ln, bias=epsb[:], scale=1.0 / 256), reads=[b2res, const_res], writes=[rstd_res[ri]])
            P.op("act", lambda E: E.activation(out=rstd[ri][:], in_=rstd[ri][:], func=AF.Exp, scale=-0.5), reads=[rstd_res[ri]], writes=[rstd_res[ri]])
            qln = []
            for c in range(2):
                si = nxt("stg", 2)
                P.op("dve", lambda E, si=si, c=c: E.scalar_tensor_tensor(out=stg[si][:], in0=ql[c][0][:], scalar=smalls[:, qo + c:qo + c + 1], in1=rstd[ri][:],
                                                                     op0=ALU.mult, op1=ALU.mult),
                     reads=[ql[c][1], rstd_res[ri], const_res], writes=[stg_res[si]])
                qln.append((stg[si], stg_res[si]))
            bqs = []
            for hh in range(2):
                bq, bqres = get_bank()
                for c in range(2):
                    P.op("pe", lambda E, c=c, hh=hh, bq=bq: E.matmul(bq[0:96, :], lhsT=Wq[:, c, hh * 96:(hh + 1) * 96], rhs=qln[c][0][:], start=(c == 0), stop=(c == 1)),
                         reads=[res, qln[c][1]], writes=[bqres])
                bqs.append((bq, bqres))
            p2q = [finish_qk(bqs[hh][0], bqs[hh][1], 96, qc_v[0:96, hh, tt * TT:(tt + 1) * TT], q_res[hh][tt],
                             rope=(R96[0:96, 0:96], tab, tabres), defer=True, spool=cpool) for hh in range(2)]
            qi2 = nxt("sq", 2)
            P.op("act", lambda E: E.activation(out=sq[qi2][:], in_=kvl[0][:], func=AF.Square), reads=[kvl[1]], writes=[sq_res[qi2]])
            b4, b4res = get_bank()
            P.op("pe", lambda E: E.matmul(b4[:], lhsT=ones_bf[:], rhs=sq[qi2][:], start=True, stop=True), reads=[sq_res[qi2], const_res], writes=[b4res])
            ri2 = nxt("rstd", NRSTD)
            P.op("act", lambda E: E.activation(out=rstd[ri2][:], in_=b4[:], func=AF.Ln, bias=epsb[:], scale=1.0 / 128), reads=[b4res, const_res], writes=[rstd_res[ri2]])
            P.op("act", lambda E: E.activation(out=rstd[ri2][:], in_=rstd[ri2][:], func=AF.Exp, scale=-0.5), reads=[rstd_res[ri2]], writes=[rstd_res[ri2]])
            kvn, kvnres = pT[0], pT_res[0]
            P.op("dve", lambda E: E.scalar_tensor_tensor(out=kvn[:], in0=kvl[0][:], scalar=smalls[:, kvo:kvo + 1], in1=rstd[ri2][:], op0=ALU.mult, op1=ALU.mult),
                 reads=[kvl[1], rstd_res[ri2], const_res], writes=[kvnres])
            for p2 in p2q:
                p2()
            p2k = []
            for hh in range(2):
                bkk, bkres = bkks[hh]
                P.op("pe", lambda E, hh=hh, bkk=bkk: E.matmul(bkk[0:96, :], lhsT=slot[:, C_WKN + hh * 96: C_WKN + (hh + 1) * 96], rhs=kvn[:], start=False, stop=True),
                     reads=[res, kvnres], writes=[bkres])
                p2k.append(finish_qk(bkk, bkres, 96, kc_v[0:96, hh, tt * TT:(tt + 1) * TT], k_res[hh][tt], rope=(R96[0:96, 0:96], tab, tabres), defer=True, spool=cpool))
            for i4 in range(4):
                i = tt * 4 + i4
                bv, bvres = get_bank()
                P.op("pe", lambda E, i4=i4, bv=bv: E.matmul(bv[:, 0:128], lhsT=kvn[:, i4 * 128:(i4 + 1) * 128], rhs=slot[:, C_WV:C_WV + 128], start=True, stop=True),
                     reads=[res, kvnres], writes=[bvres])
                P.op("act", lambda E, bv=bv, i=i: E.activation(out=va_v[:, i, 0:64], in_=bv[:, 0:64], func=AF.Copy), reads=[bvres], writes=[v_res])
                P.op("dve", lambda E, bv=bv, i=i: E.tensor_copy(out=va_v[:, i, 128:192], in_=bv[:, 64:128]), reads=[bvres], writes=[v_res])
            for p2 in p2k:
                p2()

        def comp(slot, res):
            vaug_ones()
            P.op("dve", lambda E: E.memset(scr[64:128, O_Q:O_Q + 4 * S], 0.0),
                 writes=[q_res[hh][t] for hh in range(2) for t in range(NT)] + [k_res[hh][t] for hh in range(2) for t in range(NT)])
            for tt in range(NT):
                c_tile(slot, res, tt)

        def attn(slot, res):
            def qsel(h, tt):
                return qc_v[:, h, tt * TT:(tt + 1) * TT], q_res[h][tt]

            def ksel(h, kc):
                return kc_v[:, h, kc * 128:(kc + 1) * 128], k_res[h][kc // 4]

            def omap(h, tt):
                return oc_v[h * 64:(h + 1) * 64, p, tt * TT:(tt + 1) * TT], o_res["c"][p][tt]
            attn_dense(qsel, ksel, 96, [[(hh, vsel_ab(hh), hh)] for hh in range(2)], C_SCALE, omap)
        add(job, comp)
        add(None, attn)

    def merge_steps(l, s):
        for tt in range(NT):
            for dc in range(KC):
                def job(slot, res, ch, dc=dc):
                    G = slot[:, 0:3 * KC * 128].rearrange("p (m k j) -> p m k j", m=3, k=KC)
                    for m in range(3):
                        src = winl[l][:, :, 1696 + m * 1024 + dc * 128: 1696 + m * 1024 + (dc + 1) * 128]
                        P.op("pool", lambda E, m=m, src=src: E.dma_start(out=G[:, m, :, :], in_=src), writes=[res], chan=ch)
                    BR = slot[:, 3072:4096].rearrange("p (c j) -> p c j", c=8)
                    for mi, wd in enumerate((wbra_d, wbrb_d)):
                        for half in range(2):
                            src = wd[l].rearrange("(half j p) n -> half p j n", half=2, j=3)[half, :, :, dc * 128:(dc + 1) * 128]
                            P.op("pool", lambda E, mi=mi, half=half, src=src: E.dma_start(out=BR[half * 64:(half + 1) * 64, mi * 3:(mi + 1) * 3, :], in_=src),
                                 writes=[res], chan=ch)
                    src = wbrc_d[l].rearrange("(j p) n -> p j n", p=128)[:, :, dc * 128:(dc + 1) * 128]
                    P.op("pool", lambda E, src=src: E.dma_start(out=BR[:, 6:8, :], in_=src), writes=[res], chan=ch)

                def comp(slot, res, dc=dc, tt=tt):
                    G = slot[:, 0:3 * KC * 128].rearrange("p (m k j) -> p m k j", m=3, k=KC)
                    BR = slot[:, 3072:4096].rearrange("p (c j) -> p c j", c=8)
                    sgs = []
                    for m in range(3):
                        bk, bres = get_bank()
                        for k in range(KC):
                            P.op("pe", lambda E, bk=bk, m=m, k=k: E.matmul(bk[:], lhsT=G[:, m, k, :], rhs=hT[:, k, tt * TT:(tt + 1) * TT], start=(k == 0), stop=(k == KC - 1)),
                                 reads=[res, h_res[tt][k]], writes=[bres])
                        tf, tres = get_tmpf()
                        P.op("act", lambda E, bk=bk, tf=tf: E.activation(out=tf[:], in_=bk[:], func=AF.Sigmoid), reads=[bres], writes=[tres])
                        sgs.append((tf, tres))
                    acc = None
                    for m, (ov, nch, key) in enumerate(((oa_v, 3, "a"), (ob_v, 3, "b"), (oc_v, 2, "c"))):
                        bk, bres = get_bank()
                        for c in range(nch):
                            P.op("pe", lambda E, bk=bk, m=m, c=c, ov=ov, nch=nch: E.matmul(bk[:], lhsT=BR[:, m * 3 + c, :], rhs=ov[:, c, tt * TT:(tt + 1) * TT],
                                                                                       start=(c == 0), stop=(c == nch - 1)),
                                 reads=[res, o_res[key][c][tt]], writes=[bres])
                        tf, tres = sgs[m]
                        P.op("dve", lambda E, bk=bk, tf=tf: E.tensor_tensor(out=tf[:], in0=tf[:], in1=bk[:], op=ALU.mult), reads=[tres, bres], writes=[tres])
                    t0, t0res = sgs[0]
                    P.op("dve", lambda E: E.tensor_tensor(out=t0[:], in0=t0[:], in1=sgs[1][0][:], op=ALU.add), reads=[t0res, sgs[1][1]], writes=[t0res])
                    P.op("dve", lambda E: E.tensor_tensor(out=mg_v[:, dc, :], in0=t0[:], in1=sgs[2][0][:], op=ALU.add), reads=[t0res, sgs[2][1]], writes=[mg_res[dc]])
                add(job, comp)
            for oc2 in range(2):
                def jobo(slot, res, ch, oc2=oc2):
                    W = slot[:, 0:KC * 512].rearrange("p (k j) -> p k j", k=KC)
                    src = wout_d[l].rearrange("(k p) n -> p k n", p=128)[:, :, oc2 * 512:(oc2 + 1) * 512]
                    P.op("pool", lambda E, src=src: E.dma_start(out=W, in_=src), writes=[res], chan=ch)

                def compo(slot, res, oc2=oc2, tt=tt):
                    W = slot[:, 0:KC * 512].rearrange("p (k j) -> p k j", k=KC)
                    for c4 in range(4):
                        dcc = oc2 * 4 + c4
                        bk, bres = get_bank()
                        for k in range(KC):
                            P.op("pe", lambda E, bk=bk, k=k, c4=c4: E.matmul(bk[:], lhsT=W[:, k, c4 * 128:(c4 + 1) * 128], rhs=mg_v[:, k, :], start=(k == 0), stop=(k == KC - 1)),
                                 reads=[res, mg_res[k]], writes=[bres])
                        xs = xT[:, dcc, tt * TT:(tt + 1) * TT]
                        P.op("dve", lambda E, bk=bk, xs=xs, dcc=dcc: E.scalar_tensor_tensor(out=xs, in0=bk[:], scalar=mG(l, 1, s, dcc), in1=xs, op0=ALU.mult, op1=ALU.add),
                             reads=[bres, x_res[tt][dcc], mods_res], writes=[x_res[tt][dcc]])
                add(jobo, compo)
            if tt >= 1:
                add_next_norm(l, 1, s, [tt - 1])
        add_next_norm(l, 1, s, [NT - 1])

    def mix_steps(l, s):

        def zero_unused(slot, res):
            for key, ov, nch in (("a", oa_v, 3), ("b", ob_v, 3), ("c", oc_v, 2)):
                if key not in mixers:
                    for c in range(nch):
                        P.op("dve", lambda E, ov=ov, c=c: E.memset(ov[:, c, :], 0.0), writes=o_res[key][c])

        if "a" in mixers:
            mixer_ab_steps(l, s, "a")
        if "b" in mixers:
            mixer_ab_steps(l, s, "b")
        if "c" in mixers:
            mixer_c_steps(l, s, 0)
            mixer_c_steps(l, s, 1)
        if mixers != "abc":
            add(None, zero_unused)
        merge_steps(l, s)

    if do_mix:
        mix_consts()
    ada_pending = []
    if do_ada:
        for l in range(depth):
            for m3 in range(3):
                ada_steps(l, m3, steps if (l == 0 and m3 == 0) else ada_pending)
    n_pre = len(steps)
    for s in range(n_seq):
        load_seq_steps(s)
        for l in range(depth):
            if do_ffn:
                ffn_steps(l, 0, s)
            if do_mix:
                mix_steps(l, s)
            if do_ffn:
                ffn_steps(l, 1, s)
        final_steps(s)

    if ada_pending:
        merged = steps[:n_pre]
        npop = {}
        for st in steps[n_pre:]:
            merged.append(st)
            tag = step_tag.get(id(st[1]))
            if tag is not None and ada_pending and npop.get(tag, 0) < 36:
                npop[tag] = npop.get(tag, 0) + 1
                merged.append(ada_pending.pop(0))
        assert not ada_pending
        steps = merged

    wjobs = [(i, st[0]) for i, st in enumerate(steps) if st[0] is not None]
    jidx = {i: n for n, (i, _) in enumerate(wjobs)}
    loaded = 0

    def ensure(nj):
        nonlocal loaded
        while loaded < min(nj, len(wjobs)):
            sl = loaded % NSLOT
            wjobs[loaded][1](slots[sl], slot_res[sl], slot_ch[sl])
            loaded += 1

    for i, (wj, fn) in enumerate(steps):
        if wj is not None:
            n = jidx[i]
            ensure(n + NSLOT)
            fn(slots[n % NSLOT], slot_res[n % NSLOT])
        else:
            fn(None, None)
    stats = P.emit(final_chans=xout_ch)
    return nc, stats


_CACHE = {}


_CONSTS = {}


def make_consts():
    if _CONSTS:
        return _CONSTS
    import math
    import jax
    import jax.numpy as jnp
    theta = 10000.0
    with jax.default_device(jax.devices("cpu")[0]):
        def angles(pos, dim):
            inv = theta ** (-jnp.arange(0, dim, 2, dtype=jnp.float32) / dim)
            ang = pos.astype(jnp.float32)[:, None] * inv[None, :]
            return np.asarray(jnp.cos(ang)), np.asarray(jnp.sin(ang))
        t = jnp.arange(S)
        row_c, row_s = angles(t // 64, 32)
        col_c, col_s = angles(t % 64, 32)
        seq_c, seq_s = angles(t, 32)
        idx = np.arange(512)
        rel = jnp.asarray(128 - (idx - 127))
        nb, max_exact = 16, 8
        ret = jnp.where(rel > 0, nb, 0)
        n = jnp.abs(rel)
        large = max_exact + (jnp.log(jnp.maximum(n, 1).astype(jnp.float32) / max_exact)
                             / math.log(128 / max_exact) * (nb - max_exact)).astype(jnp.int32)
        large = jnp.minimum(large, nb - 1)
        bucket = np.asarray(ret + jnp.where(n < max_exact, n, large))
    ropeA = np.zeros((128, 2, S), np.float32)
    for p in range(128):
        d = p % 64
        cs, sn = (row_c, row_s) if d < 32 else (col_c, col_s)
        dd = d % 32
        i = dd % 16
        ropeA[p, 0] = cs[:, i]
        ropeA[p, 1] = -sn[:, i] if dd < 16 else sn[:, i]
    ropeC = np.zeros((128, 2, S), np.float32)
    ropeC[0:64, 0] = 1.0
    for p in range(64, 96):
        dd = p - 64
        i = dd % 16
        ropeC[p, 0] = seq_c[:, i]
        ropeC[p, 1] = -seq_s[:, i] if dd < 16 else seq_s[:, i]
    cf = np.zeros((128, CF_COLS), np.float32)
    for m in range(128):
        dd = m % 32
        k = m + 16 if dd < 16 else m - 16
        cf[k, m] = 1.0
    for m in range(64, 96):
        dd = m - 64
        k = m + 16 if dd < 16 else m - 16
        cf[k, 128 + m] = 1.0
    cf[0:64, 256:320] = 1.0
    cf[64:128, 320:384] = 1.0
    u = idx - 127
    valid = (u >= 0) & (u <= 256) & (idx < 511)
    for i in range(511):
        cf[bucket[i], 384 + i] = 1.0
    cf[0:6, 896:1408] = valid.astype(np.float32)[None, :]
    for m in range(128):
        cf[127 - m, 1408 + m] = 1.0
    _CONSTS.update(cf=cf, ropeA=ropeA, ropeC=ropeC, zer=np.zeros((128, KC * 64), np.float32))
    return _CONSTS


def make_in_maps(inp, n_seq, ncores):
    x = np.ascontiguousarray(inp["x"], np.float32)
    sm = pack_smalls(inp)
    shared = {"smalls": sm}
    shared.update(make_consts())
    for k in ("ada_w", "ffn1_w_gu", "ffn2_w_gu", "ffn1_w_down", "ffn2_w_down", "w_in", "c_w_q_up", "c_w_kv_up",
              "w_br_a", "w_br_b", "w_br_c", "w_out"):
        shared[k] = np.ascontiguousarray(inp[k], np.float32)
    c = np.asarray(inp["c"], np.float32)
    in_maps = []
    for i in range(ncores):
        cs = c[i * n_seq:(i + 1) * n_seq]
        ct = cs.reshape(n_seq, KC, 128).transpose(2, 1, 0)
        m = dict(shared)
        m["x"] = x[i * n_seq:(i + 1) * n_seq].reshape(n_seq * S, D)
        m["cT"] = np.ascontiguousarray(ct.reshape(128, KC * n_seq))
        in_maps.append(m)
    return in_maps


def kernel(**inp):
    B = inp["x"].shape[0]
    n_seq = B // NCORES
    if "nc" not in _CACHE:
        _CACHE["nc"] = build(n_seq=n_seq)[0]
    nc = _CACHE["nc"]
    in_maps = make_in_maps(inp, n_seq, NCORES)
    res = run_bass_kernel_spmd(nc, in_maps, core_ids=list(range(NCORES)))
    out = np.concatenate([r["out"].reshape(n_seq, S, D) for r in res.results], axis=0)
    return out.astype(np.float32)
```

```python
import numpy as np
import concourse.bass as bass
import concourse.mybir as mybir
from concourse.bass_utils import run_bass_kernel_spmd

F32 = mybir.dt.float32
BF16 = mybir.dt.bfloat16
AF = mybir.ActivationFunctionType
ALU = mybir.AluOpType

D = 1024
S = 2048
KC = 8
FF = 2816
FC = 22
TT = 512
NT = 4
L = 2
EPS = 1e-6
IN_COLS = 4768
NCORES = 8


class Res:
    __slots__ = ("name", "lw", "rd")

    def __init__(self, name):
        self.name = name
        self.lw = None
        self.rd = {}


class Chan:
    def __init__(self, name):
        self.name = name
        self.n = 0
        self.sem = None


class Prog:
    ENG = ("pe", "act", "dve", "pool", "sp")

    def __init__(self, nc, serialize=False):
        self.nc = nc
        self.ops = []
        self.serialize = serialize
        self.eng = {"pe": nc.tensor, "act": nc.scalar, "dve": nc.vector,
                    "pool": nc.gpsimd, "sp": nc.sync}
        self.chans = []

    def chan(self, name):
        c = Chan(name)
        self.chans.append(c)
        return c

    def op(self, eng, fn, reads=(), writes=(), chan=None):
        idx = len(self.ops)
        deps = set()
        rkey = ("c", id(chan)) if chan is not None else eng
        for r in reads:
            if r.lw is not None:
                deps.add((r.lw, True))
        for w in writes:
            if w.lw is not None:
                pw = self.ops[w.lw]
                if not (chan is not None and pw["chan"] is chan and pw["eng"] == eng):
                    deps.add((w.lw, False))
            for k, i in w.rd.items():
                deps.add((i, False))
        if self.serialize and idx > 0:
            deps.add((idx - 1, True))
        for r in reads:
            r.rd[rkey] = idx
        for w in writes:
            w.lw = idx
            w.rd = {}
        ordn = None
        if chan is not None:
            chan.n += 1
            ordn = chan.n
        self.ops.append(dict(eng=eng, fn=fn, deps=deps, chan=chan, ordn=ordn, sig=False))
        return idx

    def emit(self, final_chans=()):
        nc = self.nc
        ops = self.ops
        real = []
        for i, o in enumerate(ops):
            rl = []
            for (j, raw) in o["deps"]:
                p = ops[j]
                if p["chan"] is None:
                    if p["eng"] == o["eng"] and o["chan"] is None:
                        if o["eng"] == "pe":
                            continue
                    p["sig"] = True
                rl.append(j)
            real.append(rl)
        sem = {e: nc.alloc_semaphore("s_" + e) for e in ("pe", "act", "dve", "pool")}
        for c in self.chans:
            c.sem = nc.alloc_semaphore("c_" + c.name)
        cnt = {e: 0 for e in sem}
        sigval = {}
        seen = {e: {} for e in self.ENG}
        nwait = 0
        for i, o in enumerate(ops):
            e = o["eng"]
            E = self.eng[e]
            need = {}
            for j in real[i]:
                p = ops[j]
                if p["chan"] is not None:
                    key = ("c", id(p["chan"]))
                    s, v = p["chan"].sem, 16 * p["ordn"]
                else:
                    key = p["eng"]
                    s, v = sem[p["eng"]], sigval[j]
                if seen[e].get(key, 0) >= v:
                    continue
                if key not in need or need[key][1] < v:
                    need[key] = (s, v)
            for key, (s, v) in need.items():
                E.wait_ge(s, v)
                seen[e][key] = v
                nwait += 1
            ins = o["fn"](E)
            if o["chan"] is not None:
                ins.then_inc(o["chan"].sem, 16)
            elif o["sig"]:
                cnt[e] += 1
                sigval[i] = cnt[e]
                ins.then_inc(sem[e], 1)
        for c in final_chans:
            nc.sync.wait_ge(c.sem, 16 * c.n)
        self.stats = dict(n_ops=len(ops), n_wait=nwait, cnt=dict(cnt))
        return self.stats


def fm(v):
    v = np.asarray(v, np.float32)
    return np.ascontiguousarray(v.reshape(-1, 128).T)


SM = {}
_o = 0
for _l in range(L):
    for _n, _w in (("nf1", 8), ("nmix", 8), ("nf2", 8), ("adab", 72), ("aqn", 1), ("akn", 1), ("cqn", 2), ("ckvn", 1), ("sink", 6)):
        SM[(_n, _l)] = (_o, _w)
        _o += _w
SM[("fin", 0)] = (_o, 8)
_o += 8
SM[("relb", 0)] = (_o, 6)
_o += 6
CF_COLS = 1536
SM_COLS = _o


def pack_smalls(inp):
    sm = np.zeros((128, SM_COLS), np.float32)

    def put(key, arr):
        o, w = SM[key]
        assert arr.shape == (128, w), (key, arr.shape)
        sm[:, o:o + w] = arr

    for l in range(L):
        put(("nf1", l), fm(inp["norm_ffn1"][l]))
        put(("nmix", l), fm(inp["norm_mix"][l]))
        put(("nf2", l), fm(inp["norm_ffn2"][l]))
        put(("adab", l), fm(inp["ada_b"][l]))
        put(("aqn", l), np.tile(np.asarray(inp["a_q_norm"][l], np.float32), 2).reshape(128, 1))
        put(("akn", l), np.tile(np.asarray(inp["a_k_norm"][l], np.float32), 2).reshape(128, 1))
        put(("cqn", l), fm(inp["c_q_lat_norm"][l]))
        put(("ckvn", l), fm(inp["c_kv_lat_norm"][l]))
        put(("sink", l), np.broadcast_to(np.asarray(inp["b_sink"][l], np.float32)[None, :], (128, 6)))
    put(("fin", 0), fm(inp["final_norm"]))
    o, w = SM[("relb", 0)]
    sm[0:32, o:o + w] = np.asarray(inp["rel_bias"], np.float32)
    return sm


NSLOT = 3
SLOT_E = 4608


def build(n_seq=4, depth=L, do_mix=True, serialize=False, do_ffn=True, do_ada=True, ffn_parts="ngd", mixers="abc"):
    nc = bass.Bass("TRN2", target_bir_lowering=False)
    NTOK = n_seq * S
    x_d = nc.dram_tensor("x", [NTOK, D], F32, kind="ExternalInput").ap()
    ct_d = nc.dram_tensor("cT", [128, KC * n_seq], F32, kind="ExternalInput").ap()
    sm_d = nc.dram_tensor("smalls", [128, SM_COLS], F32, kind="ExternalInput").ap()
    adaw_d = nc.dram_tensor("ada_w", [L, D, 9 * D], F32, kind="ExternalInput").ap()
    wgu_d = [nc.dram_tensor("ffn%d_w_gu" % i, [L, D, 2 * FF], F32, kind="ExternalInput").ap() for i in (1, 2)]
    wdn_d = [nc.dram_tensor("ffn%d_w_down" % i, [L, FF, D], F32, kind="ExternalInput").ap() for i in (1, 2)]
    out_d = nc.dram_tensor("out", [NTOK, D], F32, kind="ExternalOutput").ap()
    win_d = nc.dram_tensor("w_in", [L, D, IN_COLS], F32, kind="ExternalInput").ap()
    cqup_d = nc.dram_tensor("c_w_q_up", [L, 256, 384], F32, kind="ExternalInput").ap()
    ckvup_d = nc.dram_tensor("c_w_kv_up", [L, 128, 512], F32, kind="ExternalInput").ap()
    wbra_d = nc.dram_tensor("w_br_a", [L, 384, D], F32, kind="ExternalInput").ap()
    wbrb_d = nc.dram_tensor("w_br_b", [L, 384, D], F32, kind="ExternalInput").ap()
    wbrc_d = nc.dram_tensor("w_br_c", [L, 256, D], F32, kind="ExternalInput").ap()
    wout_d = nc.dram_tensor("w_out", [L, D, D], F32, kind="ExternalInput").ap()
    cf_d = nc.dram_tensor("cf", [128, CF_COLS], F32, kind="ExternalInput").ap()
    zer_d = nc.dram_tensor("zer", [128, KC * 64], F32, kind="ExternalInput").ap()
    tabs_d = [nc.dram_tensor(n, [128, 2, S], F32, kind="ExternalInput").ap() for n in ("ropeA", "ropeC")]
    tb_d = nc.dram_tensor("tb_scratch", [6, 512], BF16)

    P = Prog(nc, serialize=serialize)

    xT = nc.alloc_sbuf_tensor("xT", [128, KC, S], F32)
    hT = nc.alloc_sbuf_tensor("hT", [128, KC, S], BF16)
    SCR_E = 12 * S + 16 * 192
    assert SCR_E >= FC * 2 * TT + 2 * 2 * D
    scr = nc.alloc_sbuf_tensor("scr", [128, SCR_E], BF16)
    slots = [nc.alloc_sbuf_tensor("wslot%d" % i, [128, SLOT_E], BF16) for i in range(NSLOT)]
    slot_res = [Res("wslot%d" % i) for i in range(NSLOT)]
    slot_ch = [P.chan("w%d" % i) for i in range(NSLOT)]
    xin = [scr[:, FC * 2 * TT + i * 2 * D: FC * 2 * TT + (i + 1) * 2 * D].bitcast(F32) for i in range(2)]
    xin_res = [Res("xin%d" % i) for i in range(2)]
    xin_ch = [P.chan("xi%d" % i) for i in range(2)]
    xout_ch = [P.chan("xo%d" % i) for i in range(2)]
    smalls = nc.alloc_sbuf_tensor("smalls_sb", [128, SM_COLS], F32)
    cT = nc.alloc_sbuf_tensor("cT_sb", [128, KC * n_seq], F32)
    condT = nc.alloc_sbuf_tensor("condT", [128, KC * n_seq], F32)
    MODW = 72 * n_seq
    mods = nc.alloc_sbuf_tensor("mods", [128, L * MODW], F32)
    modA = nc.alloc_sbuf_tensor("modA", [128, L * 3 * n_seq * KC], F32)
    modG = nc.alloc_sbuf_tensor("modG", [128, L * 3 * n_seq * KC], F32)
    ident = nc.alloc_sbuf_tensor("ident", [128, 128], F32)
    ones_bf = nc.alloc_sbuf_tensor("ones_bf", [128, 128], BF16)
    epsb = nc.alloc_sbuf_tensor("epsb", [128, 1], F32)
    sq = [nc.alloc_sbuf_tensor("sq%d" % i, [128, TT], BF16) for i in range(2)]
    sq_res = [Res("sq%d" % i) for i in range(2)]
    NRSTD = 1
    rstd = [nc.alloc_sbuf_tensor("rstd%d" % i, [128, TT], F32) for i in range(NRSTD)]
    rstd_res = [Res("rstd%d" % i) for i in range(NRSTD)]
    tmpf = [nc.alloc_sbuf_tensor("tmpf%d" % i, [128, TT], F32) for i in range(3)]
    tmpf_res = [Res("tmpf%d" % i) for i in range(3)]
    banks = [nc.alloc_psum_tensor("bank%d" % i, [128, TT], F32) for i in range(8)]
    bank_res = [Res("bank%d" % i) for i in range(8)]
    const_res = Res("consts")
    const_ch = P.chan("const")
    mods_res = Res("mods")
    x_res = [[Res("x%d_%d" % (i, c)) for c in range(KC)] for i in range(NT)]
    h_res = [[Res("h%d_%d" % (i, c)) for c in range(KC)] for i in range(NT)]

    rr = {"bank": 0, "abank": 0, "tmpf": 0, "sq": 0, "sqx": 0, "stgx": 0, "rstd": 0, "xin": 0}

    def nxt(kind, n):
        i = rr[kind]
        rr[kind] = (i + 1) % n
        return i

    held = set()

    def get_bank(acc=False):
        if acc:
            i = nxt("abank", 4)
        else:
            for _ in range(4):
                i = 4 + nxt("bank", 4)
                if i not in held:
                    break
            else:
                raise RuntimeError("no free PSUM bank")
        return banks[i], bank_res[i]

    def hold(bk):
        held.add(banks.index(bk))

    def release(bk):
        held.discard(banks.index(bk))

    def get_tmpf():
        i = nxt("tmpf", 3)
        return tmpf[i], tmpf_res[i]

    P.op("sp", lambda E: E.dma_start(out=smalls[:], in_=sm_d), writes=[const_res], chan=const_ch)
    P.op("sp", lambda E: E.dma_start(out=cT[:], in_=ct_d), writes=[const_res], chan=const_ch)
    P.op("dve", lambda E: E.memset(ident[:], 0.0), writes=[const_res])
    P.op("pool", lambda E: E.affine_select(out=ident[:], in_=ident[:], compare_op=ALU.not_equal, fill=1.0,
                                           base=0, pattern=[[-1, 128]], channel_multiplier=1),
         reads=[const_res], writes=[const_res])
    P.op("dve", lambda E: E.memset(ones_bf[:], 1.0), writes=[const_res])
    P.op("dve", lambda E: E.memset(epsb[:], EPS), writes=[const_res])
    P.op("act", lambda E: E.activation(out=condT[:], in_=cT[:], func=AF.Silu), reads=[const_res], writes=[const_res])

    def smcol(key, c=None):
        o, w = SM[key]
        if c is None:
            return smalls[:, o:o + w]
        return smalls[:, o + c:o + c + 1]

    steps = []

    step_tag = {}

    def add(wjob, fn, tag=None):
        step_tag[id(fn)] = tag
        steps.append((wjob, fn))

    ADA_CW = 256

    def ada_steps(l, m3, out_list):
        pbank = {}
        T0 = m3 * 12

        def wjob(t):
            def load(slot, res, ch):
                dst = slot[:, 0:2 * KC * ADA_CW].bitcast(F32).rearrange("p (k j) -> p k j", k=KC)
                src = adaw_d[l].rearrange("(k p) n -> p k n", p=128)[:, :, t * ADA_CW:(t + 1) * ADA_CW]
                P.op("pool", lambda E: E.dma_start(out=dst, in_=src), writes=[res], chan=ch)
            return load

        def comp(t):
            def fn(slot, res):
                if t == T0:
                    pbank["b"] = get_bank(acc=True)
                bk, bres = pbank["b"]
                W = slot[:, 0:2 * KC * ADA_CW].bitcast(F32).rearrange("p (k j) -> p k j", k=KC)
                for jj in range(ADA_CW // 128):
                    j = t * (ADA_CW // 128) + jj
                    for k in range(KC):
                        P.op("pe", lambda E, j=j, jj=jj, k=k: E.matmul(
                            bk[:, j * n_seq:(j + 1) * n_seq], lhsT=W[:, k, jj * 128:(jj + 1) * 128],
                            rhs=condT[:, k * n_seq:(k + 1) * n_seq], start=(k == 0), stop=(k == KC - 1)),
                            reads=[res, const_res], writes=[bres])
                if t == T0 + 11:
                    o, w = SM[("adab", l)]
                    j0, j1 = m3 * 24, (m3 + 1) * 24
                    src_b = smalls[:, o + j0:o + j1].unsqueeze(2).to_broadcast([128, 24, n_seq])
                    dstm = mods[:, l * MODW:(l + 1) * MODW].rearrange("p (j s) -> p j s", s=n_seq)[:, j0:j1, :]
                    srcp = bk[:, 0:MODW].rearrange("p (j s) -> p j s", s=n_seq)[:, j0:j1, :]
                    P.op("dve", lambda E: E.tensor_tensor(out=dstm, in0=srcp, in1=src_b, op=ALU.add),
                         reads=[bres, const_res], writes=[mods_res])
                    for m in (m3,):
                        gk = (("nf1", l), ("nmix", l), ("nf2", l))[m]
                        go, _ = SM[gk]
                        for s in range(n_seq):
                            base = ((l * 3 + m) * n_seq + s) * KC
                            sc = mods[:, l * MODW:(l + 1) * MODW].rearrange("p (j s) -> p j s", s=n_seq)[:, (3 * m + 1) * 8:(3 * m + 2) * 8, s]
                            gt = mods[:, l * MODW:(l + 1) * MODW].rearrange("p (j s) -> p j s", s=n_seq)[:, (3 * m + 2) * 8:(3 * m + 3) * 8, s]
                            P.op("dve", lambda E, sc=sc, base=base, go=go: E.scalar_tensor_tensor(
                                out=modA[:, base:base + KC], in0=sc, scalar=1.0, in1=smalls[:, go:go + KC],
                                op0=ALU.add, op1=ALU.mult), reads=[mods_res, const_res], writes=[mods_res])
                            rw = 1.0 if m == 1 else 0.5
                            P.op("dve", lambda E, gt=gt, base=base, rw=rw: E.tensor_scalar(
                                out=modG[:, base:base + KC], in0=gt, scalar1=rw, scalar2=None, op0=ALU.mult),
                                reads=[mods_res], writes=[mods_res])
            return fn

        for t in range(T0, T0 + 12):
            out_list.append((wjob(t), comp(t)))

    def mA(l, m, s, c):
        b = ((l * 3 + m) * n_seq + s) * KC + c
        return modA[:, b:b + 1]

    def mG(l, m, s, c):
        b = ((l * 3 + m) * n_seq + s) * KC + c
        return modG[:, b:b + 1]

    def mB(l, m, s, c):
        j = (3 * m) * 8 + c
        b = l * MODW + j * n_seq + s
        return mods[:, b:b + 1]

    def load_seq_steps(s):
        def fn(slot, res):
            for i in range(S // 128):
                xi = nxt("xin", 2)
                src = x_d[s * S + i * 128: s * S + (i + 1) * 128, :]
                P.op("sp", lambda E, xi=xi, src=src: E.dma_start(out=xin[xi], in_=src),
                     writes=[xin_res[xi]], chan=xin_ch[xi])
                for half in range(2):
                    bk, bres = get_bank()
                    for cc in range(4):
                        c = half * 4 + cc
                        P.op("pe", lambda E, bk=bk, cc=cc, c=c, xi=xi: E.transpose(
                            out=bk[:, cc * 128:(cc + 1) * 128], in_=xin[xi][:, c * 128:(c + 1) * 128], identity=ident[:]),
                            reads=[xin_res[xi], const_res], writes=[bres])
                    dst = xT[:, half * 4:(half + 1) * 4, i * 128:(i + 1) * 128]
                    srcp = bk[:, :].rearrange("p (c t) -> p c t", c=4)
                    eng = "dve" if half == 0 else "act"
                    if eng == "dve":
                        P.op("dve", lambda E, dst=dst, srcp=srcp: E.tensor_copy(out=dst, in_=srcp),
                             reads=[bres], writes=x_res[i // 4][half * 4:(half + 1) * 4])
                    else:
                        P.op("act", lambda E, dst=dst, srcp=srcp: E.activation(out=dst, in_=srcp, func=AF.Copy),
                             reads=[bres], writes=x_res[i // 4][half * 4:(half + 1) * 4])
                if subs and i % 4 == 3 and i // 4 >= 1:
                    emit_norm_tile(subs[0][0], subs[0][1], s, i // 4 - 1)
            if subs:
                emit_norm_tile(subs[0][0], subs[0][1], s, NT - 1)
        add(None, fn)

    def emit_rstd(tt):
        bk, bres = get_bank()
        pool6 = [(sq[0], sq_res[0]), (sq[1], sq_res[1])] + ([(pT[i], pT_res[i]) for i in range(NPT)] if do_mix else [])
        for c in range(KC):
            qb, qbres = pool6[nxt("sqx", len(pool6))]
            P.op("act", lambda E, qb=qb, c=c: E.activation(out=qb[:], in_=xT[:, c, tt * TT:(tt + 1) * TT], func=AF.Square),
                 reads=[x_res[tt][c]], writes=[qbres])
            P.op("pe", lambda E, qb=qb, c=c, bk=bk: E.matmul(bk[:], lhsT=ones_bf[:], rhs=qb[:], start=(c == 0), stop=(c == KC - 1)),
                 reads=[qbres, const_res], writes=[bres])
        ri = nxt("rstd", NRSTD)
        P.op("act", lambda E, ri=ri, bk=bk: E.activation(out=rstd[ri][:], in_=bk[:], func=AF.Ln, bias=epsb[:], scale=1.0 / D),
             reads=[bres, const_res], writes=[rstd_res[ri]])
        P.op("act", lambda E, ri=ri: E.activation(out=rstd[ri][:], in_=rstd[ri][:], func=AF.Exp, scale=-0.5),
             reads=[rstd_res[ri]], writes=[rstd_res[ri]])
        return rstd[ri], rstd_res[ri]

    def emit_norm_tile(l, m, s, tt):
        r, rres = emit_rstd(tt)
        for c in range(KC):
            tf, tres = get_tmpf()
            P.op("dve", lambda E, tf=tf, c=c: E.tensor_tensor(
                out=tf[:], in0=xT[:, c, tt * TT:(tt + 1) * TT], in1=r[:], op=ALU.mult),
                reads=[x_res[tt][c], rres], writes=[tres])
            P.op("act", lambda E, tf=tf, c=c: E.activation(
                out=hT[:, c, tt * TT:(tt + 1) * TT], in_=tf[:], func=AF.Identity,
                scale=mA(l, m, s, c), bias=mB(l, m, s, c)),
                reads=[tres, mods_res], writes=[h_res[tt][c]])

    subs = []
    for _l in range(depth):
        if do_ffn:
            subs.append((_l, 0))
        if do_mix:
            subs.append((_l, 1))
        if do_ffn:
            subs.append((_l, 2))
    nxt_sub = {subs[i]: (subs[i + 1] if i + 1 < len(subs) else None) for i in range(len(subs))}

    def add_next_norm(l, m, s, tiles):
        ns = nxt_sub[(l, m)]
        if ns is None:
            return

        def fn(slot, res):
            for tt in tiles:
                emit_norm_tile(ns[0], ns[1], s, tt)
        add(None, fn)

    GU_CW = 256
    NGU = FF // GU_CW
    act_res = [[Res("act%d_%d" % (f, t)) for t in range(2)] for f in range(FC)]

    def act_view(f, t2):
        return scr[:, (f * 2 + t2) * TT:(f * 2 + t2 + 1) * TT]

    def ffn_steps(l, which, s):
        m = 0 if which == 0 else 2
        wgu = wgu_d[which][l].rearrange("(k p) n -> p k n", p=128)
        wdn = wdn_d[which][l].rearrange("(f p) n -> p f n", p=128)

        def gu_job(wt):
            def load(slot, res, ch):
                for g in range(2):
                    dst = slot[:, 0:KC * 2 * GU_CW].rearrange("p (k g j) -> p k g j", k=KC, g=2)[:, :, g, :]
                    src = wgu[:, :, g * FF + wt * GU_CW: g * FF + (wt + 1) * GU_CW]
                    P.op("pool", lambda E, dst=dst, src=src: E.dma_start(out=dst, in_=src), writes=[res], chan=ch)
            return load

        def gu_comp(wt, tg):
            def fn(slot, res):
                W = slot[:, 0:KC * 2 * GU_CW].rearrange("p (k g j) -> p k g j", k=KC, g=2)
                for t2 in range(2):
                    tt = tg * 2 + t2
                    for half in range(GU_CW // 128):
                        f = wt * (GU_CW // 128) + half
                        pg, pgres = get_bank()
                        pu, pures = get_bank()
                        for g, (bk, bres) in enumerate(((pg, pgres), (pu, pures))):
                            for k in range(KC):
                                P.op("pe", lambda E, bk=bk, g=g, k=k, half=half, tt=tt: E.matmul(
                                    bk[:], lhsT=W[:, k, g, half * 128:(half + 1) * 128],
                                    rhs=hT[:, k, tt * TT:(tt + 1) * TT], start=(k == 0), stop=(k == KC - 1)),
                                    reads=[res, h_res[tt][k]], writes=[bres])
                        tf, tres = get_tmpf()
                        P.op("act", lambda E, tf=tf, pg=pg: E.activation(out=tf[:], in_=pg[:], func=AF.Silu),
                             reads=[pgres], writes=[tres])
                        P.op("dve", lambda E, tf=tf, pu=pu, f=f, t2=t2: E.tensor_tensor(
                            out=act_view(f, t2), in0=tf[:], in1=pu[:], op=ALU.mult),
                            reads=[tres, pures], writes=[act_res[f][t2]])
            return fn

        def dn_job(dc):
            def load(slot, res, ch):
                for a in range(2):
                    f0, f1 = a * 11, (a + 1) * 11
                    dst = slot[:, 0:FC * 128].rearrange("p (f j) -> p f j", f=FC)[:, f0:f1, :]
                    src = wdn[:, f0:f1, dc * 128:(dc + 1) * 128]
                    P.op("pool", lambda E, dst=dst, src=src: E.dma_start(out=dst, in_=src), writes=[res], chan=ch)
            return load

        def dn_comp(dc, tg):
            def fn(slot, res):
                W = slot[:, 0:FC * 128].rearrange("p (f j) -> p f j", f=FC)
                for t2 in range(2):
                    tt = tg * 2 + t2
                    bk, bres = get_bank()
                    for f in range(FC):
                        P.op("pe", lambda E, bk=bk, f=f, t2=t2: E.matmul(
                            bk[:], lhsT=W[:, f, :], rhs=act_view(f, t2), start=(f == 0), stop=(f == FC - 1)),
                            reads=[res, act_res[f][t2]], writes=[bres])
                    xs = xT[:, dc, tt * TT:(tt + 1) * TT]
                    P.op("dve", lambda E, bk=bk, xs=xs: E.scalar_tensor_tensor(
                        out=xs, in0=bk[:], scalar=mG(l, m, s, dc), in1=xs, op0=ALU.mult, op1=ALU.add),
                        reads=[bres, x_res[tt][dc], mods_res], writes=[x_res[tt][dc]])
            return fn

        for tg in range(2):
            for wt in range(NGU):
                add(gu_job(wt), gu_comp(wt, tg), tag=("ffn", l, which, s))
                if tg == 1 and wt == 1:
                    add_next_norm(l, m, s, [0, 1])
            for dc in range(KC):
                add(dn_job(dc), dn_comp(dc, tg), tag=("ffn", l, which, s))
        add_next_norm(l, m, s, [2, 3])

    def final_steps(s):
        def fn(slot, res):
            fo, _ = SM[("fin", 0)]
            for tt in range(NT):
                r, rres = emit_rstd(tt)
                for c in range(KC):
                    P.op("dve", lambda E, c=c, r=r, tt=tt: E.scalar_tensor_tensor(
                        out=xT[:, c, tt * TT:(tt + 1) * TT], in0=xT[:, c, tt * TT:(tt + 1) * TT],
                        scalar=smalls[:, fo + c:fo + c + 1], in1=r[:], op0=ALU.mult, op1=ALU.mult),
                        reads=[x_res[tt][c], rres, const_res], writes=[x_res[tt][c]])
                for i4 in range(4):
                    i = tt * 4 + i4
                    xi = nxt("xin", 2)
                    for half in range(2):
                        bk, bres = get_bank()
                        for cc in range(4):
                            c = half * 4 + cc
                            P.op("pe", lambda E, bk=bk, cc=cc, c=c, i=i: E.transpose(
                                out=bk[:, cc * 128:(cc + 1) * 128], in_=xT[:, c, i * 128:(i + 1) * 128], identity=ident[:]),
                                reads=[x_res[tt][c], const_res], writes=[bres])
                        dst = xin[xi][:, half * 512:(half + 1) * 512]
                        if half == 0:
                            P.op("dve", lambda E, dst=dst, bk=bk: E.tensor_copy(out=dst, in_=bk[:]),
                                 reads=[bres], writes=[xin_res[xi]])
                        else:
                            P.op("act", lambda E, dst=dst, bk=bk: E.activation(out=dst, in_=bk[:], func=AF.Copy),
                                 reads=[bres], writes=[xin_res[xi]])
                    dsto = out_d[s * S + i * 128: s * S + (i + 1) * 128, :]
                    P.op("sp", lambda E, xi=xi, dsto=dsto: E.dma_start(out=dsto, in_=xin[xi]),
                         reads=[xin_res[xi]], chan=xout_ch[xi])
        add(None, fn)

    HD = 64
    A_SCALE = HD ** -0.5
    C_SCALE = 96 ** -0.5
    winl = [win_d[l].rearrange("(k p) n -> p k n", p=128) for l in range(L)]
    O_OA, O_OB, O_OC = 0, 3 * S, 6 * S
    O_Q = 8 * S
    O_K = O_Q + 3 * S
    O_V = O_K + S
    O_MG = O_Q
    assert O_V + 16 * 192 <= SCR_E
    oa_v = scr[:, O_OA:O_OA + 3 * S].rearrange("p (c t) -> p c t", c=3)
    ob_v = scr[:, O_OB:O_OB + 3 * S].rearrange("p (c t) -> p c t", c=3)
    oc_v = scr[:, O_OC:O_OC + 2 * S].rearrange("p (c t) -> p c t", c=2)
    q_v = scr[:, O_Q:O_Q + 3 * S].rearrange("p (c t) -> p c t", c=3)
    k_v = scr[:, O_K:O_K + S]
    qc_v = scr[:, O_Q:O_Q + 2 * S].rearrange("p (c t) -> p c t", c=2)
    kc_v = scr[:, O_Q + 2 * S:O_Q + 4 * S].rearrange("p (c t) -> p c t", c=2)
    va_v = scr[:, O_V:O_V + 16 * 192].rearrange("p (i c) -> p i c", c=192)
    mg_v = scr[:, O_MG:O_MG + KC * TT].rearrange("p (c t) -> p c t", c=KC)
    q_res = [[Res("q%d_%d" % (c, t)) for t in range(NT)] for c in range(3)]
    k_res = [[Res("k%d_%d" % (c, t)) for t in range(NT)] for c in range(2)]
    v_res = Res("vaug")
    o_res = {m: [[Res("o%s%d_%d" % (m, c, t)) for t in range(NT)] for c in range(3)] for m in "abc"}
    mg_res = [Res("mg%d" % c) for c in range(KC)]
    NPT = 4
    pT = [nc.alloc_sbuf_tensor("pT%d" % i, [128, TT], BF16) for i in range(NPT)]
    pT_res = [Res("pT%d" % i) for i in range(NPT)]
    NPF = 2
    pF = [nc.alloc_sbuf_tensor("pF%d" % i, [128, 384], F32) for i in range(NPF)]
    pF_res = [Res("pF%d" % i) for i in range(NPF)]
    stg = [nc.alloc_sbuf_tensor("stg%d" % i, [128, TT], BF16) for i in range(2)]
    stg_res = [Res("stg%d" % i) for i in range(2)]
    tabs = [scr[:, O_OC + S:O_OC + 2 * S].bitcast(F32).rearrange("p (a t) -> p a t", a=2)]
    tab_res = [Res("tab0")]
    tab_ch = [P.chan("tab0")]
    rr.update({"pT": 0, "pF": 0, "stg": 0, "tab": 0})
    cbf = nc.alloc_sbuf_tensor("cbf", [128, 5 * 128], BF16)
    RA, R96, BONES, JREV, ZER = (cbf[:, i * 128:(i + 1) * 128] for i in range(5))
    EB = nc.alloc_sbuf_tensor("EB", [128, 6 * 384], BF16)
    esink = nc.alloc_sbuf_tensor("esink", [128, L * 6], F32)
    eb_res = Res("EB")

    def mix_consts():
        def fn(slot, res):
            cf = scr[:, 0:2 * CF_COLS].bitcast(F32)
            P.op("sp", lambda E: E.dma_start(out=cf, in_=cf_d), writes=[const_res], chan=const_ch)
            P.op("dve", lambda E: E.tensor_copy(out=cbf[:, 0:384], in_=cf[:, 0:384]), reads=[const_res], writes=[const_res])
            P.op("dve", lambda E: E.tensor_copy(out=cbf[:, 384:512], in_=cf[:, 1408:1536]), reads=[const_res], writes=[const_res])
            P.op("dve", lambda E: E.memset(cbf[:, 512:640], 0.0), writes=[const_res])
            for l in range(L):
                o, _ = SM[("sink", l)]
                P.op("act", lambda E, o=o, l=l: E.activation(out=esink[:, l * 6:(l + 1) * 6], in_=smalls[:, o:o + 6], func=AF.Exp),
                     reads=[const_res], writes=[const_res])
            bk, bres = get_bank()
            ro, _ = SM[("relb", 0)]
            P.op("pe", lambda E: E.matmul(bk[0:6, 0:512], lhsT=smalls[0:32, ro:ro + 6], rhs=cf[0:32, 384:896], start=True, stop=True),
                 reads=[const_res], writes=[bres])
            tb = tmpf[0][0:6, :]
            tbb = stg[0][0:6, :]
            P.op("act", lambda E: E.activation(out=tb, in_=bk[0:6, 0:512], func=AF.Exp), reads=[bres], writes=[eb_res, tmpf_res[0]])
            P.op("dve", lambda E: E.tensor_tensor(out=tbb, in0=tb, in1=cf[0:6, 896:1408], op=ALU.mult),
                 reads=[eb_res, const_res, tmpf_res[0]], writes=[eb_res, stg_res[0]])
            ebc = P.chan("ebc")
            P.op("sp", lambda E: E.dma_start(out=tb_d.ap(), in_=tbb), reads=[eb_res, stg_res[0]], writes=[eb_res], chan=ebc)
            for h in range(6):
                src = bass.AP(tensor=tb_d, offset=h * 512, ap=[[1, 128], [1, 384]])
                P.op("sp", lambda E, h=h, src=src: E.dma_start(out=stg[1][:, 0:384], in_=src),
                     reads=[eb_res], writes=[stg_res[1]], chan=ebc)
                b5, b5res = get_bank()
                P.op("pe", lambda E, b5=b5: E.matmul(b5[:, 0:384], lhsT=JREV, rhs=stg[1][:, 0:384], start=True, stop=True),
                     reads=[stg_res[1], const_res], writes=[b5res])
                P.op("act", lambda E, b5=b5, h=h: E.activation(out=EB[:, h * 384:(h + 1) * 384], in_=b5[:, 0:384], func=AF.Copy),
                     reads=[b5res], writes=[eb_res])
        add(None, fn)

    def load_tab(which, tt):
        ti = 0
        src = tabs_d[which][:, :, tt * TT:(tt + 1) * TT]
        P.op("sp", lambda E, ti=ti, src=src: E.dma_start(out=tabs[ti], in_=src), writes=[tab_res[ti]] + o_res["c"][1] + [act_res[14][0], act_res[14][1], act_res[15][0], act_res[15][1]], chan=tab_ch[ti])
        return tabs[ti], tab_res[ti]

    def finish_qk(ps, pres, nr, dst, dres, norm=None, rope=None, defer=False, spool=None):
        if spool is None:
            spool = [(stg[0], stg_res[0]), (stg[1], stg_res[1])]
        rr["stgx"] = (rr["stgx"] + 1) % len(spool)
        st, sres = spool[rr["stgx"]]
        tgt = st[0:nr, :] if rope is not None else dst
        tres = sres if rope is not None else dres
        if norm is not None:
            ones_l, nd, gain = norm
            qi = nxt("sq", 2)
            P.op("act", lambda E: E.activation(out=sq[qi][0:nr, :], in_=ps[0:nr, :], func=AF.Square), reads=[pres], writes=[sq_res[qi]])
            b2, b2res = get_bank()
            P.op("pe", lambda E: E.matmul(b2[0:nr, :], lhsT=ones_l, rhs=sq[qi][0:nr, :], start=True, stop=True),
                 reads=[sq_res[qi], const_res], writes=[b2res])
            ri = nxt("rstd", NRSTD)
            P.op("act", lambda E: E.activation(out=rstd[ri][0:nr, :], in_=b2[0:nr, :], func=AF.Ln, bias=epsb[0:nr, :], scale=1.0 / nd),
                 reads=[b2res, const_res], writes=[rstd_res[ri]])
            P.op("act", lambda E: E.activation(out=rstd[ri][0:nr, :], in_=rstd[ri][0:nr, :], func=AF.Exp, scale=-0.5),
                 reads=[rstd_res[ri]], writes=[rstd_res[ri]])
            P.op("dve", lambda E: E.scalar_tensor_tensor(out=tgt, in0=ps[0:nr, :], scalar=gain, in1=rstd[ri][0:nr, :],
                                                         op0=ALU.mult, op1=ALU.mult),
                 reads=[pres, rstd_res[ri], const_res], writes=[tres])
        else:
            P.op("act", lambda E: E.activation(out=tgt, in_=ps[0:nr, :], func=AF.Copy), reads=[pres], writes=[tres])
        if rope is None:
            return None

        def part2():
            Rl, tab, tabres = rope
            b3, b3res = get_bank()
            P.op("pe", lambda E: E.matmul(b3[0:nr, :], lhsT=Rl, rhs=st[0:nr, :], start=True, stop=True),
                 reads=[sres, const_res], writes=[b3res])
            t1, t1res = get_tmpf()
            t2, t2res = get_tmpf()
            P.op("dve", lambda E: E.tensor_tensor(out=t1[0:nr, :], in0=st[0:nr, :], in1=tab[0:nr, 0, :], op=ALU.mult),
                 reads=[sres, tabres], writes=[t1res])
            P.op("dve", lambda E: E.tensor_tensor(out=t2[0:nr, :], in0=b3[0:nr, :], in1=tab[0:nr, 1, :], op=ALU.mult),
                 reads=[b3res, tabres], writes=[t2res])
            P.op("dve", lambda E: E.tensor_tensor(out=dst, in0=t1[0:nr, :], in1=t2[0:nr, :], op=ALU.add),
                 reads=[t1res, t2res], writes=[dres])
        if defer:
            return part2
        part2()
        return None

    def proj(W, wres, ncolchunk, tt, M=128, acc=False):
        bk, bres = get_bank(acc)
        for k in range(KC):
            P.op("pe", lambda E, k=k: E.matmul(bk[0:M, :], lhsT=W[:, k, ncolchunk], rhs=hT[:, k, tt * TT:(tt + 1) * TT],
                                               start=(k == 0), stop=(k == KC - 1)),
                 reads=[wres, h_res[tt][k]], writes=[bres])
        return bk, bres

    def proj_v(W, wres, cols0, ncols, i):
        bk, bres = get_bank()
        for k in range(KC):
            P.op("pe", lambda E, k=k: E.matmul(bk[:, 0:ncols], lhsT=hT[:, k, i * 128:(i + 1) * 128], rhs=W[:, k, cols0:cols0 + ncols],
                                               start=(k == 0), stop=(k == KC - 1)),
                 reads=[wres, h_res[i // 4][k]], writes=[bres])
        return bk, bres

    LA = 3

    def attn_dense(qsel, ksel, K, groups, scale, omap):
        its = [(g, tt, kc) for g in range(len(groups)) for tt in range(NT) for kc in range(S // 128)]
        obs = {}
        pend = []
        la = 1 if max(len(g) for g in groups) > 1 else 2

        def emit_S(g, tt, kc):
            pis = []
            sbs = []
            for (h, vsel, half) in groups[g]:
                if kc == 0:
                    obs[(h, tt)] = get_bank(acc=True)
                qa, qres = qsel(h, tt)
                ka, kres = ksel(h, kc)
                sb, sbres = get_bank()
                P.op("pe", lambda E, sb=sb, ka=ka, qa=qa: E.matmul(sb[:], lhsT=ka, rhs=qa, start=True, stop=True), reads=[kres, qres], writes=[sbres])
                sbs.append((sb, sbres))
            for (sb, sbres) in sbs:
                pi = nxt("pT", NPT)
                P.op("act", lambda E, sb=sb, pi=pi: E.activation(out=pT[pi][:], in_=sb[:], func=AF.Exp, scale=scale), reads=[sbres], writes=[pT_res[pi]])
                pis.append(pi)
            return pis

        def emit_PV(g, tt, kc, pis):
            for (h, vsel, half), pi in zip(groups[g], pis):
                ob, obres = obs[(h, tt)]
                va = vsel(kc)
                P.op("pe", lambda E, ob=ob, va=va, pi=pi: E.matmul(ob[:], lhsT=va, rhs=pT[pi][:], start=(kc == 0), stop=(kc == S // 128 - 1)),
                     reads=[v_res, pT_res[pi]], writes=[obres])
            if kc == S // 128 - 1:
                for (h, vsel, half) in groups[g]:
                    ob, obres = obs[(h, tt)]
                    emit_onorm(ob, obres, half, omap(h, tt), None)

        for (g, tt, kc) in its:
            pend.append((g, tt, kc, emit_S(g, tt, kc)))
            if len(pend) > la:
                emit_PV(*pend.pop(0))
        while pend:
            emit_PV(*pend.pop(0))

    def emit_onorm(ob, obres, half, out, sink_ap, ncols=TT):
        oap, ores = out
        olo, dlo = (0, 64) if half == 0 else (64, 0)
        tf, tres = get_tmpf()
        if sink_ap is not None:
            sap = sink_ap(dlo)
            P.op("act", lambda E: E.activation(out=tf[dlo:dlo + 64, 0:ncols], in_=ob[dlo:dlo + 64, 0:ncols], func=AF.Ln, bias=sap, scale=1.0),
                 reads=[obres, const_res], writes=[tres])
            P.op("act", lambda E: E.activation(out=tf[dlo:dlo + 64, 0:ncols], in_=tf[dlo:dlo + 64, 0:ncols], func=AF.Exp, scale=-1.0),
                 reads=[tres], writes=[tres])
        else:
            P.op("dve", lambda E: E.reciprocal(out=tf[dlo:dlo + 64, 0:ncols], in_=ob[dlo:dlo + 64, 0:ncols]), reads=[obres], writes=[tres])
        P.op("dve", lambda E: E.tensor_tensor(out=oap, in0=ob[olo:olo + 64, 0:ncols], in1=tf[dlo:dlo + 64, 0:ncols], op=ALU.mult),
             reads=[obres, tres], writes=[ores])

    def vaug_ones():
        P.op("dve", lambda E: E.memset(va_v[:, :, 64:128], 1.0), writes=[v_res])

    def vsel_ab(half):
        return (lambda kc: va_v[:, kc, 0:128]) if half == 0 else (lambda kc: va_v[:, kc, 64:192])

    def mixer_ab_steps(l, s, which):
        base = 0 if which == "a" else 640
        ov = oa_v if which == "a" else ob_v

        def job1(slot, res, ch):
            W = slot[:, 0:KC * 512].rearrange("p (k j) -> p k j", k=KC)
            for j in range(3):
                for half in range(2):
                    h = j + 3 * half
                    src = winl[l][:, :, base + h * 64: base + (h + 1) * 64]
                    dst = W[:, :, j * 128 + half * 64: j * 128 + (half + 1) * 64]
                    P.op("pool", lambda E, dst=dst, src=src: E.dma_start(out=dst, in_=src), writes=[res], chan=ch)
            src = winl[l][:, :, base + 384: base + 512]
            P.op("pool", lambda E, src=src: E.dma_start(out=W[:, :, 384:512], in_=src), writes=[res], chan=ch)

        def comp1(slot, res):
            W = slot[:, 0:KC * 512].rearrange("p (k j) -> p k j", k=KC)
            if which == "a":
                go, _ = SM[("aqn", l)]
                ko, _ = SM[("akn", l)]
            items = [(tt, c) for tt in range(NT) for c in range(4)]
            pj = {}
            tabst = {}

            def do_proj(idx):
                tt, c = items[idx]
                pj[idx] = proj(W, res, slice(c * 128, (c + 1) * 128), tt)
                hold(pj[idx][0])

            do_proj(0)
            pend2 = []
            for idx, (tt, c) in enumerate(items):
                if idx + 1 < len(items):
                    do_proj(idx + 1)
                bk, bres = pj.pop(idx)
                if c < 3:
                    dst, dres = q_v[:, c, tt * TT:(tt + 1) * TT], q_res[c][tt]
                else:
                    dst, dres = k_v[:, tt * TT:(tt + 1) * TT], k_res[0][tt]
                if which == "a":
                    if c == 0:
                        while pend2:
                            pend2.pop(0)()
                        tabst["t"] = load_tab(0, tt)
                    tab, tabres = tabst["t"]
                    g = smalls[:, go:go + 1] if c < 3 else smalls[:, ko:ko + 1]
                    p2 = finish_qk(bk, bres, 128, dst, dres, norm=(BONES, HD, g), rope=(RA, tab, tabres), defer=True,
                                   spool=[(stg[0], stg_res[0]), (stg[1], stg_res[1])] + [(pT[i], pT_res[i]) for i in range(NPT)])
                    release(bk)
                    pend2.append(p2)
                    if len(pend2) > 2:
                        pend2.pop(0)()
                else:
                    finish_qk(bk, bres, 128, dst, dres)
                    release(bk)
            while pend2:
                pend2.pop(0)()

        def job2(slot, res, ch):
            W = slot[:, 0:KC * 128].rearrange("p (k j) -> p k j", k=KC)
            src = winl[l][:, :, base + 512: base + 640]
            P.op("pool", lambda E, src=src: E.dma_start(out=W, in_=src), writes=[res], chan=ch)

        def comp2(slot, res):
            W = slot[:, 0:KC * 128].rearrange("p (k j) -> p k j", k=KC)
            vaug_ones()
            for i in range(S // 128):
                bk, bres = proj_v(W, res, 0, 128, i)
                P.op("act", lambda E, bk=bk, i=i: E.activation(out=va_v[:, i, 0:64], in_=bk[:, 0:64], func=AF.Copy), reads=[bres], writes=[v_res])
                P.op("dve", lambda E, bk=bk, i=i: E.tensor_copy(out=va_v[:, i, 128:192], in_=bk[:, 64:128]), reads=[bres], writes=[v_res])

        def attn(slot, res):
            if which == "a":
                def qsel(h, tt):
                    j, half = h % 3, h // 3
                    return q_v[half * 64:(half + 1) * 64, j, tt * TT:(tt + 1) * TT], q_res[j][tt]

                def ksel(h, kc):
                    half = h // 3
                    return k_v[half * 64:(half + 1) * 64, kc * 128:(kc + 1) * 128], k_res[0][kc // 4]

                def omap(h, tt):
                    j, half = h % 3, h // 3
                    return ov[half * 64:(half + 1) * 64, j, tt * TT:(tt + 1) * TT], o_res["a"][j][tt]
                attn_dense(qsel, ksel, 64, [[(j, vsel_ab(0), 0), (j + 3, vsel_ab(1), 1)] for j in range(3)], A_SCALE, omap)
            else:
                attn_window(l)
        add(job1, comp1)
        add(job2, comp2)
        add(None, attn)

    def attn_window(l):
        its = []
        for h in range(6):
            for tt in range(NT):
                kcs = [kc for kc in range(4 * tt - 1, 4 * tt + 5) if 0 <= kc < S // 128]
                for n_i, kc in enumerate(kcs):
                    its.append((h, tt, kc, n_i == 0, n_i == len(kcs) - 1))
        obs = {}
        pend = []

        def emit_S(h, tt, kc, first, last):
            j, half = h % 3, h // 3
            if first:
                ob, obres = get_bank(acc=True)
                obs[(h, tt)] = (ob, obres)
                P.op("pe", lambda E: E.matmul(ob[:], lhsT=ZER, rhs=hT[:, 0, 0:TT], start=True, stop=False),
                     reads=[const_res, h_res[0][0]], writes=[obres])
            qb0 = max(kc - 1, 4 * tt)
            qb1 = min(kc + 1, 4 * tt + 3)
            ncol = (qb1 - qb0 + 1) * 128
            q0 = qb0 * 128
            e0 = (qb0 - (kc - 1)) * 128
            c0 = q0 - tt * TT
            sb, sbres = get_bank()
            P.op("pe", lambda E: E.matmul(
                sb[:, 0:ncol], lhsT=k_v[half * 64:(half + 1) * 64, kc * 128:(kc + 1) * 128],
                rhs=q_v[half * 64:(half + 1) * 64, j, q0:q0 + ncol], start=True, stop=True),
                reads=[k_res[0][kc // 4], q_res[j][tt]], writes=[sbres])
            fi = nxt("pF", NPF)
            P.op("act", lambda E: E.activation(out=pF[fi][:, 0:ncol], in_=sb[:, 0:ncol], func=AF.Exp, scale=A_SCALE),
                 reads=[sbres], writes=[pF_res[fi]])
            pi = nxt("pT", NPT)
            P.op("pool" if (kc % 2 == 0) else "dve", lambda E: E.tensor_tensor(
                out=pT[pi][:, 0:ncol], in0=pF[fi][:, 0:ncol], in1=EB[:, h * 384 + e0:h * 384 + e0 + ncol], op=ALU.mult),
                reads=[pF_res[fi], eb_res], writes=[pT_res[pi]])
            return (pi, c0, ncol)

        def emit_PV(h, tt, kc, first, last, info):
            pi, c0, ncol = info
            j, half = h % 3, h // 3
            ob, obres = obs[(h, tt)]
            va = vsel_ab(half)(kc)
            P.op("pe", lambda E: E.matmul(ob[:, c0:c0 + ncol], lhsT=va, rhs=pT[pi][:, 0:ncol], start=False, stop=last),
                 reads=[v_res, pT_res[pi]], writes=[obres])
            if last:
                sk = (lambda dlo: esink[dlo:dlo + 64, l * 6 + h:l * 6 + h + 1])
                emit_onorm(ob, obres, half, (ob_v[half * 64:(half + 1) * 64, j, tt * TT:(tt + 1) * TT], o_res["b"][j][tt]), sk)

        for it in its:
            pend.append(it + (emit_S(*it),))
            if len(pend) > LA:
                emit_PV(*pend.pop(0))
        while pend:
            emit_PV(*pend.pop(0))

    NW = 256 + 128 + 96
    C_WQ = KC * NW
    C_WKN = C_WQ + 384
    C_WV = C_WKN + 192
    assert C_WV + 128 <= SLOT_E

    def mixer_c_steps(l, s, p):
        def job(slot, res, ch):
            W = slot[:, 0:KC * NW].rearrange("p (k j) -> p k j", k=KC)
            P.op("pool", lambda E: E.dma_start(out=W[:, :, 384:448], in_=zer_d.rearrange("p (k j) -> p k j", k=KC)), writes=[res], chan=ch)
            src = winl[l][:, :, 1280:1664]
            P.op("pool", lambda E: E.dma_start(out=W[:, :, 0:384], in_=src), writes=[res], chan=ch)
            src2 = winl[l][:, :, 1664:1696]
            P.op("pool", lambda E: E.dma_start(out=W[:, :, 448:480], in_=src2), writes=[res], chan=ch)
            wq = cqup_d[l].rearrange("(k p) n -> p k n", p=128)[:, :, p * 192:(p + 1) * 192]
            P.op("pool", lambda E: E.dma_start(out=slot[:, C_WQ:C_WQ + 384].rearrange("p (k j) -> p k j", k=2), in_=wq), writes=[res], chan=ch)
            for hh in range(2):
                hd = 2 * p + hh
                P.op("pool", lambda E, hh=hh, hd=hd: E.dma_start(out=slot[:, C_WKN + hh * 96: C_WKN + hh * 96 + 64], in_=ckvup_d[l][:, hd * 128: hd * 128 + 64]),
                     writes=[res], chan=ch)
                P.op("pool", lambda E, hh=hh: E.dma_start(out=slot[:, C_WKN + hh * 96 + 64: C_WKN + (hh + 1) * 96], in_=zer_d[:, 0:32]),
                     writes=[res], chan=ch)
                P.op("pool", lambda E, hh=hh, hd=hd: E.dma_start(out=slot[:, C_WV + hh * 64: C_WV + (hh + 1) * 64], in_=ckvup_d[l][:, hd * 128 + 64: hd * 128 + 128]),
                     writes=[res], chan=ch)

        def c_proj(slot, res, tt):
            W1 = slot[:, 0:KC * NW].rearrange("p (k j) -> p k j", k=KC)
            ql = [proj(W1, res, slice(c * 128, (c + 1) * 128), tt) for c in range(2)]
            for (b, _r) in ql:
                hold(b)
            kvl = proj(W1, res, slice(256, 384), tt, acc=True)
            return ql, kvl

        def c_tile(slot, res, tt, pre):
            W1 = slot[:, 0:KC * NW].rearrange("p (k j) -> p k j", k=KC)
            Wq = slot[:, C_WQ:C_WQ + 384].rearrange("p (k j) -> p k j", k=2)
            qo, _ = SM[("cqn", l)]
            kvo, _ = SM[("ckvn", l)]
            tab, tabres = load_tab(1, tt)
            cpool = [(pT[i], pT_res[i]) for i in range(1, NPT)]
            ql, kvl = pre
            nxt_pre = None
            bkks = []
            for hh in range(2):
                bkk, bkres = get_bank(acc=True)
                for k in range(KC):
                    P.op("pe", lambda E, k=k, bkk=bkk: E.matmul(bkk[0:96, :], lhsT=W1[:, k, 384:480], rhs=hT[:, k, tt * TT:(tt + 1) * TT], start=(k == 0), stop=False),
                         reads=[res, h_res[tt][k]], writes=[bkres])
                bkks.append((bkk, bkres))
            b2, b2res = get_bank()
            for c in range(2):
                qi = nxt("sq", 2)
                P.op("act", lambda E, qi=qi, c=c: E.activation(out=sq[qi][:], in_=ql[c][0][:], func=AF.Square), reads=[ql[c][1]], writes=[sq_res[qi]])
                P.op("pe", lambda E, qi=qi, c=c: E.matmul(b2[:], lhsT=ones_bf[:], rhs=sq[qi][:], start=(c == 0), stop=(c == 1)),
                     reads=[sq_res[qi], const_res], writes=[b2res])
            ri = nxt("rstd", NRSTD)
            P.op("act", lambda E: E.activation(out=rstd[ri][:], in_=b2[:], func=AF.Ln, bias=epsb[:], scale=1.0 / 256), reads=[b2res, const_res], writes=[rstd_res[ri]])
            P.op("act", lambda E: E.activation(out=rstd[ri][:], in_=rstd[ri][:], func=AF.Exp, scale=-0.5), reads=[rstd_res[ri]], writes=[rstd_res[ri]])
            qln = []
            for c in range(2):
                si = nxt("stg", 2)
                P.op("dve", lambda E, si=si, c=c: E.scalar_tensor_tensor(out=stg[si][:], in0=ql[c][0][:], scalar=smalls[:, qo + c:qo + c + 1], in1=rstd[ri][:],
                                                                     op0=ALU.mult, op1=ALU.mult),
                     reads=[ql[c][1], rstd_res[ri], const_res], writes=[stg_res[si]])
                qln.append((stg[si], stg_res[si]))
            for (b, _r) in ql:
                release(b)
            if tt + 1 < NT:
                nxt_pre = c_proj(slot, res, tt + 1)
            bqs = []
            for hh in range(2):
                bq, bqres = get_bank()
                for c in range(2):
                    P.op("pe", lambda E, c=c, hh=hh, bq=bq: E.matmul(bq[0:96, :], lhsT=Wq[:, c, hh * 96:(hh + 1) * 96], rhs=qln[c][0][:], start=(c == 0), stop=(c == 1)),
                         reads=[res, qln[c][1]], writes=[bqres])
                bqs.append((bq, bqres))
            p2q = [finish_qk(bqs[hh][0], bqs[hh][1], 96, qc_v[0:96, hh, tt * TT:(tt + 1) * TT], q_res[hh][tt],
                             rope=(R96[0:96, 0:96], tab, tabres), defer=True, spool=cpool) for hh in range(2)]
            qi2 = nxt("sq", 2)
            P.op("act", lambda E: E.activation(out=sq[qi2][:], in_=kvl[0][:], func=AF.Square), reads=[kvl[1]], writes=[sq_res[qi2]])
            b4, b4res = get_bank()
            P.op("pe", lambda E: E.matmul(b4[:], lhsT=ones_bf[:], rhs=sq[qi2][:], start=True, stop=True), reads=[sq_res[qi2], const_res], writes=[b4res])
            ri2 = nxt("rstd", NRSTD)
            P.op("act", lambda E: E.activation(out=rstd[ri2][:], in_=b4[:], func=AF.Ln, bias=epsb[:], scale=1.0 / 128), reads=[b4res, const_res], writes=[rstd_res[ri2]])
            P.op("act", lambda E: E.activation(out=rstd[ri2][:], in_=rstd[ri2][:], func=AF.Exp, scale=-0.5), reads=[rstd_res[ri2]], writes=[rstd_res[ri2]])
            kvn, kvnres = pT[0], pT_res[0]
            P.op("dve", lambda E: E.scalar_tensor_tensor(out=kvn[:], in0=kvl[0][:], scalar=smalls[:, kvo:kvo + 1], in1=rstd[ri2][:], op0=ALU.mult, op1=ALU.mult),
                 reads=[kvl[1], rstd_res[ri2], const_res], writes=[kvnres])
            for p2 in p2q:
                p2()
            p2k = []
            for hh in range(2):
                bkk, bkres = bkks[hh]
                P.op("pe", lambda E, hh=hh, bkk=bkk: E.matmul(bkk[0:96, :], lhsT=slot[:, C_WKN + hh * 96: C_WKN + (hh + 1) * 96], rhs=kvn[:], start=False, stop=True),
                     reads=[res, kvnres], writes=[bkres])
                p2k.append(finish_qk(bkk, bkres, 96, kc_v[0:96, hh, tt * TT:(tt + 1) * TT], k_res[hh][tt], rope=(R96[0:96, 0:96], tab, tabres), defer=True, spool=cpool))
            for i4 in range(4):
                i = tt * 4 + i4
                bv, bvres = get_bank()
                P.op("pe", lambda E, i4=i4, bv=bv: E.matmul(bv[:, 0:128], lhsT=kvn[:, i4 * 128:(i4 + 1) * 128], rhs=slot[:, C_WV:C_WV + 128], start=True, stop=True),
                     reads=[res, kvnres], writes=[bvres])
                P.op("act", lambda E, bv=bv, i=i: E.activation(out=va_v[:, i, 0:64], in_=bv[:, 0:64], func=AF.Copy), reads=[bvres], writes=[v_res])
                P.op("dve", lambda E, bv=bv, i=i: E.tensor_copy(out=va_v[:, i, 128:192], in_=bv[:, 64:128]), reads=[bvres], writes=[v_res])
            for p2 in p2k:
                p2()
            return nxt_pre

        def comp(slot, res):
            vaug_ones()
            P.op("dve", lambda E: E.memset(scr[64:128, O_Q:O_Q + 4 * S], 0.0),
                 writes=[q_res[hh][t] for hh in range(2) for t in range(NT)] + [k_res[hh][t] for hh in range(2) for t in range(NT)])
            pre = c_proj(slot, res, 0)
            for tt in range(NT):
                pre = c_tile(slot, res, tt, pre)

        def attn(slot, res):
            def qsel(h, tt):
                return qc_v[:, h, tt * TT:(tt + 1) * TT], q_res[h][tt]

            def ksel(h, kc):
                return kc_v[:, h, kc * 128:(kc + 1) * 128], k_res[h][kc // 4]

            def omap(h, tt):
                return oc_v[h * 64:(h + 1) * 64, p, tt * TT:(tt + 1) * TT], o_res["c"][p][tt]
            attn_dense(qsel, ksel, 96, [[(hh, vsel_ab(hh), hh)] for hh in range(2)], C_SCALE, omap)
        add(job, comp)
        add(None, attn)

    def merge_steps(l, s):
        for tt in range(NT):
            for dc in range(KC):
                def job(slot, res, ch, dc=dc):
                    G = slot[:, 0:3 * KC * 128].rearrange("p (m k j) -> p m k j", m=3, k=KC)
                    for m in range(3):
                        src = winl[l][:, :, 1696 + m * 1024 + dc * 128: 1696 + m * 1024 + (dc + 1) * 128]
                        P.op("pool", lambda E, m=m, src=src: E.dma_start(out=G[:, m, :, :], in_=src), writes=[res], chan=ch)
                    BR = slot[:, 3072:4096].rearrange("p (c j) -> p c j", c=8)
                    for mi, wd in enumerate((wbra_d, wbrb_d)):
                        for half in range(2):
                            src = wd[l].rearrange("(half j p) n -> half p j n", half=2, j=3)[half, :, :, dc * 128:(dc + 1) * 128]
                            P.op("pool", lambda E, mi=mi, half=half, src=src: E.dma_start(out=BR[half * 64:(half + 1) * 64, mi * 3:(mi + 1) * 3, :], in_=src),
                                 writes=[res], chan=ch)
                    src = wbrc_d[l].rearrange("(j p) n -> p j n", p=128)[:, :, dc * 128:(dc + 1) * 128]
                    P.op("pool", lambda E, src=src: E.dma_start(out=BR[:, 6:8, :], in_=src), writes=[res], chan=ch)

                def comp(slot, res, dc=dc, tt=tt):
                    G = slot[:, 0:3 * KC * 128].rearrange("p (m k j) -> p m k j", m=3, k=KC)
                    BR = slot[:, 3072:4096].rearrange("p (c j) -> p c j", c=8)
                    sgs = []
                    for m in range(3):
                        bk, bres = get_bank()
                        for k in range(KC):
                            P.op("pe", lambda E, bk=bk, m=m, k=k: E.matmul(bk[:], lhsT=G[:, m, k, :], rhs=hT[:, k, tt * TT:(tt + 1) * TT], start=(k == 0), stop=(k == KC - 1)),
                                 reads=[res, h_res[tt][k]], writes=[bres])
                        tf, tres = get_tmpf()
                        P.op("act", lambda E, bk=bk, tf=tf: E.activation(out=tf[:], in_=bk[:], func=AF.Sigmoid), reads=[bres], writes=[tres])
                        sgs.append((tf, tres))
                    acc = None
                    for m, (ov, nch, key) in enumerate(((oa_v, 3, "a"), (ob_v, 3, "b"), (oc_v, 2, "c"))):
                        bk, bres = get_bank()
                        for c in range(nch):
                            P.op("pe", lambda E, bk=bk, m=m, c=c, ov=ov, nch=nch: E.matmul(bk[:], lhsT=BR[:, m * 3 + c, :], rhs=ov[:, c, tt * TT:(tt + 1) * TT],
                                                                                       start=(c == 0), stop=(c == nch - 1)),
                                 reads=[res, o_res[key][c][tt]], writes=[bres])
                        tf, tres = sgs[m]
                        P.op("dve", lambda E, bk=bk, tf=tf: E.tensor_tensor(out=tf[:], in0=tf[:], in1=bk[:], op=ALU.mult), reads=[tres, bres], writes=[tres])
                    t0, t0res = sgs[0]
                    P.op("dve", lambda E: E.tensor_tensor(out=t0[:], in0=t0[:], in1=sgs[1][0][:], op=ALU.add), reads=[t0res, sgs[1][1]], writes=[t0res])
                    P.op("dve", lambda E: E.tensor_tensor(out=mg_v[:, dc, :], in0=t0[:], in1=sgs[2][0][:], op=ALU.add), reads=[t0res, sgs[2][1]], writes=[mg_res[dc]])
                add(job, comp)
            for oc2 in range(2):
                def jobo(slot, res, ch, oc2=oc2):
                    W = slot[:, 0:KC * 512].rearrange("p (k j) -> p k j", k=KC)
                    src = wout_d[l].rearrange("(k p) n -> p k n", p=128)[:, :, oc2 * 512:(oc2 + 1) * 512]
                    P.op("pool", lambda E, src=src: E.dma_start(out=W, in_=src), writes=[res], chan=ch)

                def compo(slot, res, oc2=oc2, tt=tt):
                    W = slot[:, 0:KC * 512].rearrange("p (k j) -> p k j", k=KC)
                    for c4 in range(4):
                        dcc = oc2 * 4 + c4
                        bk, bres = get_bank()
                        for k in range(KC):
                            P.op("pe", lambda E, bk=bk, k=k, c4=c4: E.matmul(bk[:], lhsT=W[:, k, c4 * 128:(c4 + 1) * 128], rhs=mg_v[:, k, :], start=(k == 0), stop=(k == KC - 1)),
                                 reads=[res, mg_res[k]], writes=[bres])
                        xs = xT[:, dcc, tt * TT:(tt + 1) * TT]
                        P.op("dve", lambda E, bk=bk, xs=xs, dcc=dcc: E.scalar_tensor_tensor(out=xs, in0=bk[:], scalar=mG(l, 1, s, dcc), in1=xs, op0=ALU.mult, op1=ALU.add),
                             reads=[bres, x_res[tt][dcc], mods_res], writes=[x_res[tt][dcc]])
                add(jobo, compo)
            if tt >= 1:
                add_next_norm(l, 1, s, [tt - 1])
        add_next_norm(l, 1, s, [NT - 1])

    def mix_steps(l, s):

        def zero_unused(slot, res):
            for key, ov, nch in (("a", oa_v, 3), ("b", ob_v, 3), ("c", oc_v, 2)):
                if key not in mixers:
                    for c in range(nch):
                        P.op("dve", lambda E, ov=ov, c=c: E.memset(ov[:, c, :], 0.0), writes=o_res[key][c])

        if "a" in mixers:
            mixer_ab_steps(l, s, "a")
        if "b" in mixers:
            mixer_ab_steps(l, s, "b")
        if "c" in mixers:
            mixer_c_steps(l, s, 0)
            mixer_c_steps(l, s, 1)
        if mixers != "abc":
            add(None, zero_unused)
        merge_steps(l, s)

    if do_mix:
        mix_consts()
    ada_pending = []
    if do_ada:
        for l in range(depth):
            for m3 in range(3):
                ada_steps(l, m3, steps if (l == 0 and m3 == 0) else ada_pending)
    n_pre = len(steps)
    for s in range(n_seq):
        load_seq_steps(s)
        for l in range(depth):
            if do_ffn:
                ffn_steps(l, 0, s)
            if do_mix:
                mix_steps(l, s)
            if do_ffn:
                ffn_steps(l, 1, s)
        final_steps(s)

    if ada_pending:
        merged = steps[:n_pre]
        npop = {}
        for st in steps[n_pre:]:
            merged.append(st)
            tag = step_tag.get(id(st[1]))
            if tag is not None and ada_pending and npop.get(tag, 0) < 36:
                npop[tag] = npop.get(tag, 0) + 1
                merged.append(ada_pending.pop(0))
        assert not ada_pending
        steps = merged

    wjobs = [(i, st[0]) for i, st in enumerate(steps) if st[0] is not None]
    jidx = {i: n for n, (i, _) in enumerate(wjobs)}
    loaded = 0

    def ensure(nj):
        nonlocal loaded
        while loaded < min(nj, len(wjobs)):
            sl = loaded % NSLOT
            wjobs[loaded][1](slots[sl], slot_res[sl], slot_ch[sl])
            loaded += 1

    for i, (wj, fn) in enumerate(steps):
        if wj is not None:
            n = jidx[i]
            ensure(n + NSLOT)
            fn(slots[n % NSLOT], slot_res[n % NSLOT])
        else:
            fn(None, None)
    stats = P.emit(final_chans=xout_ch)
    return nc, stats


_CACHE = {}


_CONSTS = {}


def make_consts():
    if _CONSTS:
        return _CONSTS
    import math
    import jax
    import jax.numpy as jnp
    theta = 10000.0
    with jax.default_device(jax.devices("cpu")[0]):
        def angles(pos, dim):
            inv = theta ** (-jnp.arange(0, dim, 2, dtype=jnp.float32) / dim)
            ang = pos.astype(jnp.float32)[:, None] * inv[None, :]
            return np.asarray(jnp.cos(ang)), np.asarray(jnp.sin(ang))
        t = jnp.arange(S)
        row_c, row_s = angles(t // 64, 32)
        col_c, col_s = angles(t % 64, 32)
        seq_c, seq_s = angles(t, 32)
        idx = np.arange(512)
        rel = jnp.asarray(128 - (idx - 127))
        nb, max_exact = 16, 8
        ret = jnp.where(rel > 0, nb, 0)
        n = jnp.abs(rel)
        large = max_exact + (jnp.log(jnp.maximum(n, 1).astype(jnp.float32) / max_exact)
                             / math.log(128 / max_exact) * (nb - max_exact)).astype(jnp.int32)
        large = jnp.minimum(large, nb - 1)
        bucket = np.asarray(ret + jnp.where(n < max_exact, n, large))
    ropeA = np.zeros((128, 2, S), np.float32)
    for p in range(128):
        d = p % 64
        cs, sn = (row_c, row_s) if d < 32 else (col_c, col_s)
        dd = d % 32
        i = dd % 16
        ropeA[p, 0] = cs[:, i]
        ropeA[p, 1] = -sn[:, i] if dd < 16 else sn[:, i]
    ropeC = np.zeros((128, 2, S), np.float32)
    ropeC[0:64, 0] = 1.0
    for p in range(64, 96):
        dd = p - 64
        i = dd % 16
        ropeC[p, 0] = seq_c[:, i]
        ropeC[p, 1] = -seq_s[:, i] if dd < 16 else seq_s[:, i]
    cf = np.zeros((128, CF_COLS), np.float32)
    for m in range(128):
        dd = m % 32
        k = m + 16 if dd < 16 else m - 16
        cf[k, m] = 1.0
    for m in range(64, 96):
        dd = m - 64
        k = m + 16 if dd < 16 else m - 16
        cf[k, 128 + m] = 1.0
    cf[0:64, 256:320] = 1.0
    cf[64:128, 320:384] = 1.0
    u = idx - 127
    valid = (u >= 0) & (u <= 256) & (idx < 511)
    for i in range(511):
        cf[bucket[i], 384 + i] = 1.0
    cf[0:6, 896:1408] = valid.astype(np.float32)[None, :]
    for m in range(128):
        cf[127 - m, 1408 + m] = 1.0
    _CONSTS.update(cf=cf, ropeA=ropeA, ropeC=ropeC, zer=np.zeros((128, KC * 64), np.float32))
    return _CONSTS


def make_in_maps(inp, n_seq, ncores):
    x = np.ascontiguousarray(inp["x"], np.float32)
    sm = pack_smalls(inp)
    shared = {"smalls": sm}
    shared.update(make_consts())
    for k in ("ada_w", "ffn1_w_gu", "ffn2_w_gu", "ffn1_w_down", "ffn2_w_down", "w_in", "c_w_q_up", "c_w_kv_up",
              "w_br_a", "w_br_b", "w_br_c", "w_out"):
        shared[k] = np.ascontiguousarray(inp[k], np.float32)
    c = np.asarray(inp["c"], np.float32)
    in_maps = []
    for i in range(ncores):
        cs = c[i * n_seq:(i + 1) * n_seq]
        ct = cs.reshape(n_seq, KC, 128).transpose(2, 1, 0)
        m = dict(shared)
        m["x"] = x[i * n_seq:(i + 1) * n_seq].reshape(n_seq * S, D)
        m["cT"] = np.ascontiguousarray(ct.reshape(128, KC * n_seq))
        in_maps.append(m)
    return in_maps


def kernel(**inp):
    B = inp["x"].shape[0]
    n_seq = B // NCORES
    if "nc" not in _CACHE:
        _CACHE["nc"] = build(n_seq=n_seq)[0]
    nc = _CACHE["nc"]
    in_maps = make_in_maps(inp, n_seq, NCORES)
    res = run_bass_kernel_spmd(nc, in_maps, core_ids=list(range(NCORES)))
    out = np.concatenate([r["out"].reshape(n_seq, S, D) for r in res.results], axis=0)
    return out.astype(np.float32)
```

```python
import numpy as np
import concourse.bass as bass
import concourse.mybir as mybir
from concourse.bass_utils import run_bass_kernel_spmd

F32 = mybir.dt.float32
BF16 = mybir.dt.bfloat16
AF = mybir.ActivationFunctionType
ALU = mybir.AluOpType

D = 1024
S = 2048
KC = 8
FF = 2816
FC = 22
TT = 512
NT = 4
L = 2
EPS = 1e-6
IN_COLS = 4768
NCORES = 8


class Res:
    __slots__ = ("name", "lw", "rd")

    def __init__(self, name):
        self.name = name
        self.lw = None
        self.rd = {}


class Chan:
    def __init__(self, name):
        self.name = name
        self.n = 0
        self.sem = None


class Prog:
    ENG = ("pe", "act", "dve", "pool", "sp")

    def __init__(self, nc, serialize=False):
        self.nc = nc
        self.ops = []
        self.serialize = serialize
        self.eng = {"pe": nc.tensor, "act": nc.scalar, "dve": nc.vector,
                    "pool": nc.gpsimd, "sp": nc.sync}
        self.chans = []

    def chan(self, name):
        c = Chan(name)
        self.chans.append(c)
        return c

    def op(self, eng, fn, reads=(), writes=(), chan=None):
        idx = len(self.ops)
        deps = set()
        rkey = ("c", id(chan)) if chan is not None else eng
        for r in reads:
            if r.lw is not None:
                deps.add((r.lw, True))
        for w in writes:
            if w.lw is not None:
                pw = self.ops[w.lw]
                if not (chan is not None and pw["chan"] is chan and pw["eng"] == eng):
                    deps.add((w.lw, False))
            for k, i in w.rd.items():
                deps.add((i, False))
        if self.serialize and idx > 0:
            deps.add((idx - 1, True))
        for r in reads:
            r.rd[rkey] = idx
        for w in writes:
            w.lw = idx
            w.rd = {}
        ordn = None
        if chan is not None:
            chan.n += 1
            ordn = chan.n
        self.ops.append(dict(eng=eng, fn=fn, deps=deps, chan=chan, ordn=ordn, sig=False))
        return idx

    def emit(self, final_chans=()):
        nc = self.nc
        ops = self.ops
        real = []
        for i, o in enumerate(ops):
            rl = []
            for (j, raw) in o["deps"]:
                p = ops[j]
                if p["chan"] is None:
                    if p["eng"] == o["eng"] and o["chan"] is None:
                        if o["eng"] == "pe":
                            continue
                    p["sig"] = True
                rl.append(j)
            real.append(rl)
        sem = {e: nc.alloc_semaphore("s_" + e) for e in ("pe", "act", "dve", "pool")}
        for c in self.chans:
            c.sem = nc.alloc_semaphore("c_" + c.name)
        cnt = {e: 0 for e in sem}
        sigval = {}
        seen = {e: {} for e in self.ENG}
        nwait = 0
        for i, o in enumerate(ops):
            e = o["eng"]
            E = self.eng[e]
            need = {}
            for j in real[i]:
                p = ops[j]
                if p["chan"] is not None:
                    key = ("c", id(p["chan"]))
                    s, v = p["chan"].sem, 16 * p["ordn"]
                else:
                    key = p["eng"]
                    s, v = sem[p["eng"]], sigval[j]
                if seen[e].get(key, 0) >= v:
                    continue
                if key not in need or need[key][1] < v:
                    need[key] = (s, v)
            for key, (s, v) in need.items():
                E.wait_ge(s, v)
                seen[e][key] = v
                nwait += 1
            ins = o["fn"](E)
            if o["chan"] is not None:
                ins.then_inc(o["chan"].sem, 16)
            elif o["sig"]:
                cnt[e] += 1
                sigval[i] = cnt[e]
                ins.then_inc(sem[e], 1)
        for c in final_chans:
            nc.sync.wait_ge(c.sem, 16 * c.n)
        self.stats = dict(n_ops=len(ops), n_wait=nwait, cnt=dict(cnt))
        return self.stats


def fm(v):
    v = np.asarray(v, np.float32)
    return np.ascontiguousarray(v.reshape(-1, 128).T)


SM = {}
_o = 0
for _l in range(L):
    for _n, _w in (("nf1", 8), ("nmix", 8), ("nf2", 8), ("adab", 72), ("aqn", 1), ("akn", 1), ("cqn", 2), ("ckvn", 1), ("sink", 6)):
        SM[(_n, _l)] = (_o, _w)
        _o += _w
SM[("fin", 0)] = (_o, 8)
_o += 8
SM[("relb", 0)] = (_o, 6)
_o += 6
CF_COLS = 1536
SM_COLS = _o


def pack_smalls(inp):
    sm = np.zeros((128, SM_COLS), np.float32)

    def put(key, arr):
        o, w = SM[key]
        assert arr.shape == (128, w), (key, arr.shape)
        sm[:, o:o + w] = arr

    for l in range(L):
        put(("nf1", l), fm(inp["norm_ffn1"][l]))
        put(("nmix", l), fm(inp["norm_mix"][l]))
        put(("nf2", l), fm(inp["norm_ffn2"][l]))
        put(("adab", l), fm(inp["ada_b"][l]))
        put(("aqn", l), np.tile(np.asarray(inp["a_q_norm"][l], np.float32), 2).reshape(128, 1))
        put(("akn", l), np.tile(np.asarray(inp["a_k_norm"][l], np.float32), 2).reshape(128, 1))
        put(("cqn", l), fm(inp["c_q_lat_norm"][l]))
        put(("ckvn", l), fm(inp["c_kv_lat_norm"][l]))
        put(("sink", l), np.broadcast_to(np.asarray(inp["b_sink"][l], np.float32)[None, :], (128, 6)))
    put(("fin", 0), fm(inp["final_norm"]))
    o, w = SM[("relb", 0)]
    sm[0:32, o:o + w] = np.asarray(inp["rel_bias"], np.float32)
    return sm


NSLOT = 3
SLOT_E = 4608


def build(n_seq=4, depth=L, do_mix=True, serialize=False, do_ffn=True, do_ada=True, ffn_parts="ngd", mixers="abc"):
    nc = bass.Bass("TRN2", target_bir_lowering=False)
    NTOK = n_seq * S
    x_d = nc.dram_tensor("x", [NTOK, D], F32, kind="ExternalInput").ap()
    ct_d = nc.dram_tensor("cT", [128, KC * n_seq], F32, kind="ExternalInput").ap()
    sm_d = nc.dram_tensor("smalls", [128, SM_COLS], F32, kind="ExternalInput").ap()
    adaw_d = nc.dram_tensor("ada_w", [L, D, 9 * D], F32, kind="ExternalInput").ap()
    wgu_d = [nc.dram_tensor("ffn%d_w_gu" % i, [L, D, 2 * FF], F32, kind="ExternalInput").ap() for i in (1, 2)]
    wdn_d = [nc.dram_tensor("ffn%d_w_down" % i, [L, FF, D], F32, kind="ExternalInput").ap() for i in (1, 2)]
    out_d = nc.dram_tensor("out", [NTOK, D], F32, kind="ExternalOutput").ap()
    win_d = nc.dram_tensor("w_in", [L, D, IN_COLS], F32, kind="ExternalInput").ap()
    cqup_d = nc.dram_tensor("c_w_q_up", [L, 256, 384], F32, kind="ExternalInput").ap()
    ckvup_d = nc.dram_tensor("c_w_kv_up", [L, 128, 512], F32, kind="ExternalInput").ap()
    wbra_d = nc.dram_tensor("w_br_a", [L, 384, D], F32, kind="ExternalInput").ap()
    wbrb_d = nc.dram_tensor("w_br_b", [L, 384, D], F32, kind="ExternalInput").ap()
    wbrc_d = nc.dram_tensor("w_br_c", [L, 256, D], F32, kind="ExternalInput").ap()
    wout_d = nc.dram_tensor("w_out", [L, D, D], F32, kind="ExternalInput").ap()
    cf_d = nc.dram_tensor("cf", [128, CF_COLS], F32, kind="ExternalInput").ap()
    zer_d = nc.dram_tensor("zer", [128, KC * 64], F32, kind="ExternalInput").ap()
    tabs_d = [nc.dram_tensor(n, [128, 2, S], F32, kind="ExternalInput").ap() for n in ("ropeA", "ropeC")]
    tb_d = nc.dram_tensor("tb_scratch", [6, 512], BF16)

    P = Prog(nc, serialize=serialize)

    xT = nc.alloc_sbuf_tensor("xT", [128, KC, S], F32)
    hT = nc.alloc_sbuf_tensor("hT", [128, KC, S], BF16)
    SCR_E = 12 * S + 16 * 192
    assert SCR_E >= FC * 2 * TT + 2 * 2 * D
    scr = nc.alloc_sbuf_tensor("scr", [128, SCR_E], BF16)
    slots = [nc.alloc_sbuf_tensor("wslot%d" % i, [128, SLOT_E], BF16) for i in range(NSLOT)]
    slot_res = [Res("wslot%d" % i) for i in range(NSLOT)]
    slot_ch = [P.chan("w%d" % i) for i in range(NSLOT)]
    xin = [scr[:, FC * 2 * TT + i * 2 * D: FC * 2 * TT + (i + 1) * 2 * D].bitcast(F32) for i in range(2)]
    xin_res = [Res("xin%d" % i) for i in range(2)]
    xin_ch = [P.chan("xi%d" % i) for i in range(2)]
    xout_ch = [P.chan("xo%d" % i) for i in range(2)]
    smalls = nc.alloc_sbuf_tensor("smalls_sb", [128, SM_COLS], F32)
    cT = nc.alloc_sbuf_tensor("cT_sb", [128, KC * n_seq], F32)
    condT = nc.alloc_sbuf_tensor("condT", [128, KC * n_seq], F32)
    MODW = 72 * n_seq
    mods = nc.alloc_sbuf_tensor("mods", [128, L * MODW], F32)
    modA = nc.alloc_sbuf_tensor("modA", [128, L * 3 * n_seq * KC], F32)
    modG = nc.alloc_sbuf_tensor("modG", [128, L * 3 * n_seq * KC], F32)
    ident = nc.alloc_sbuf_tensor("ident", [128, 128], F32)
    ones_bf = nc.alloc_sbuf_tensor("ones_bf", [128, 128], BF16)
    epsb = nc.alloc_sbuf_tensor("epsb", [128, 1], F32)
    sq = [nc.alloc_sbuf_tensor("sq%d" % i, [128, TT], BF16) for i in range(2)]
    sq_res = [Res("sq%d" % i) for i in range(2)]
    NRSTD = 1
    rstd = [nc.alloc_sbuf_tensor("rstd%d" % i, [128, TT], F32) for i in range(NRSTD)]
    rstd_res = [Res("rstd%d" % i) for i in range(NRSTD)]
    tmpf = [nc.alloc_sbuf_tensor("tmpf%d" % i, [128, TT], F32) for i in range(3)]
    tmpf_res = [Res("tmpf%d" % i) for i in range(3)]
    banks = [nc.alloc_psum_tensor("bank%d" % i, [128, TT], F32) for i in range(8)]
    bank_res = [Res("bank%d" % i) for i in range(8)]
    const_res = Res("consts")
    const_ch = P.chan("const")
    mods_res = Res("mods")
    x_res = [[Res("x%d_%d" % (i, c)) for c in range(KC)] for i in range(NT)]
    h_res = [[Res("h%d_%d" % (i, c)) for c in range(KC)] for i in range(NT)]

    rr = {"bank": 0, "abank": 0, "tmpf": 0, "sq": 0, "sqx": 0, "stgx": 0, "rstd": 0, "xin": 0}

    def nxt(kind, n):
        i = rr[kind]
        rr[kind] = (i + 1) % n
        return i

    held = set()

    def get_bank(acc=False):
        if acc:
            i = nxt("abank", 4)
        else:
            for _ in range(4):
                i = 4 + nxt("bank", 4)
                if i not in held:
                    break
            else:
                raise RuntimeError("no free PSUM bank")
        return banks[i], bank_res[i]

    def hold(bk):
        held.add(banks.index(bk))

    def release(bk):
        held.discard(banks.index(bk))

    def get_tmpf():
        i = nxt("tmpf", 3)
        return tmpf[i], tmpf_res[i]

    P.op("sp", lambda E: E.dma_start(out=smalls[:], in_=sm_d), writes=[const_res], chan=const_ch)
    P.op("sp", lambda E: E.dma_start(out=cT[:], in_=ct_d), writes=[const_res], chan=const_ch)
    P.op("dve", lambda E: E.memset(ident[:], 0.0), writes=[const_res])
    P.op("pool", lambda E: E.affine_select(out=ident[:], in_=ident[:], compare_op=ALU.not_equal, fill=1.0,
                                           base=0, pattern=[[-1, 128]], channel_multiplier=1),
         reads=[const_res], writes=[const_res])
    P.op("dve", lambda E: E.memset(ones_bf[:], 1.0), writes=[const_res])
    P.op("dve", lambda E: E.memset(epsb[:], EPS), writes=[const_res])
    P.op("act", lambda E: E.activation(out=condT[:], in_=cT[:], func=AF.Silu), reads=[const_res], writes=[const_res])

    def smcol(key, c=None):
        o, w = SM[key]
        if c is None:
            return smalls[:, o:o + w]
        return smalls[:, o + c:o + c + 1]

    steps = []

    step_tag = {}

    def add(wjob, fn, tag=None):
        step_tag[id(fn)] = tag
        steps.append((wjob, fn))

    ADA_CW = 256

    def ada_steps(l, m3, out_list):
        pbank = {}
        T0 = m3 * 12

        def wjob(t):
            def load(slot, res, ch):
                dst = slot[:, 0:2 * KC * ADA_CW].bitcast(F32).rearrange("p (k j) -> p k j", k=KC)
                src = adaw_d[l].rearrange("(k p) n -> p k n", p=128)[:, :, t * ADA_CW:(t + 1) * ADA_CW]
                P.op("pool", lambda E: E.dma_start(out=dst, in_=src), writes=[res], chan=ch)
            return load

        def comp(t):
            def fn(slot, res):
                if t == T0:
                    pbank["b"] = get_bank(acc=True)
                bk, bres = pbank["b"]
                W = slot[:, 0:2 * KC * ADA_CW].bitcast(F32).rearrange("p (k j) -> p k j", k=KC)
                for jj in range(ADA_CW // 128):
                    j = t * (ADA_CW // 128) + jj
                    for k in range(KC):
                        P.op("pe", lambda E, j=j, jj=jj, k=k: E.matmul(
                            bk[:, j * n_seq:(j + 1) * n_seq], lhsT=W[:, k, jj * 128:(jj + 1) * 128],
                            rhs=condT[:, k * n_seq:(k + 1) * n_seq], start=(k == 0), stop=(k == KC - 1)),
                            reads=[res, const_res], writes=[bres])
                if t == T0 + 11:
                    o, w = SM[("adab", l)]
                    j0, j1 = m3 * 24, (m3 + 1) * 24
                    src_b = smalls[:, o + j0:o + j1].unsqueeze(2).to_broadcast([128, 24, n_seq])
                    dstm = mods[:, l * MODW:(l + 1) * MODW].rearrange("p (j s) -> p j s", s=n_seq)[:, j0:j1, :]
                    srcp = bk[:, 0:MODW].rearrange("p (j s) -> p j s", s=n_seq)[:, j0:j1, :]
                    P.op("dve", lambda E: E.tensor_tensor(out=dstm, in0=srcp, in1=src_b, op=ALU.add),
                         reads=[bres, const_res], writes=[mods_res])
                    for m in (m3,):
                        gk = (("nf1", l), ("nmix", l), ("nf2", l))[m]
                        go, _ = SM[gk]
                        for s in range(n_seq):
                            base = ((l * 3 + m) * n_seq + s) * KC
                            sc = mods[:, l * MODW:(l + 1) * MODW].rearrange("p (j s) -> p j s", s=n_seq)[:, (3 * m + 1) * 8:(3 * m + 2) * 8, s]
                            gt = mods[:, l * MODW:(l + 1) * MODW].rearrange("p (j s) -> p j s", s=n_seq)[:, (3 * m + 2) * 8:(3 * m + 3) * 8, s]
                            P.op("dve", lambda E, sc=sc, base=base, go=go: E.scalar_tensor_tensor(
                                out=modA[:, base:base + KC], in0=sc, scalar=1.0, in1=smalls[:, go:go + KC],
                                op0=ALU.add, op1=ALU.mult), reads=[mods_res, const_res], writes=[mods_res])
                            rw = 1.0 if m == 1 else 0.5
                            P.op("dve", lambda E, gt=gt, base=base, rw=rw: E.tensor_scalar(
                                out=modG[:, base:base + KC], in0=gt, scalar1=rw, scalar2=None, op0=ALU.mult),
                                reads=[mods_res], writes=[mods_res])
            return fn

        for t in range(T0, T0 + 12):
            out_list.append((wjob(t), comp(t)))

    def mA(l, m, s, c):
        b = ((l * 3 + m) * n_seq + s) * KC + c
        return modA[:, b:b + 1]

    def mG(l, m, s, c):
        b = ((l * 3 + m) * n_seq + s) * KC + c
        return modG[:, b:b + 1]

    def mB(l, m, s, c):
        j = (3 * m) * 8 + c
        b = l * MODW + j * n_seq + s
        return mods[:, b:b + 1]

    def load_seq_steps(s):
        def fn(slot, res):
            for i in range(S // 128):
                xi = nxt("xin", 2)
                src = x_d[s * S + i * 128: s * S + (i + 1) * 128, :]
                P.op("sp", lambda E, xi=xi, src=src: E.dma_start(out=xin[xi], in_=src),
                     writes=[xin_res[xi]], chan=xin_ch[xi])
                for half in range(2):
                    bk, bres = get_bank()
                    for cc in range(4):
                        c = half * 4 + cc
                        P.op("pe", lambda E, bk=bk, cc=cc, c=c, xi=xi: E.transpose(
                            out=bk[:, cc * 128:(cc + 1) * 128], in_=xin[xi][:, c * 128:(c + 1) * 128], identity=ident[:]),
                            reads=[xin_res[xi], const_res], writes=[bres])
                    dst = xT[:, half * 4:(half + 1) * 4, i * 128:(i + 1) * 128]
                    srcp = bk[:, :].rearrange("p (c t) -> p c t", c=4)
                    eng = "dve" if half == 0 else "act"
                    if eng == "dve":
                        P.op("dve", lambda E, dst=dst, srcp=srcp: E.tensor_copy(out=dst, in_=srcp),
                             reads=[bres], writes=x_res[i // 4][half * 4:(half + 1) * 4])
                    else:
                        P.op("act", lambda E, dst=dst, srcp=srcp: E.activation(out=dst, in_=srcp, func=AF.Copy),
                             reads=[bres], writes=x_res[i // 4][half * 4:(half + 1) * 4])
                if subs and i % 4 == 3 and i // 4 >= 1:
                    emit_norm_tile(subs[0][0], subs[0][1], s, i // 4 - 1)
            if subs:
                emit_norm_tile(subs[0][0], subs[0][1], s, NT - 1)
        add(None, fn)

    def emit_rstd(tt):
        bk, bres = get_bank()
        pool6 = [(sq[0], sq_res[0]), (sq[1], sq_res[1])] + ([(pT[i], pT_res[i]) for i in range(NPT)] if do_mix else [])
        for c in range(KC):
            qb, qbres = pool6[nxt("sqx", len(pool6))]
            P.op("act", lambda E, qb=qb, c=c: E.activation(out=qb[:], in_=xT[:, c, tt * TT:(tt + 1) * TT], func=AF.Square),
                 reads=[x_res[tt][c]], writes=[qbres])
            P.op("pe", lambda E, qb=qb, c=c, bk=bk: E.matmul(bk[:], lhsT=ones_bf[:], rhs=qb[:], start=(c == 0), stop=(c == KC - 1)),
                 reads=[qbres, const_res], writes=[bres])
        ri = nxt("rstd", NRSTD)
        P.op("act", lambda E, ri=ri, bk=bk: E.activation(out=rstd[ri][:], in_=bk[:], func=AF.Ln, bias=epsb[:], scale=1.0 / D),
             reads=[bres, const_res], writes=[rstd_res[ri]])
        P.op("act", lambda E, ri=ri: E.activation(out=rstd[ri][:], in_=rstd[ri][:], func=AF.Exp, scale=-0.5),
             reads=[rstd_res[ri]], writes=[rstd_res[ri]])
        return rstd[ri], rstd_res[ri]

    def emit_norm_tile(l, m, s, tt):
        r, rres = emit_rstd(tt)
        for c in range(KC):
            tf, tres = get_tmpf()
            P.op("dve", lambda E, tf=tf, c=c: E.tensor_tensor(
                out=tf[:], in0=xT[:, c, tt * TT:(tt + 1) * TT], in1=r[:], op=ALU.mult),
                reads=[x_res[tt][c], rres], writes=[tres])
            P.op("act", lambda E, tf=tf, c=c: E.activation(
                out=hT[:, c, tt * TT:(tt + 1) * TT], in_=tf[:], func=AF.Identity,
                scale=mA(l, m, s, c), bias=mB(l, m, s, c)),
                reads=[tres, mods_res], writes=[h_res[tt][c]])

    subs = []
    for _l in range(depth):
        if do_ffn:
            subs.append((_l, 0))
        if do_mix:
            subs.append((_l, 1))
        if do_ffn:
            subs.append((_l, 2))
    nxt_sub = {subs[i]: (subs[i + 1] if i + 1 < len(subs) else None) for i in range(len(subs))}

    def add_next_norm(l, m, s, tiles):
        ns = nxt_sub[(l, m)]
        if ns is None:
            return

        def fn(slot, res):
            for tt in tiles:
                emit_norm_tile(ns[0], ns[1], s, tt)
        add(None, fn)

    GU_CW = 256
    NGU = FF // GU_CW
    act_res = [[Res("act%d_%d" % (f, t)) for t in range(2)] for f in range(FC)]

    def act_view(f, t2):
        return scr[:, (f * 2 + t2) * TT:(f * 2 + t2 + 1) * TT]

    def ffn_steps(l, which, s):
        m = 0 if which == 0 else 2
        wgu = wgu_d[which][l].rearrange("(k p) n -> p k n", p=128)
        wdn = wdn_d[which][l].rearrange("(f p) n -> p f n", p=128)

        def gu_job(wt):
            def load(slot, res, ch):
                for g in range(2):
                    dst = slot[:, 0:KC * 2 * GU_CW].rearrange("p (k g j) -> p k g j", k=KC, g=2)[:, :, g, :]
                    src = wgu[:, :, g * FF + wt * GU_CW: g * FF + (wt + 1) * GU_CW]
                    P.op("pool", lambda E, dst=dst, src=src: E.dma_start(out=dst, in_=src), writes=[res], chan=ch)
            return load

        def gu_comp(wt, tg):
            def fn(slot, res):
                W = slot[:, 0:KC * 2 * GU_CW].rearrange("p (k g j) -> p k g j", k=KC, g=2)
                for t2 in range(2):
                    tt = tg * 2 + t2
                    for half in range(GU_CW // 128):
                        f = wt * (GU_CW // 128) + half
                        pg, pgres = get_bank()
                        pu, pures = get_bank()
                        for g, (bk, bres) in enumerate(((pg, pgres), (pu, pures))):
                            for k in range(KC):
                                P.op("pe", lambda E, bk=bk, g=g, k=k, half=half, tt=tt: E.matmul(
                                    bk[:], lhsT=W[:, k, g, half * 128:(half + 1) * 128],
                                    rhs=hT[:, k, tt * TT:(tt + 1) * TT], start=(k == 0), stop=(k == KC - 1)),
                                    reads=[res, h_res[tt][k]], writes=[bres])
                        tf, tres = get_tmpf()
                        P.op("act", lambda E, tf=tf, pg=pg: E.activation(out=tf[:], in_=pg[:], func=AF.Silu),
                             reads=[pgres], writes=[tres])
                        P.op("dve", lambda E, tf=tf, pu=pu, f=f, t2=t2: E.tensor_tensor(
                            out=act_view(f, t2), in0=tf[:], in1=pu[:], op=ALU.mult),
                            reads=[tres, pures], writes=[act_res[f][t2]])
            return fn

        def dn_job(dc):
            def load(slot, res, ch):
                for a in range(2):
                    f0, f1 = a * 11, (a + 1) * 11
                    dst = slot[:, 0:FC * 128].rearrange("p (f j) -> p f j", f=FC)[:, f0:f1, :]
                    src = wdn[:, f0:f1, dc * 128:(dc + 1) * 128]
                    P.op("pool", lambda E, dst=dst, src=src: E.dma_start(out=dst, in_=src), writes=[res], chan=ch)
            return load

        def dn_comp(dc, tg):
            def fn(slot, res):
                W = slot[:, 0:FC * 128].rearrange("p (f j) -> p f j", f=FC)
                for t2 in range(2):
                    tt = tg * 2 + t2
                    bk, bres = get_bank()
                    for f in range(FC):
                        P.op("pe", lambda E, bk=bk, f=f, t2=t2: E.matmul(
                            bk[:], lhsT=W[:, f, :], rhs=act_view(f, t2), start=(f == 0), stop=(f == FC - 1)),
                            reads=[res, act_res[f][t2]], writes=[bres])
                    xs = xT[:, dc, tt * TT:(tt + 1) * TT]
                    P.op("dve", lambda E, bk=bk, xs=xs: E.scalar_tensor_tensor(
                        out=xs, in0=bk[:], scalar=mG(l, m, s, dc), in1=xs, op0=ALU.mult, op1=ALU.add),
                        reads=[bres, x_res[tt][dc], mods_res], writes=[x_res[tt][dc]])
            return fn

        for tg in range(2):
            for wt in range(NGU):
                add(gu_job(wt), gu_comp(wt, tg), tag=("ffn", l, which, s))
                if tg == 1 and wt in (1, 3):
                    add_next_norm(l, m, s, [wt // 2])
            for dc in range(KC):
                add(dn_job(dc), dn_comp(dc, tg), tag=("ffn", l, which, s))
        add_next_norm(l, m, s, [2, 3])

    def final_steps(s):
        def fn(slot, res):
            fo, _ = SM[("fin", 0)]
            for tt in range(NT):
                r, rres = emit_rstd(tt)
                for c in range(KC):
                    P.op("dve", lambda E, c=c, r=r, tt=tt: E.scalar_tensor_tensor(
                        out=xT[:, c, tt * TT:(tt + 1) * TT], in0=xT[:, c, tt * TT:(tt + 1) * TT],
                        scalar=smalls[:, fo + c:fo + c + 1], in1=r[:], op0=ALU.mult, op1=ALU.mult),
                        reads=[x_res[tt][c], rres, const_res], writes=[x_res[tt][c]])
                for i4 in range(4):
                    i = tt * 4 + i4
                    xi = nxt("xin", 2)
                    for half in range(2):
                        bk, bres = get_bank()
                        for cc in range(4):
                            c = half * 4 + cc
                            P.op("pe", lambda E, bk=bk, cc=cc, c=c, i=i: E.transpose(
                                out=bk[:, cc * 128:(cc + 1) * 128], in_=xT[:, c, i * 128:(i + 1) * 128], identity=ident[:]),
                                reads=[x_res[tt][c], const_res], writes=[bres])
                        dst = xin[xi][:, half * 512:(half + 1) * 512]
                        if half == 0:
                            P.op("dve", lambda E, dst=dst, bk=bk: E.tensor_copy(out=dst, in_=bk[:]),
                                 reads=[bres], writes=[xin_res[xi]])
                        else:
                            P.op("act", lambda E, dst=dst, bk=bk: E.activation(out=dst, in_=bk[:], func=AF.Copy),
                                 reads=[bres], writes=[xin_res[xi]])
                    dsto = out_d[s * S + i * 128: s * S + (i + 1) * 128, :]
                    P.op("sp", lambda E, xi=xi, dsto=dsto: E.dma_start(out=dsto, in_=xin[xi]),
                         reads=[xin_res[xi]], chan=xout_ch[xi])
        add(None, fn)

    HD = 64
    A_SCALE = HD ** -0.5
    C_SCALE = 96 ** -0.5
    winl = [win_d[l].rearrange("(k p) n -> p k n", p=128) for l in range(L)]
    O_OA, O_OB, O_OC = 0, 3 * S, 6 * S
    O_Q = 8 * S
    O_K = O_Q + 3 * S
    O_V = O_K + S
    O_MG = O_Q
    assert O_V + 16 * 192 <= SCR_E
    oa_v = scr[:, O_OA:O_OA + 3 * S].rearrange("p (c t) -> p c t", c=3)
    ob_v = scr[:, O_OB:O_OB + 3 * S].rearrange("p (c t) -> p c t", c=3)
    oc_v = scr[:, O_OC:O_OC + 2 * S].rearrange("p (c t) -> p c t", c=2)
    q_v = scr[:, O_Q:O_Q + 3 * S].rearrange("p (c t) -> p c t", c=3)
    k_v = scr[:, O_K:O_K + S]
    qc_v = scr[:, O_Q:O_Q + 2 * S].rearrange("p (c t) -> p c t", c=2)
    kc_v = scr[:, O_Q + 2 * S:O_Q + 4 * S].rearrange("p (c t) -> p c t", c=2)
    va_v = scr[:, O_V:O_V + 16 * 192].rearrange("p (i c) -> p i c", c=192)
    mg_v = scr[:, O_MG:O_MG + KC * TT].rearrange("p (c t) -> p c t", c=KC)
    q_res = [[Res("q%d_%d" % (c, t)) for t in range(NT)] for c in range(3)]
    k_res = [[Res("k%d_%d" % (c, t)) for t in range(NT)] for c in range(2)]
    v_res = Res("vaug")
    o_res = {m: [[Res("o%s%d_%d" % (m, c, t)) for t in range(NT)] for c in range(3)] for m in "abc"}
    mg_res = [Res("mg%d" % c) for c in range(KC)]
    NPT = 4
    pT = [nc.alloc_sbuf_tensor("pT%d" % i, [128, TT], BF16) for i in range(NPT)]
    pT_res = [Res("pT%d" % i) for i in range(NPT)]
    NPF = 2
    pF = [nc.alloc_sbuf_tensor("pF%d" % i, [128, 384], F32) for i in range(NPF)]
    pF_res = [Res("pF%d" % i) for i in range(NPF)]
    stg = [nc.alloc_sbuf_tensor("stg%d" % i, [128, TT], BF16) for i in range(2)]
    stg_res = [Res("stg%d" % i) for i in range(2)]
    tabs = [scr[:, O_OC + S:O_OC + 2 * S].bitcast(F32).rearrange("p (a t) -> p a t", a=2)]
    tab_res = [Res("tab0")]
    tab_ch = [P.chan("tab0")]
    rr.update({"pT": 0, "pF": 0, "stg": 0, "tab": 0})
    cbf = nc.alloc_sbuf_tensor("cbf", [128, 5 * 128], BF16)
    RA, R96, BONES, JREV, ZER = (cbf[:, i * 128:(i + 1) * 128] for i in range(5))
    EB = nc.alloc_sbuf_tensor("EB", [128, 6 * 384], BF16)
    esink = nc.alloc_sbuf_tensor("esink", [128, L * 6], F32)
    eb_res = Res("EB")

    def mix_consts():
        def fn(slot, res):
            cf = scr[:, 0:2 * CF_COLS].bitcast(F32)
            P.op("sp", lambda E: E.dma_start(out=cf, in_=cf_d), writes=[const_res], chan=const_ch)
            P.op("dve", lambda E: E.tensor_copy(out=cbf[:, 0:384], in_=cf[:, 0:384]), reads=[const_res], writes=[const_res])
            P.op("dve", lambda E: E.tensor_copy(out=cbf[:, 384:512], in_=cf[:, 1408:1536]), reads=[const_res], writes=[const_res])
            P.op("dve", lambda E: E.memset(cbf[:, 512:640], 0.0), writes=[const_res])
            for l in range(L):
                o, _ = SM[("sink", l)]
                P.op("act", lambda E, o=o, l=l: E.activation(out=esink[:, l * 6:(l + 1) * 6], in_=smalls[:, o:o + 6], func=AF.Exp),
                     reads=[const_res], writes=[const_res])
            bk, bres = get_bank()
            ro, _ = SM[("relb", 0)]
            P.op("pe", lambda E: E.matmul(bk[0:6, 0:512], lhsT=smalls[0:32, ro:ro + 6], rhs=cf[0:32, 384:896], start=True, stop=True),
                 reads=[const_res], writes=[bres])
            tb = tmpf[0][0:6, :]
            tbb = stg[0][0:6, :]
            P.op("act", lambda E: E.activation(out=tb, in_=bk[0:6, 0:512], func=AF.Exp), reads=[bres], writes=[eb_res, tmpf_res[0]])
            P.op("dve", lambda E: E.tensor_tensor(out=tbb, in0=tb, in1=cf[0:6, 896:1408], op=ALU.mult),
                 reads=[eb_res, const_res, tmpf_res[0]], writes=[eb_res, stg_res[0]])
            ebc = P.chan("ebc")
            P.op("sp", lambda E: E.dma_start(out=tb_d.ap(), in_=tbb), reads=[eb_res, stg_res[0]], writes=[eb_res], chan=ebc)
            for h in range(6):
                src = bass.AP(tensor=tb_d, offset=h * 512, ap=[[1, 128], [1, 384]])
                P.op("sp", lambda E, h=h, src=src: E.dma_start(out=stg[1][:, 0:384], in_=src),
                     reads=[eb_res], writes=[stg_res[1]], chan=ebc)
                b5, b5res = get_bank()
                P.op("pe", lambda E, b5=b5: E.matmul(b5[:, 0:384], lhsT=JREV, rhs=stg[1][:, 0:384], start=True, stop=True),
                     reads=[stg_res[1], const_res], writes=[b5res])
                P.op("act", lambda E, b5=b5, h=h: E.activation(out=EB[:, h * 384:(h + 1) * 384], in_=b5[:, 0:384], func=AF.Copy),
                     reads=[b5res], writes=[eb_res])
        add(None, fn)

    def load_tab(which, tt):
        ti = 0
        src = tabs_d[which][:, :, tt * TT:(tt + 1) * TT]
        P.op("sp", lambda E, ti=ti, src=src: E.dma_start(out=tabs[ti], in_=src), writes=[tab_res[ti]] + o_res["c"][1] + [act_res[14][0], act_res[14][1], act_res[15][0], act_res[15][1]], chan=tab_ch[ti])
        return tabs[ti], tab_res[ti]

    def finish_qk(ps, pres, nr, dst, dres, norm=None, rope=None, defer=False, spool=None):
        if spool is None:
            spool = [(stg[0], stg_res[0]), (stg[1], stg_res[1])]
        rr["stgx"] = (rr["stgx"] + 1) % len(spool)
        st, sres = spool[rr["stgx"]]
        tgt = st[0:nr, :] if rope is not None else dst
        tres = sres if rope is not None else dres
        if norm is not None:
            ones_l, nd, gain = norm
            qi = nxt("sq", 2)
            P.op("act", lambda E: E.activation(out=sq[qi][0:nr, :], in_=ps[0:nr, :], func=AF.Square), reads=[pres], writes=[sq_res[qi]])
            b2, b2res = get_bank()
            P.op("pe", lambda E: E.matmul(b2[0:nr, :], lhsT=ones_l, rhs=sq[qi][0:nr, :], start=True, stop=True),
                 reads=[sq_res[qi], const_res], writes=[b2res])
            ri = nxt("rstd", NRSTD)
            P.op("act", lambda E: E.activation(out=rstd[ri][0:nr, :], in_=b2[0:nr, :], func=AF.Ln, bias=epsb[0:nr, :], scale=1.0 / nd),
                 reads=[b2res, const_res], writes=[rstd_res[ri]])
            P.op("act", lambda E: E.activation(out=rstd[ri][0:nr, :], in_=rstd[ri][0:nr, :], func=AF.Exp, scale=-0.5),
                 reads=[rstd_res[ri]], writes=[rstd_res[ri]])
            P.op("dve", lambda E: E.scalar_tensor_tensor(out=tgt, in0=ps[0:nr, :], scalar=gain, in1=rstd[ri][0:nr, :],
                                                         op0=ALU.mult, op1=ALU.mult),
                 reads=[pres, rstd_res[ri], const_res], writes=[tres])
        else:
            P.op("act", lambda E: E.activation(out=tgt, in_=ps[0:nr, :], func=AF.Copy), reads=[pres], writes=[tres])
        if rope is None:
            return None

        def part2():
            Rl, tab, tabres = rope
            b3, b3res = get_bank()
            P.op("pe", lambda E: E.matmul(b3[0:nr, :], lhsT=Rl, rhs=st[0:nr, :], start=True, stop=True),
                 reads=[sres, const_res], writes=[b3res])
            t1, t1res = get_tmpf()
            t2, t2res = get_tmpf()
            P.op("dve", lambda E: E.tensor_tensor(out=t1[0:nr, :], in0=st[0:nr, :], in1=tab[0:nr, 0, :], op=ALU.mult),
                 reads=[sres, tabres], writes=[t1res])
            P.op("dve", lambda E: E.tensor_tensor(out=t2[0:nr, :], in0=b3[0:nr, :], in1=tab[0:nr, 1, :], op=ALU.mult),
                 reads=[b3res, tabres], writes=[t2res])
            P.op("dve", lambda E: E.tensor_tensor(out=dst, in0=t1[0:nr, :], in1=t2[0:nr, :], op=ALU.add),
                 reads=[t1res, t2res], writes=[dres])
        if defer:
            return part2
        part2()
        return None

    def proj(W, wres, ncolchunk, tt, M=128, acc=False):
        bk, bres = get_bank(acc)
        for k in range(KC):
            P.op("pe", lambda E, k=k: E.matmul(bk[0:M, :], lhsT=W[:, k, ncolchunk], rhs=hT[:, k, tt * TT:(tt + 1) * TT],
                                               start=(k == 0), stop=(k == KC - 1)),
                 reads=[wres, h_res[tt][k]], writes=[bres])
        return bk, bres

    def proj_v(W, wres, cols0, ncols, i):
        bk, bres = get_bank()
        for k in range(KC):
            P.op("pe", lambda E, k=k: E.matmul(bk[:, 0:ncols], lhsT=hT[:, k, i * 128:(i + 1) * 128], rhs=W[:, k, cols0:cols0 + ncols],
                                               start=(k == 0), stop=(k == KC - 1)),
                 reads=[wres, h_res[i // 4][k]], writes=[bres])
        return bk, bres

    LA = 3

    def attn_dense(qsel, ksel, K, groups, scale, omap):
        its = [(g, tt, kc) for g in range(len(groups)) for tt in range(NT) for kc in range(S // 128)]
        obs = {}
        pend = []
        la = 1 if max(len(g) for g in groups) > 1 else 2

        def emit_S(g, tt, kc):
            pis = []
            sbs = []
            for (h, vsel, half) in groups[g]:
                if kc == 0:
                    obs[(h, tt)] = get_bank(acc=True)
                qa, qres = qsel(h, tt)
                ka, kres = ksel(h, kc)
                sb, sbres = get_bank()
                P.op("pe", lambda E, sb=sb, ka=ka, qa=qa: E.matmul(sb[:], lhsT=ka, rhs=qa, start=True, stop=True), reads=[kres, qres], writes=[sbres])
                sbs.append((sb, sbres))
            for (sb, sbres) in sbs:
                pi = nxt("pT", NPT)
                P.op("act", lambda E, sb=sb, pi=pi: E.activation(out=pT[pi][:], in_=sb[:], func=AF.Exp, scale=scale), reads=[sbres], writes=[pT_res[pi]])
                pis.append(pi)
            return pis

        def emit_PV(g, tt, kc, pis):
            for (h, vsel, half), pi in zip(groups[g], pis):
                ob, obres = obs[(h, tt)]
                va = vsel(kc)
                P.op("pe", lambda E, ob=ob, va=va, pi=pi: E.matmul(ob[:], lhsT=va, rhs=pT[pi][:], start=(kc == 0), stop=(kc == S // 128 - 1)),
                     reads=[v_res, pT_res[pi]], writes=[obres])
            if kc == S // 128 - 1:
                for (h, vsel, half) in groups[g]:
                    ob, obres = obs[(h, tt)]
                    emit_onorm(ob, obres, half, omap(h, tt), None)

        for (g, tt, kc) in its:
            pend.append((g, tt, kc, emit_S(g, tt, kc)))
            if len(pend) > la:
                emit_PV(*pend.pop(0))
        while pend:
            emit_PV(*pend.pop(0))

    def emit_onorm(ob, obres, half, out, sink_ap, ncols=TT):
        oap, ores = out
        olo, dlo = (0, 64) if half == 0 else (64, 0)
        tf, tres = get_tmpf()
        if sink_ap is not None:
            sap = sink_ap(dlo)
            P.op("act", lambda E: E.activation(out=tf[dlo:dlo + 64, 0:ncols], in_=ob[dlo:dlo + 64, 0:ncols], func=AF.Ln, bias=sap, scale=1.0),
                 reads=[obres, const_res], writes=[tres])
            P.op("act", lambda E: E.activation(out=tf[dlo:dlo + 64, 0:ncols], in_=tf[dlo:dlo + 64, 0:ncols], func=AF.Exp, scale=-1.0),
                 reads=[tres], writes=[tres])
        else:
            P.op("dve", lambda E: E.reciprocal(out=tf[dlo:dlo + 64, 0:ncols], in_=ob[dlo:dlo + 64, 0:ncols]), reads=[obres], writes=[tres])
        P.op("dve", lambda E: E.tensor_tensor(out=oap, in0=ob[olo:olo + 64, 0:ncols], in1=tf[dlo:dlo + 64, 0:ncols], op=ALU.mult),
             reads=[obres, tres], writes=[ores])

    def vaug_ones():
        P.op("dve", lambda E: E.memset(va_v[:, :, 64:128], 1.0), writes=[v_res])

    def vsel_ab(half):
        return (lambda kc: va_v[:, kc, 0:128]) if half == 0 else (lambda kc: va_v[:, kc, 64:192])

    def mixer_ab_steps(l, s, which):
        base = 0 if which == "a" else 640
        ov = oa_v if which == "a" else ob_v

        def job1(slot, res, ch):
            W = slot[:, 0:KC * 512].rearrange("p (k j) -> p k j", k=KC)
            for j in range(3):
                for half in range(2):
                    h = j + 3 * half
                    src = winl[l][:, :, base + h * 64: base + (h + 1) * 64]
                    dst = W[:, :, j * 128 + half * 64: j * 128 + (half + 1) * 64]
                    P.op("pool", lambda E, dst=dst, src=src: E.dma_start(out=dst, in_=src), writes=[res], chan=ch)
            src = winl[l][:, :, base + 384: base + 512]
            P.op("pool", lambda E, src=src: E.dma_start(out=W[:, :, 384:512], in_=src), writes=[res], chan=ch)

        def comp1(slot, res):
            W = slot[:, 0:KC * 512].rearrange("p (k j) -> p k j", k=KC)
            if which == "a":
                go, _ = SM[("aqn", l)]
                ko, _ = SM[("akn", l)]
            items = [(tt, c) for tt in range(NT) for c in range(4)]
            pj = {}
            tabst = {}

            def do_proj(idx):
                tt, c = items[idx]
                pj[idx] = proj(W, res, slice(c * 128, (c + 1) * 128), tt)
                hold(pj[idx][0])

            do_proj(0)
            pend2 = []
            for idx, (tt, c) in enumerate(items):
                if idx + 1 < len(items):
                    do_proj(idx + 1)
                bk, bres = pj.pop(idx)
                if c < 3:
                    dst, dres = q_v[:, c, tt * TT:(tt + 1) * TT], q_res[c][tt]
                else:
                    dst, dres = k_v[:, tt * TT:(tt + 1) * TT], k_res[0][tt]
                if which == "a":
                    if c == 0:
                        while pend2:
                            pend2.pop(0)()
                        tabst["t"] = load_tab(0, tt)
                    tab, tabres = tabst["t"]
                    g = smalls[:, go:go + 1] if c < 3 else smalls[:, ko:ko + 1]
                    p2 = finish_qk(bk, bres, 128, dst, dres, norm=(BONES, HD, g), rope=(RA, tab, tabres), defer=True,
                                   spool=[(stg[0], stg_res[0]), (stg[1], stg_res[1])] + [(pT[i], pT_res[i]) for i in range(NPT)])
                    release(bk)
                    pend2.append(p2)
                    if len(pend2) > 2:
                        pend2.pop(0)()
                else:
                    finish_qk(bk, bres, 128, dst, dres)
                    release(bk)
            while pend2:
                pend2.pop(0)()

        def job2(slot, res, ch):
            W = slot[:, 0:KC * 128].rearrange("p (k j) -> p k j", k=KC)
            src = winl[l][:, :, base + 512: base + 640]
            P.op("pool", lambda E, src=src: E.dma_start(out=W, in_=src), writes=[res], chan=ch)

        def comp2(slot, res):
            W = slot[:, 0:KC * 128].rearrange("p (k j) -> p k j", k=KC)
            vaug_ones()
            for i in range(S // 128):
                bk, bres = proj_v(W, res, 0, 128, i)
                P.op("act", lambda E, bk=bk, i=i: E.activation(out=va_v[:, i, 0:64], in_=bk[:, 0:64], func=AF.Copy), reads=[bres], writes=[v_res])
                P.op("dve", lambda E, bk=bk, i=i: E.tensor_copy(out=va_v[:, i, 128:192], in_=bk[:, 64:128]), reads=[bres], writes=[v_res])

        def attn(slot, res):
            if which == "a":
                def qsel(h, tt):
                    j, half = h % 3, h // 3
                    return q_v[half * 64:(half + 1) * 64, j, tt * TT:(tt + 1) * TT], q_res[j][tt]

                def ksel(h, kc):
                    half = h // 3
                    return k_v[half * 64:(half + 1) * 64, kc * 128:(kc + 1) * 128], k_res[0][kc // 4]

                def omap(h, tt):
                    j, half = h % 3, h // 3
                    return ov[half * 64:(half + 1) * 64, j, tt * TT:(tt + 1) * TT], o_res["a"][j][tt]
                attn_dense(qsel, ksel, 64, [[(j, vsel_ab(0), 0), (j + 3, vsel_ab(1), 1)] for j in range(3)], A_SCALE, omap)
            else:
                attn_window(l)
        add(job1, comp1)
        add(job2, comp2)
        add(None, attn)

    def attn_window(l):
        its = []
        for h in range(6):
            for tt in range(NT):
                kcs = [kc for kc in range(4 * tt - 1, 4 * tt + 5) if 0 <= kc < S // 128]
                for n_i, kc in enumerate(kcs):
                    its.append((h, tt, kc, n_i == 0, n_i == len(kcs) - 1))
        obs = {}
        pend = []

        def emit_S(h, tt, kc, first, last):
            j, half = h % 3, h // 3
            if first:
                ob, obres = get_bank(acc=True)
                obs[(h, tt)] = (ob, obres)
                P.op("pe", lambda E: E.matmul(ob[:], lhsT=ZER, rhs=hT[:, 0, 0:TT], start=True, stop=False),
                     reads=[const_res, h_res[0][0]], writes=[obres])
            qb0 = max(kc - 1, 4 * tt)
            qb1 = min(kc + 1, 4 * tt + 3)
            ncol = (qb1 - qb0 + 1) * 128
            q0 = qb0 * 128
            e0 = (qb0 - (kc - 1)) * 128
            c0 = q0 - tt * TT
            sb, sbres = get_bank()
            P.op("pe", lambda E: E.matmul(
                sb[:, 0:ncol], lhsT=k_v[half * 64:(half + 1) * 64, kc * 128:(kc + 1) * 128],
                rhs=q_v[half * 64:(half + 1) * 64, j, q0:q0 + ncol], start=True, stop=True),
                reads=[k_res[0][kc // 4], q_res[j][tt]], writes=[sbres])
            fi = nxt("pF", NPF)
            P.op("act", lambda E: E.activation(out=pF[fi][:, 0:ncol], in_=sb[:, 0:ncol], func=AF.Exp, scale=A_SCALE),
                 reads=[sbres], writes=[pF_res[fi]])
            pi = nxt("pT", NPT)
            P.op("pool" if (kc % 2 == 0) else "dve", lambda E: E.tensor_tensor(
                out=pT[pi][:, 0:ncol], in0=pF[fi][:, 0:ncol], in1=EB[:, h * 384 + e0:h * 384 + e0 + ncol], op=ALU.mult),
                reads=[pF_res[fi], eb_res], writes=[pT_res[pi]])
            return (pi, c0, ncol)

        def emit_PV(h, tt, kc, first, last, info):
            pi, c0, ncol = info
            j, half = h % 3, h // 3
            ob, obres = obs[(h, tt)]
            va = vsel_ab(half)(kc)
            P.op("pe", lambda E: E.matmul(ob[:, c0:c0 + ncol], lhsT=va, rhs=pT[pi][:, 0:ncol], start=False, stop=last),
                 reads=[v_res, pT_res[pi]], writes=[obres])
            if last:
                sk = (lambda dlo: esink[dlo:dlo + 64, l * 6 + h:l * 6 + h + 1])
                emit_onorm(ob, obres, half, (ob_v[half * 64:(half + 1) * 64, j, tt * TT:(tt + 1) * TT], o_res["b"][j][tt]), sk)

        for it in its:
            pend.append(it + (emit_S(*it),))
            if len(pend) > LA:
                emit_PV(*pend.pop(0))
        while pend:
            emit_PV(*pend.pop(0))

    NW = 256 + 128 + 96
    C_WQ = KC * NW
    C_WKN = C_WQ + 384
    C_WV = C_WKN + 192
    assert C_WV + 128 <= SLOT_E

    def mixer_c_steps(l, s, p):
        def job(slot, res, ch):
            W = slot[:, 0:KC * NW].rearrange("p (k j) -> p k j", k=KC)
            P.op("pool", lambda E: E.dma_start(out=W[:, :, 384:448], in_=zer_d.rearrange("p (k j) -> p k j", k=KC)), writes=[res], chan=ch)
            src = winl[l][:, :, 1280:1664]
            P.op("pool", lambda E: E.dma_start(out=W[:, :, 0:384], in_=src), writes=[res], chan=ch)
            src2 = winl[l][:, :, 1664:1696]
            P.op("pool", lambda E: E.dma_start(out=W[:, :, 448:480], in_=src2), writes=[res], chan=ch)
            wq = cqup_d[l].rearrange("(k p) n -> p k n", p=128)[:, :, p * 192:(p + 1) * 192]
            P.op("pool", lambda E: E.dma_start(out=slot[:, C_WQ:C_WQ + 384].rearrange("p (k j) -> p k j", k=2), in_=wq), writes=[res], chan=ch)
            for hh in range(2):
                hd = 2 * p + hh
                P.op("pool", lambda E, hh=hh, hd=hd: E.dma_start(out=slot[:, C_WKN + hh * 96: C_WKN + hh * 96 + 64], in_=ckvup_d[l][:, hd * 128: hd * 128 + 64]),
                     writes=[res], chan=ch)
                P.op("pool", lambda E, hh=hh: E.dma_start(out=slot[:, C_WKN + hh * 96 + 64: C_WKN + (hh + 1) * 96], in_=zer_d[:, 0:32]),
                     writes=[res], chan=ch)
                P.op("pool", lambda E, hh=hh, hd=hd: E.dma_start(out=slot[:, C_WV + hh * 64: C_WV + (hh + 1) * 64], in_=ckvup_d[l][:, hd * 128 + 64: hd * 128 + 128]),
                     writes=[res], chan=ch)

        def c_proj(slot, res, tt):
            W1 = slot[:, 0:KC * NW].rearrange("p (k j) -> p k j", k=KC)
            ql = [proj(W1, res, slice(c * 128, (c + 1) * 128), tt) for c in range(2)]
            for (b, _r) in ql:
                hold(b)
            kvl = proj(W1, res, slice(256, 384), tt, acc=True)
            return ql, kvl

        def c_tile(slot, res, tt, pre):
            W1 = slot[:, 0:KC * NW].rearrange("p (k j) -> p k j", k=KC)
            Wq = slot[:, C_WQ:C_WQ + 384].rearrange("p (k j) -> p k j", k=2)
            qo, _ = SM[("cqn", l)]
            kvo, _ = SM[("ckvn", l)]
            tab, tabres = load_tab(1, tt)
            cpool = [(pT[i], pT_res[i]) for i in range(1, NPT)]
            ql, kvl = pre
            nxt_pre = None
            bkks = []
            for hh in range(2):
                bkk, bkres = get_bank(acc=True)
                for k in range(KC):
                    P.op("pe", lambda E, k=k, bkk=bkk: E.matmul(bkk[0:96, :], lhsT=W1[:, k, 384:480], rhs=hT[:, k, tt * TT:(tt + 1) * TT], start=(k == 0), stop=False),
                         reads=[res, h_res[tt][k]], writes=[bkres])
                bkks.append((bkk, bkres))
            b2, b2res = get_bank()
            for c in range(2):
                qi = nxt("sq", 2)
                P.op("act", lambda E, qi=qi, c=c: E.activation(out=sq[qi][:], in_=ql[c][0][:], func=AF.Square), reads=[ql[c][1]], writes=[sq_res[qi]])
                P.op("pe", lambda E, qi=qi, c=c: E.matmul(b2[:], lhsT=ones_bf[:], rhs=sq[qi][:], start=(c == 0), stop=(c == 1)),
                     reads=[sq_res[qi], const_res], writes=[b2res])
            ri = nxt("rstd", NRSTD)
            P.op("act", lambda E: E.activation(out=rstd[ri][:], in_=b2[:], func=AF.Ln, bias=epsb[:], scale=1.0 / 256), reads=[b2res, const_res], writes=[rstd_res[ri]])
            P.op("act", lambda E: E.activation(out=rstd[ri][:], in_=rstd[ri][:], func=AF.Exp, scale=-0.5), reads=[rstd_res[ri]], writes=[rstd_res[ri]])
            qln = []
            for c in range(2):
                si = nxt("stg", 2)
                P.op("dve", lambda E, si=si, c=c: E.scalar_tensor_tensor(out=stg[si][:], in0=ql[c][0][:], scalar=smalls[:, qo + c:qo + c + 1], in1=rstd[ri][:],
                                                                     op0=ALU.mult, op1=ALU.mult),
                     reads=[ql[c][1], rstd_res[ri], const_res], writes=[stg_res[si]])
                qln.append((stg[si], stg_res[si]))
            for (b, _r) in ql:
                release(b)
            if tt + 1 < NT:
                nxt_pre = c_proj(slot, res, tt + 1)
            bqs = []
            for hh in range(2):
                bq, bqres = get_bank()
                for c in range(2):
                    P.op("pe", lambda E, c=c, hh=hh, bq=bq: E.matmul(bq[0:96, :], lhsT=Wq[:, c, hh * 96:(hh + 1) * 96], rhs=qln[c][0][:], start=(c == 0), stop=(c == 1)),
                         reads=[res, qln[c][1]], writes=[bqres])
                bqs.append((bq, bqres))
            p2q = [finish_qk(bqs[hh][0], bqs[hh][1], 96, qc_v[0:96, hh, tt * TT:(tt + 1) * TT], q_res[hh][tt],
                             rope=(R96[0:96, 0:96], tab, tabres), defer=True, spool=cpool) for hh in range(2)]
            qi2 = nxt("sq", 2)
            P.op("act", lambda E: E.activation(out=sq[qi2][:], in_=kvl[0][:], func=AF.Square), reads=[kvl[1]], writes=[sq_res[qi2]])
            b4, b4res = get_bank()
            P.op("pe", lambda E: E.matmul(b4[:], lhsT=ones_bf[:], rhs=sq[qi2][:], start=True, stop=True), reads=[sq_res[qi2], const_res], writes=[b4res])
            ri2 = nxt("rstd", NRSTD)
            P.op("act", lambda E: E.activation(out=rstd[ri2][:], in_=b4[:], func=AF.Ln, bias=epsb[:], scale=1.0 / 128), reads=[b4res, const_res], writes=[rstd_res[ri2]])
            P.op("act", lambda E: E.activation(out=rstd[ri2][:], in_=rstd[ri2][:], func=AF.Exp, scale=-0.5), reads=[rstd_res[ri2]], writes=[rstd_res[ri2]])
            kvn, kvnres = pT[0], pT_res[0]
            P.op("dve", lambda E: E.scalar_tensor_tensor(out=kvn[:], in0=kvl[0][:], scalar=smalls[:, kvo:kvo + 1], in1=rstd[ri2][:], op0=ALU.mult, op1=ALU.mult),
                 reads=[kvl[1], rstd_res[ri2], const_res], writes=[kvnres])
            for p2 in p2q:
                p2()
            p2k = []
            for hh in range(2):
                bkk, bkres = bkks[hh]
                P.op("pe", lambda E, hh=hh, bkk=bkk: E.matmul(bkk[0:96, :], lhsT=slot[:, C_WKN + hh * 96: C_WKN + (hh + 1) * 96], rhs=kvn[:], start=False, stop=True),
                     reads=[res, kvnres], writes=[bkres])
                p2k.append(finish_qk(bkk, bkres, 96, kc_v[0:96, hh, tt * TT:(tt + 1) * TT], k_res[hh][tt], rope=(R96[0:96, 0:96], tab, tabres), defer=True, spool=cpool))
            for i4 in range(4):
                i = tt * 4 + i4
                bv, bvres = get_bank()
                P.op("pe", lambda E, i4=i4, bv=bv: E.matmul(bv[:, 0:128], lhsT=kvn[:, i4 * 128:(i4 + 1) * 128], rhs=slot[:, C_WV:C_WV + 128], start=True, stop=True),
                     reads=[res, kvnres], writes=[bvres])
                P.op("act", lambda E, bv=bv, i=i: E.activation(out=va_v[:, i, 0:64], in_=bv[:, 0:64], func=AF.Copy), reads=[bvres], writes=[v_res])
                P.op("dve", lambda E, bv=bv, i=i: E.tensor_copy(out=va_v[:, i, 128:192], in_=bv[:, 64:128]), reads=[bvres], writes=[v_res])
            for p2 in p2k:
                p2()
            return nxt_pre

        def comp(slot, res):
            vaug_ones()
            P.op("dve", lambda E: E.memset(scr[64:128, O_Q:O_Q + 4 * S], 0.0),
                 writes=[q_res[hh][t] for hh in range(2) for t in range(NT)] + [k_res[hh][t] for hh in range(2) for t in range(NT)])
            pre = c_proj(slot, res, 0)
            for tt in range(NT):
                pre = c_tile(slot, res, tt, pre)

        def attn(slot, res):
            def qsel(h, tt):
                return qc_v[:, h, tt * TT:(tt + 1) * TT], q_res[h][tt]

            def ksel(h, kc):
                return kc_v[:, h, kc * 128:(kc + 1) * 128], k_res[h][kc // 4]

            def omap(h, tt):
                return oc_v[h * 64:(h + 1) * 64, p, tt * TT:(tt + 1) * TT], o_res["c"][p][tt]
            attn_dense(qsel, ksel, 96, [[(hh, vsel_ab(hh), hh)] for hh in range(2)], C_SCALE, omap)
        add(job, comp)
        add(None, attn)

    def merge_steps(l, s):
        for tt in range(NT):
            for dc in range(KC):
                def job(slot, res, ch, dc=dc):
                    G = slot[:, 0:3 * KC * 128].rearrange("p (m k j) -> p m k j", m=3, k=KC)
                    for m in range(3):
                        src = winl[l][:, :, 1696 + m * 1024 + dc * 128: 1696 + m * 1024 + (dc + 1) * 128]
                        P.op("pool", lambda E, m=m, src=src: E.dma_start(out=G[:, m, :, :], in_=src), writes=[res], chan=ch)
                    BR = slot[:, 3072:4096].rearrange("p (c j) -> p c j", c=8)
                    for mi, wd in enumerate((wbra_d, wbrb_d)):
                        for half in range(2):
                            src = wd[l].rearrange("(half j p) n -> half p j n", half=2, j=3)[half, :, :, dc * 128:(dc + 1) * 128]
                            P.op("pool", lambda E, mi=mi, half=half, src=src: E.dma_start(out=BR[half * 64:(half + 1) * 64, mi * 3:(mi + 1) * 3, :], in_=src),
                                 writes=[res], chan=ch)
                    src = wbrc_d[l].rearrange("(j p) n -> p j n", p=128)[:, :, dc * 128:(dc + 1) * 128]
                    P.op("pool", lambda E, src=src: E.dma_start(out=BR[:, 6:8, :], in_=src), writes=[res], chan=ch)

                def comp(slot, res, dc=dc, tt=tt):
                    G = slot[:, 0:3 * KC * 128].rearrange("p (m k j) -> p m k j", m=3, k=KC)
                    BR = slot[:, 3072:4096].rearrange("p (c j) -> p c j", c=8)
                    sgs = []
                    for m in range(3):
                        bk, bres = get_bank()
                        for k in range(KC):
                            P.op("pe", lambda E, bk=bk, m=m, k=k: E.matmul(bk[:], lhsT=G[:, m, k, :], rhs=hT[:, k, tt * TT:(tt + 1) * TT], start=(k == 0), stop=(k == KC - 1)),
                                 reads=[res, h_res[tt][k]], writes=[bres])
                        tf, tres = get_tmpf()
                        P.op("act", lambda E, bk=bk, tf=tf: E.activation(out=tf[:], in_=bk[:], func=AF.Sigmoid), reads=[bres], writes=[tres])
                        sgs.append((tf, tres))
                    acc = None
                    for m, (ov, nch, key) in enumerate(((oa_v, 3, "a"), (ob_v, 3, "b"), (oc_v, 2, "c"))):
                        bk, bres = get_bank()
                        for c in range(nch):
                            P.op("pe", lambda E, bk=bk, m=m, c=c, ov=ov, nch=nch: E.matmul(bk[:], lhsT=BR[:, m * 3 + c, :], rhs=ov[:, c, tt * TT:(tt + 1) * TT],
                                                                                       start=(c == 0), stop=(c == nch - 1)),
                                 reads=[res, o_res[key][c][tt]], writes=[bres])
                        tf, tres = sgs[m]
                        P.op("dve", lambda E, bk=bk, tf=tf: E.tensor_tensor(out=tf[:], in0=tf[:], in1=bk[:], op=ALU.mult), reads=[tres, bres], writes=[tres])
                    t0, t0res = sgs[0]
                    P.op("dve", lambda E: E.tensor_tensor(out=t0[:], in0=t0[:], in1=sgs[1][0][:], op=ALU.add), reads=[t0res, sgs[1][1]], writes=[t0res])
                    P.op("dve", lambda E: E.tensor_tensor(out=mg_v[:, dc, :], in0=t0[:], in1=sgs[2][0][:], op=ALU.add), reads=[t0res, sgs[2][1]], writes=[mg_res[dc]])
                add(job, comp)
            for oc2 in range(2):
                def jobo(slot, res, ch, oc2=oc2):
                    W = slot[:, 0:KC * 512].rearrange("p (k j) -> p k j", k=KC)
                    src = wout_d[l].rearrange("(k p) n -> p k n", p=128)[:, :, oc2 * 512:(oc2 + 1) * 512]
                    P.op("pool", lambda E, src=src: E.dma_start(out=W, in_=src), writes=[res], chan=ch)

                def compo(slot, res, oc2=oc2, tt=tt):
                    W = slot[:, 0:KC * 512].rearrange("p (k j) -> p k j", k=KC)
                    for c4 in range(4):
                        dcc = oc2 * 4 + c4
                        bk, bres = get_bank()
                        for k in range(KC):
                            P.op("pe", lambda E, bk=bk, k=k, c4=c4: E.matmul(bk[:], lhsT=W[:, k, c4 * 128:(c4 + 1) * 128], rhs=mg_v[:, k, :], start=(k == 0), stop=(k == KC - 1)),
                                 reads=[res, mg_res[k]], writes=[bres])
                        xs = xT[:, dcc, tt * TT:(tt + 1) * TT]
                        P.op("dve", lambda E, bk=bk, xs=xs, dcc=dcc: E.scalar_tensor_tensor(out=xs, in0=bk[:], scalar=mG(l, 1, s, dcc), in1=xs, op0=ALU.mult, op1=ALU.add),
                             reads=[bres, x_res[tt][dcc], mods_res], writes=[x_res[tt][dcc]])
                add(jobo, compo)
            if tt >= 1:
                add_next_norm(l, 1, s, [tt - 1])
        add_next_norm(l, 1, s, [NT - 1])

    def mix_steps(l, s):

        def zero_unused(slot, res):
            for key, ov, nch in (("a", oa_v, 3), ("b", ob_v, 3), ("c", oc_v, 2)):
                if key not in mixers:
                    for c in range(nch):
                        P.op("dve", lambda E, ov=ov, c=c: E.memset(ov[:, c, :], 0.0), writes=o_res[key][c])

        if "a" in mixers:
            mixer_ab_steps(l, s, "a")
        if "b" in mixers:
            mixer_ab_steps(l, s, "b")
        if "c" in mixers:
            mixer_c_steps(l, s, 0)
            mixer_c_steps(l, s, 1)
        if mixers != "abc":
            add(None, zero_unused)
        merge_steps(l, s)

    if do_mix:
        mix_consts()
    ada_pending = []
    if do_ada:
        for l in range(depth):
            for m3 in range(3):
                ada_steps(l, m3, steps if (l == 0 and m3 == 0) else ada_pending)
    n_pre = len(steps)
    for s in range(n_seq):
        load_seq_steps(s)
        for l in range(depth):
            if do_ffn:
                ffn_steps(l, 0, s)
            if do_mix:
                mix_steps(l, s)
            if do_ffn:
                ffn_steps(l, 1, s)
        final_steps(s)

    if ada_pending:
        merged = steps[:n_pre]
        npop = {}
        for st in steps[n_pre:]:
            merged.append(st)
            tag = step_tag.get(id(st[1]))
            if tag is not None and ada_pending and npop.get(tag, 0) < 36:
                npop[tag] = npop.get(tag, 0) + 1
                merged.append(ada_pending.pop(0))
        assert not ada_pending
        steps = merged

    wjobs = [(i, st[0]) for i, st in enumerate(steps) if st[0] is not None]
    jidx = {i: n for n, (i, _) in enumerate(wjobs)}
    loaded = 0

    def ensure(nj):
        nonlocal loaded
        while loaded < min(nj, len(wjobs)):
            sl = loaded % NSLOT
            wjobs[loaded][1](slots[sl], slot_res[sl], slot_ch[sl])
            loaded += 1

    for i, (wj, fn) in enumerate(steps):
        if wj is not None:
            n = jidx[i]
            ensure(n + NSLOT)
            fn(slots[n % NSLOT], slot_res[n % NSLOT])
        else:
            fn(None, None)
    stats = P.emit(final_chans=xout_ch)
    return nc, stats


_CACHE = {}


_CONSTS = {}


def make_consts():
    if _CONSTS:
        return _CONSTS
    import math
    import jax
    import jax.numpy as jnp
    theta = 10000.0
    with jax.default_device(jax.devices("cpu")[0]):
        def angles(pos, dim):
            inv = theta ** (-jnp.arange(0, dim, 2, dtype=jnp.float32) / dim)
            ang = pos.astype(jnp.float32)[:, None] * inv[None, :]
            return np.asarray(jnp.cos(ang)), np.asarray(jnp.sin(ang))
        t = jnp.arange(S)
        row_c, row_s = angles(t // 64, 32)
        col_c, col_s = angles(t % 64, 32)
        seq_c, seq_s = angles(t, 32)
        idx = np.arange(512)
        rel = jnp.asarray(128 - (idx - 127))
        nb, max_exact = 16, 8
        ret = jnp.where(rel > 0, nb, 0)
        n = jnp.abs(rel)
        large = max_exact + (jnp.log(jnp.maximum(n, 1).astype(jnp.float32) / max_exact)
                             / math.log(128 / max_exact) * (nb - max_exact)).astype(jnp.int32)
        large = jnp.minimum(large, nb - 1)
        bucket = np.asarray(ret + jnp.where(n < max_exact, n, large))
    ropeA = np.zeros((128, 2, S), np.float32)
    for p in range(128):
        d = p % 64
        cs, sn = (row_c, row_s) if d < 32 else (col_c, col_s)
        dd = d % 32
        i = dd % 16
        ropeA[p, 0] = cs[:, i]
        ropeA[p, 1] = -sn[:, i] if dd < 16 else sn[:, i]
    ropeC = np.zeros((128, 2, S), np.float32)
    ropeC[0:64, 0] = 1.0
    for p in range(64, 96):
        dd = p - 64
        i = dd % 16
        ropeC[p, 0] = seq_c[:, i]
        ropeC[p, 1] = -seq_s[:, i] if dd < 16 else seq_s[:, i]
    cf = np.zeros((128, CF_COLS), np.float32)
    for m in range(128):
        dd = m % 32
        k = m + 16 if dd < 16 else m - 16
        cf[k, m] = 1.0
    for m in range(64, 96):
        dd = m - 64
        k = m + 16 if dd < 16 else m - 16
        cf[k, 128 + m] = 1.0
    cf[0:64, 256:320] = 1.0
    cf[64:128, 320:384] = 1.0
    u = idx - 127
    valid = (u >= 0) & (u <= 256) & (idx < 511)
    for i in range(511):
        cf[bucket[i], 384 + i] = 1.0
    cf[0:6, 896:1408] = valid.astype(np.float32)[None, :]
    for m in range(128):
        cf[127 - m, 1408 + m] = 1.0
    _CONSTS.update(cf=cf, ropeA=ropeA, ropeC=ropeC, zer=np.zeros((128, KC * 64), np.float32))
    return _CONSTS


def make_in_maps(inp, n_seq, ncores):
    x = np.ascontiguousarray(inp["x"], np.float32)
    sm = pack_smalls(inp)
    shared = {"smalls": sm}
    shared.update(make_consts())
    for k in ("ada_w", "ffn1_w_gu", "ffn2_w_gu", "ffn1_w_down", "ffn2_w_down", "w_in", "c_w_q_up", "c_w_kv_up",
              "w_br_a", "w_br_b", "w_br_c", "w_out"):
        shared[k] = np.ascontiguousarray(inp[k], np.float32)
    c = np.asarray(inp["c"], np.float32)
    in_maps = []
    for i in range(ncores):
        cs = c[i * n_seq:(i + 1) * n_seq]
        ct = cs.reshape(n_seq, KC, 128).transpose(2, 1, 0)
        m = dict(shared)
        m["x"] = x[i * n_seq:(i + 1) * n_seq].reshape(n_seq * S, D)
        m["cT"] = np.ascontiguousarray(ct.reshape(128, KC * n_seq))
        in_maps.append(m)
    return in_maps


def kernel(**inp):
    B = inp["x"].shape[0]
    n_seq = B // NCORES
    if "nc" not in _CACHE:
        _CACHE["nc"] = build(n_seq=n_seq)[0]
    nc = _CACHE["nc"]
    in_maps = make_in_maps(inp, n_seq, NCORES)
    res = run_bass_kernel_spmd(nc, in_maps, core_ids=list(range(NCORES)))
    out = np.concatenate([r["out"].reshape(n_seq, S, D) for r in res.results], axis=0)
    return out.astype(np.float32)
```
